# Optimizing a Trainium2 kernel written in Bass

```python
import math
import jax, jax.numpy as jnp
from jax import lax
import numpy as np

D_MODEL = 1024
BATCH = 16
SEQ = 2048
DEPTH = 2

N_MEM = 256
EPS = 1e-6
ROPE_THETA = 10000.0
Q_BLOCK = 128

HY_CH = D_MODEL // 2
HY_ORDER = 2
HY_EMB = 33
HY_BANDS = (HY_EMB - 1) // 2
HY_FFN = 64
HY_FAST_PCT = 0.3
HY_SLOW_PCT = 1.5
HY_TARGET = 1e-2

MLA_HEADS = 8
MLA_NOPE = 64
MLA_ROPE = 32
MLA_V = 64
MLA_QK = MLA_NOPE + MLA_ROPE
MLA_Q_RANK = D_MODEL // 4
MLA_KV_RANK = D_MODEL // 8

IN_EVEN = (HY_ORDER + 1) * HY_CH + MLA_Q_RANK + MLA_KV_RANK + MLA_ROPE
MIX_EVEN = HY_CH + MLA_HEADS * MLA_V

DIFF_HEADS = 8
DIFF_HD = D_MODEL // DIFF_HEADS // 2
SUBLN_EPS = 1e-5

X_HEADS = 4
X_HD = D_MODEL // X_HEADS

D_FF = 4 * D_MODEL

N_EVEN = (DEPTH + 1) // 2
N_ODD = DEPTH // 2

kernel_name = "hybrid_hyena_mla_diffattn_encoder"


def rms_norm(x, g, eps=EPS):
    xf = x.astype(jnp.float32)
    y = xf * lax.rsqrt(jnp.mean(xf * xf, axis=-1, keepdims=True) + eps)
    return (y * g.astype(jnp.float32)).astype(x.dtype)


def rope_tables(seq_len, dim):
    inv = ROPE_THETA ** (-jnp.arange(0, dim, 2, dtype=jnp.float32) / dim)
    ang = jnp.arange(seq_len, dtype=jnp.float32)[:, None] * inv[None, :]
    return jnp.cos(ang), jnp.sin(ang)


def apply_rope(x, cos, sin):
    c = cos[None, :, None, :].astype(x.dtype)
    s = sin[None, :, None, :].astype(x.dtype)
    x1, x2 = jnp.split(x, 2, axis=-1)
    return jnp.concatenate([x1 * c - x2 * s, x2 * c + x1 * s], axis=-1)


def short_conv(u, w, b):
    up = jnp.pad(u, ((0, 0), (1, 1), (0, 0)))
    return up[:, :-2] * w[0] + up[:, 1:-1] * w[1] + up[:, 2:] * w[2] + b


def hyena_filters(seq_len, w1, b1, w2, b2, w3, freq):
    f32 = jnp.float32
    t = jnp.linspace(0.0, 1.0, seq_len, dtype=f32)[:, None]
    w = (2.0 * math.pi / seq_len) * jnp.arange(seq_len, dtype=f32)[:, None]
    bands = jnp.linspace(1e-4, HY_BANDS - 1, HY_BANDS, dtype=f32)[None, :]
    feats = jnp.concatenate([t, jnp.cos(bands * w), -jnp.sin(bands * w)], axis=-1)
    fr = freq.astype(f32)
    a = jnp.sin(fr * (feats @ w1.astype(f32) + b1.astype(f32)))
    a = jnp.sin(fr * (a @ w2.astype(f32) + b2.astype(f32)))
    h = (a @ w3.astype(f32)).reshape(seq_len, HY_ORDER, 2, HY_CH)
    max_decay = math.log(HY_TARGET) / HY_FAST_PCT
    min_decay = math.log(HY_TARGET) / HY_SLOW_PCT
    deltas = jnp.linspace(min_decay, max_decay, HY_CH, dtype=f32)
    window = jnp.exp(-t * jnp.abs(deltas)[None, :])
    return h * window[:, None, None, :]


def two_sided_long_conv(z, h_fwd, h_bwd):
    L = z.shape[1]
    k = jnp.concatenate([h_fwd, jnp.zeros((1, h_fwd.shape[1]), h_fwd.dtype), h_bwd[:0:-1]], axis=0)
    kf = jnp.fft.rfft(k, n=2 * L, axis=0)
    zf = jnp.fft.rfft(z.astype(jnp.float32), n=2 * L, axis=1)
    y = jnp.fft.irfft(zf * kf[None], n=2 * L, axis=1)[:, :L]
    return y.astype(z.dtype)


def block_attention(q, k, v, scale):
    B, S, H, dk = q.shape
    nb = S // Q_BLOCK
    qb = jnp.moveaxis(q.reshape(B, nb, Q_BLOCK, H, dk), 1, 0)

    def one(qblk):
        s = jnp.einsum('bqhd,bkhd->bhqk', qblk, k).astype(jnp.float32) * scale
        p = jax.nn.softmax(s, axis=-1).astype(v.dtype)
        return jnp.einsum('bhqk,bkhe->bqhe', p, v)

    o = lax.map(one, qb)
    return jnp.moveaxis(o, 0, 1).reshape(B, S, H, v.shape[-1])


def diff_block_attention(q, k, v, lam):
    B, S, H2, d = q.shape
    H = H2 // 2
    nb = S // Q_BLOCK
    qb = jnp.moveaxis(q.reshape(B, nb, Q_BLOCK, H2, d), 1, 0)
    scale = d ** -0.5

    def one(qblk):
        s = jnp.einsum('bqhd,bkhd->bhqk', qblk, k).astype(jnp.float32) * scale
        p = jax.nn.softmax(s, axis=-1).reshape(B, H, 2, Q_BLOCK, S)
        a = (p[:, :, 0] - lam * p[:, :, 1]).astype(v.dtype)
        return jnp.einsum('bhqk,bkhe->bqhe', a, v)

    o = lax.map(one, qb)
    return jnp.moveaxis(o, 0, 1).reshape(B, S, H, v.shape[-1])


def even_mixer(h, w_in, conv_w, conv_b, hy_w1, hy_b1, hy_w2, hy_b2, hy_w3, hy_freq, hy_skip,
               q_norm, w_uq, kv_norm, w_ukv, w_out, cos_r, sin_r):
    B, S, _ = h.shape
    proj = h @ w_in
    c0 = (HY_ORDER + 1) * HY_CH
    c1 = c0 + MLA_Q_RANK
    c2 = c1 + MLA_KV_RANK
    hy, cq, ckv, kr = jnp.split(proj, [c0, c1, c2], axis=-1)

    hy = short_conv(hy, conv_w, conv_b)
    v, x1, x2 = jnp.split(hy, 3, axis=-1)
    filt = hyena_filters(S, hy_w1, hy_b1, hy_w2, hy_b2, hy_w3, hy_freq)
    z = v
    for o, gate in enumerate((x1, x2)):
        z = gate * (two_sided_long_conv(z, filt[:, o, 0], filt[:, o, 1]) + hy_skip[o].astype(z.dtype) * z)

    q = (rms_norm(cq, q_norm) @ w_uq).reshape(B, S, MLA_HEADS, MLA_QK)
    q_nope, q_pe = jnp.split(q, [MLA_NOPE], axis=-1)
    q_pe = apply_rope(q_pe, cos_r, sin_r)
    kv = (rms_norm(ckv, kv_norm) @ w_ukv).reshape(B, S, MLA_HEADS, MLA_NOPE + MLA_V)
    k_nope, vv = jnp.split(kv, [MLA_NOPE], axis=-1)
    k_pe = apply_rope(kr[:, :, None, :], cos_r, sin_r)
    qf = jnp.concatenate([q_nope, q_pe], axis=-1)
    kf = jnp.concatenate([k_nope, jnp.broadcast_to(k_pe, (B, S, MLA_HEADS, MLA_ROPE))], axis=-1)
    o_mla = block_attention(qf, kf, vv, MLA_QK ** -0.5).reshape(B, S, MLA_HEADS * MLA_V)

    return jnp.concatenate([z, o_mla], axis=-1) @ w_out


def odd_mixer(h, w_qkv, lq1, lk1, lq2, lk2, subln, w_out, cos_d, sin_d, lam_init):
    B, S, _ = h.shape
    q, k, v = jnp.split(h @ w_qkv, 3, axis=-1)
    q = apply_rope(q.reshape(B, S, 2 * DIFF_HEADS, DIFF_HD), cos_d, sin_d)
    k = apply_rope(k.reshape(B, S, 2 * DIFF_HEADS, DIFF_HD), cos_d, sin_d)
    v = v.reshape(B, S, DIFF_HEADS, 2 * DIFF_HD)
    f32 = jnp.float32
    lam = (jnp.exp(jnp.sum(lq1.astype(f32) * lk1.astype(f32)))
           - jnp.exp(jnp.sum(lq2.astype(f32) * lk2.astype(f32))) + lam_init)
    o = diff_block_attention(q, k, v, lam)
    o = rms_norm(o, subln, SUBLN_EPS) * (1.0 - lam_init)
    return o.reshape(B, S, D_MODEL) @ w_out


def cross_attention(h, mem_n, wq, wkv, wo):
    B, S, _ = h.shape
    M = mem_n.shape[1]
    q = (h @ wq).reshape(B, S, X_HEADS, X_HD)
    k, v = jnp.split(mem_n @ wkv, 2, axis=-1)
    k = k.reshape(B, M, X_HEADS, X_HD)
    v = v.reshape(B, M, X_HEADS, X_HD)
    s = jnp.einsum('bqhd,bkhd->bhqk', q, k).astype(jnp.float32) * (X_HD ** -0.5)
    p = jax.nn.softmax(s, axis=-1).astype(v.dtype)
    o = jnp.einsum('bhqk,bkhd->bqhd', p, v).reshape(B, S, D_MODEL)
    return o @ wo


def sq_relu_mlp(h, w_up, w_down):
    return jnp.square(jax.nn.relu(h @ w_up)) @ w_down


def setup_inputs(seed: int = 0) -> dict:
    key = jax.random.key(seed)
    keys = iter(jax.random.split(key, 48))
    f32 = jnp.float32
    D = D_MODEL
    E, O, L = N_EVEN, N_ODD, DEPTH

    def normal(shape, std):
        return std * jax.random.normal(next(keys), shape, f32)

    def gain(shape):
        return 1.0 + normal(shape, 0.02)

    return {
        "x": normal((BATCH, SEQ, D), 1.0),
        "mem": normal((BATCH, N_MEM, D), 1.0),
        "ev_w_in": normal((E, D, IN_EVEN), D ** -0.5),
        "ev_conv_w": normal((E, 3, (HY_ORDER + 1) * HY_CH), 3 ** -0.5),
        "ev_conv_b": normal((E, (HY_ORDER + 1) * HY_CH), 0.02),
        "hy_w1": normal((E, HY_EMB, HY_FFN), HY_EMB ** -0.5),
        "hy_b1": normal((E, HY_FFN), 0.1),
        "hy_w2": normal((E, HY_FFN, HY_FFN), HY_FFN ** -0.5),
        "hy_b2": normal((E, HY_FFN), 0.1),
        "hy_w3": normal((E, HY_FFN, HY_ORDER * 2 * HY_CH), 0.05 * HY_FFN ** -0.5),
        "hy_freq": gain((E, HY_FFN)),
        "hy_skip": normal((E, HY_ORDER, HY_CH), 0.5),
        "mla_q_norm": gain((E, MLA_Q_RANK)),
        "mla_w_uq": normal((E, MLA_Q_RANK, MLA_HEADS * MLA_QK), MLA_Q_RANK ** -0.5),
        "mla_kv_norm": gain((E, MLA_KV_RANK)),
        "mla_w_ukv": normal((E, MLA_KV_RANK, MLA_HEADS * (MLA_NOPE + MLA_V)), MLA_KV_RANK ** -0.5),
        "ev_w_out": normal((E, MIX_EVEN, D), MIX_EVEN ** -0.5),
        "od_w_qkv": normal((O, D, 3 * D), D ** -0.5),
        "dif_lq1": normal((O, DIFF_HD), 0.1),
        "dif_lk1": normal((O, DIFF_HD), 0.1),
        "dif_lq2": normal((O, DIFF_HD), 0.1),
        "dif_lk2": normal((O, DIFF_HD), 0.1),
        "dif_subln": gain((O, 2 * DIFF_HD)),
        "od_w_out": normal((O, D, D), D ** -0.5),
        "norm_mix": gain((L, D)),
        "norm_cross": gain((L, D)),
        "norm_mlp": gain((L, D)),
        "xa_wq": normal((L, D, D), D ** -0.5),
        "xa_wkv": normal((L, D, 2 * D), D ** -0.5),
        "xa_wo": normal((L, D, D), D ** -0.5),
        "mlp_up": normal((L, D, D_FF), D ** -0.5),
        "mlp_down": normal((L, D_FF, D), D_FF ** -0.5),
        "mem_norm": gain((D,)),
        "final_norm": gain((D,)),
    }


def reference(x, mem, ev_w_in, ev_conv_w, ev_conv_b, hy_w1, hy_b1, hy_w2, hy_b2, hy_w3, hy_freq,
              hy_skip, mla_q_norm, mla_w_uq, mla_kv_norm, mla_w_ukv, ev_w_out, od_w_qkv, dif_lq1,
              dif_lk1, dif_lq2, dif_lk2, dif_subln, od_w_out, norm_mix, norm_cross, norm_mlp, xa_wq,
              xa_wkv, xa_wo, mlp_up, mlp_down, mem_norm, final_norm):
    S = x.shape[1]
    mem_n = rms_norm(mem, mem_norm)
    cos_r, sin_r = rope_tables(S, MLA_ROPE)
    cos_d, sin_d = rope_tables(S, DIFF_HD)
    for i in range(DEPTH):
        j = i // 2
        h = rms_norm(x, norm_mix[i])
        if i % 2 == 0:
            x = x + even_mixer(h, ev_w_in[j], ev_conv_w[j], ev_conv_b[j], hy_w1[j], hy_b1[j],
                               hy_w2[j], hy_b2[j], hy_w3[j], hy_freq[j], hy_skip[j],
                               mla_q_norm[j], mla_w_uq[j], mla_kv_norm[j], mla_w_ukv[j],
                               ev_w_out[j], cos_r, sin_r)
        else:
            lam_init = 0.8 - 0.6 * math.exp(-0.3 * i)
            x = x + odd_mixer(h, od_w_qkv[j], dif_lq1[j], dif_lk1[j], dif_lq2[j], dif_lk2[j],
                              dif_subln[j], od_w_out[j], cos_d, sin_d, lam_init)
        x = x + cross_attention(rms_norm(x, norm_cross[i]), mem_n, xa_wq[i], xa_wkv[i], xa_wo[i])
        x = x + sq_relu_mlp(rms_norm(x, norm_mlp[i]), mlp_up[i], mlp_down[i])
    return rms_norm(x, final_norm)
```

```python
import math
import numpy as np
import ml_dtypes
import concourse.bass as bass
import concourse.mybir as mybir
from concourse.bass_utils import run_bass_kernel_spmd

DT = mybir.dt
F32, BF16, U8 = DT.float32, DT.bfloat16, DT.uint8
ALU = mybir.AluOpType
AF = mybir.ActivationFunctionType
AX = mybir.AxisListType
DSIZE = {F32: 4, BF16: 2, U8: 1}

D = 1024
S = 2048
NMEM = 256
NCH = 8
TB = 512
NTB = 4
EPS = 1e-6
N_CORES = 8
SEQ_PER_CORE = 2
SBUF_BYTES = 212736


class Sem:
    def __init__(self, nc, name):
        self.h = nc.alloc_semaphore(name=name)
        self.count = 0
        self.name = name


class Tile:
    def __init__(self, ap, name, space):
        self.ap = ap
        self.name = name
        self.space = space
        self.w = None
        self.r = {}

    def __getitem__(self, idx):
        return V([self], self.ap[idx])

    @property
    def v(self):
        return V([self], self.ap)


class V:
    def __init__(self, ts, ap):
        self.ts = ts
        self.ap = ap

    def __getitem__(self, idx):
        return V(self.ts, self.ap[idx])

    def re(self, pat, **kw):
        return V(self.ts, self.ap.rearrange(pat, **kw))


class Grid:
    def __init__(self, ap, name, nch, ntok, tbw):
        self.ap = ap
        self.tbw = tbw
        self.nch = nch
        self.cells = [[Tile(ap[:, c, tb * tbw:(tb + 1) * tbw], f"{name}_{c}_{tb}", "sbuf")
                       for tb in range(ntok // tbw)] for c in range(nch)]

    def all_tiles(self):
        return [t for row in self.cells for t in row]

    def sl(self, c0, c1, t0, t1):
        tiles = [self.cells[c][tb] for c in range(c0, c1)
                 for tb in range(t0 // self.tbw, (t1 - 1) // self.tbw + 1)]
        if c1 - c0 == 1:
            return V(tiles, self.ap[:, c0, t0:t1])
        return V(tiles, self.ap[:, c0:c1, t0:t1])

    def c(self, c, t0, t1):
        return self.sl(c, c + 1, t0, t1)


class Eng:
    def __init__(self, nc, name):
        self.name = name
        self.sem = Sem(nc, "s_" + name)
        self.known = {}
        self.ops = []


class FW:
    def __init__(self, nc, sbuf_bytes=SBUF_BYTES):
        self.nc = nc
        self.eng = {k: Eng(nc, k) for k in ("pe", "dve", "act", "pool", "sp")}
        self._cms = []
        cm = nc.sbuf_tensor("arena", [128, sbuf_bytes], U8)
        self.arena = cm.__enter__()
        self._cms.append(cm)
        cm = nc.psum_tensor("psum", [128, 8 * 512], F32)
        self.psum = cm.__enter__()
        self._cms.append(cm)
        self.sbuf_bytes = sbuf_bytes
        self.top = 0
        self.peak = 0
        self.freed = []
        self.live = []
        self.banks = [Tile(self.psum[:, i * 512:(i + 1) * 512], f"bank{i}", "psum") for i in range(8)]
        self.rot = 0
        self.sems = {}
        self.n_instr = 0
        self.self_sync = True

    def bank(self, lo=0, hi=8):
        n = hi - lo
        b = self.banks[lo + self.rot % n]
        self.rot += 1
        return b

    def sem(self, name):
        if name not in self.sems:
            self.sems[name] = Sem(self.nc, name)
        return self.sems[name]

    def _raw(self, name, free_shape, dtype, parts):
        n = int(np.prod(free_shape)) * DSIZE[dtype]
        n_al = (n + 63) // 64 * 64
        lo = self.top
        hi = lo + n_al
        assert hi <= self.sbuf_bytes, f"SBUF overflow allocating {name}: {hi} > {self.sbuf_bytes}"
        self.top = hi
        self.peak = max(self.peak, hi)
        ap = self.arena[0:parts, lo:lo + n].bitcast(dtype)
        if len(free_shape) == 2:
            ap = ap.rearrange("p (a b) -> p a b", a=free_shape[0])
        elif len(free_shape) == 3:
            ap = ap.rearrange("p (a b c) -> p a b c", a=free_shape[0], b=free_shape[1])
        return ap, lo, hi

    def _inherit(self, t, lo, hi):
        for (flo, fhi, deps) in self.freed:
            if flo < hi and fhi > lo:
                for s, v in deps.items():
                    self._merge(t.r, s, v)
        self.live.append((lo, hi, t))

    def alloc(self, name, free_shape, dtype, parts=128):
        ap, lo, hi = self._raw(name, free_shape, dtype, parts)
        t = Tile(ap, name, "sbuf")
        self._inherit(t, lo, hi)
        return t

    def alloc_grid(self, name, nch, ntok, dtype, tbw=TB):
        ap, lo, hi = self._raw(name, [nch, ntok], dtype, 128)
        g = Grid(ap, name, nch, ntok, tbw)
        for t in g.all_tiles():
            self._inherit(t, lo, hi)
        return g

    def mark(self):
        return (self.top, len(self.live))

    def release(self, mark):
        top, nlive = mark
        recs = {}
        while len(self.live) > nlive:
            lo, hi, t = self.live.pop()
            deps = recs.setdefault((lo, hi), {})
            if t.w is not None:
                self._merge(deps, t.w[0], self._resolve(*t.w))
            for s, v in t.r.items():
                self._merge(deps, s, self._resolve(s, v))
        for (lo, hi), deps in recs.items():
            keep = []
            for (a, b, d) in self.freed:
                if a >= lo and b <= hi:
                    for s, v in d.items():
                        self._merge(deps, s, v)
                else:
                    keep.append((a, b, d))
            self.freed = keep
            self.freed.append((lo, hi, deps))
        self.top = top

    @staticmethod
    def _resolve(s, v):
        return s.count if v is None else v

    @staticmethod
    def _merge(d, s, v):
        if s in d:
            if d[s] is None or v is None:
                d[s] = None
            else:
                d[s] = max(d[s], v)
        else:
            d[s] = v

    def dram_tile(self, ap, name):
        return Tile(ap, name, "dram")

    def _waits(self, E, reads, writes):
        deps = {}
        for t in reads:
            if t.w is not None:
                self._merge(deps, t.w[0], self._resolve(*t.w))
        for t in writes:
            if t.w is not None:
                self._merge(deps, t.w[0], self._resolve(*t.w))
            for s, v in t.r.items():
                self._merge(deps, s, self._resolve(s, v))
        for s, v in deps.items():
            if v <= 0:
                continue
            if s is E.sem and not self.self_sync:
                continue
            if E.known.get(s, 0) >= v:
                continue
            E.known[s] = v
            E.ops.append(lambda e, h=s.h, vv=v: e.wait_ge(h, vv))

    def emit(self, eng, fn, reads, writes, pe_acc=False):
        E = self.eng[eng]
        reads = list(dict.fromkeys(reads))
        writes = list(dict.fromkeys(writes))
        if pe_acc:
            self._waits(E, reads, [])
        else:
            self._waits(E, reads, writes)
        E.sem.count += 1
        n = E.sem.count
        sh = E.sem.h
        E.ops.append(lambda e: fn(e).then_inc(sh, 1))
        self.n_instr += 1
        for t in reads:
            self._merge(t.r, E.sem, n)
        for t in writes:
            t.w = (E.sem, n)
            t.r = {}

    def dma(self, queue, out, in_, sem, **kw):
        Q = self.eng[queue]
        self._waits(Q, in_.ts, out.ts)
        sem.count += 16
        sh = sem.h
        oa, ia = out.ap, in_.ap
        Q.ops.append(lambda e: e.dma_start(out=oa, in_=ia, **kw).then_inc(sh, 16))
        self.n_instr += 1
        for t in in_.ts:
            self._merge(t.r, sem, None)
        for t in out.ts:
            t.w = (sem, None)
            t.r = {}

    def matmul(self, out, lhsT, rhs, start, stop):
        o, l, r = out.ap, lhsT.ap, rhs.ap
        self.emit("pe", lambda e: e.matmul(o, l, r, start=start, stop=stop),
                  lhsT.ts + rhs.ts, out.ts, pe_acc=not start)

    def transpose(self, out, in_, ident):
        o, i, d = out.ap, in_.ap, ident.ap
        self.emit("pe", lambda e: e.transpose(o, i, d), in_.ts + ident.ts, out.ts)

    def act(self, out, in_, func, bias=None, scale=1.0, accum_out=None):
        o, i = out.ap, in_.ap
        reads = list(in_.ts)
        writes = list(out.ts)
        kw = {}
        if bias is not None:
            if isinstance(bias, V):
                reads += bias.ts
                kw["bias"] = bias.ap
            else:
                kw["bias"] = bias
        if isinstance(scale, V):
            reads += scale.ts
            kw["scale"] = scale.ap
        else:
            kw["scale"] = scale
        if accum_out is not None:
            writes += accum_out.ts
            kw["accum_out"] = accum_out.ap
        self.emit("act", lambda e: e.activation(o, i, func, **kw), reads, writes)

    def tt(self, out, in0, in1, op, eng="dve"):
        o, a, b = out.ap, in0.ap, in1.ap
        self.emit(eng, lambda e: e.tensor_tensor(o, a, b, op), in0.ts + in1.ts, out.ts)

    def ts(self, out, in0, s1, s2=None, op0=ALU.mult, op1=None, eng="dve", accum_out=None):
        o, a = out.ap, in0.ap
        reads = list(in0.ts)
        writes = list(out.ts)
        if isinstance(s1, V):
            reads += s1.ts
            s1 = s1.ap
        if isinstance(s2, V):
            reads += s2.ts
            s2 = s2.ap
        kw = {}
        if op1 is not None:
            kw["op1"] = op1
        if accum_out is not None:
            writes += accum_out.ts
            kw["accum_out"] = accum_out.ap
        self.emit(eng, lambda e: e.tensor_scalar(o, a, s1, s2, op0, **kw), reads, writes)

    def stt(self, out, in0, scalar, in1, op0, op1):
        o, a, b = out.ap, in0.ap, in1.ap
        reads = in0.ts + in1.ts
        if isinstance(scalar, V):
            reads = reads + scalar.ts
            scalar = scalar.ap
        self.emit("dve", lambda e: e.scalar_tensor_tensor(o, a, scalar, b, op0, op1), reads, out.ts)

    def copy(self, out, in_, eng="dve"):
        o, i = out.ap, in_.ap
        if eng == "act":
            self.emit(eng, lambda e: e.copy(o, i), in_.ts, out.ts)
        else:
            self.emit(eng, lambda e: e.tensor_copy(o, i), in_.ts, out.ts)

    def memset(self, out, val, eng="dve"):
        o = out.ap
        self.emit(eng, lambda e: e.memset(o, val), [], out.ts)

    def reduce(self, out, in_, op, axis=AX.X):
        o, i = out.ap, in_.ap
        self.emit("dve", lambda e: e.tensor_reduce(o, i, axis, op), in_.ts, out.ts)

    def recip(self, out, in_):
        o, i = out.ap, in_.ap
        self.emit("dve", lambda e: e.reciprocal(o, i), in_.ts, out.ts)

    def finish(self, final_tiles):
        SP = self.eng["sp"]
        self._waits(SP, [], final_tiles)
        for k, E in self.eng.items():
            if k == "sp" or E.sem.count == 0:
                continue
            if SP.known.get(E.sem, 0) < E.sem.count:
                SP.ops.append(lambda e, h=E.sem.h, v=E.sem.count: e.wait_ge(h, v))
        nc = self.nc
        engs = self.eng
        with nc.Block() as block:
            @block.tensor
            def _(e):
                for f in engs["pe"].ops:
                    f(e)

            @block.vector
            def _(e):
                for f in engs["dve"].ops:
                    f(e)

            @block.scalar
            def _(e):
                for f in engs["act"].ops:
                    f(e)

            @block.gpsimd
            def _(e):
                for f in engs["pool"].ops:
                    f(e)

            @block.sync
            def _(e):
                for f in engs["sp"].ops:
                    f(e)
        for cm in reversed(self._cms):
            cm.__exit__(None, None, None)


def fm(vec, nch):
    return np.ascontiguousarray(np.asarray(vec, np.float32).reshape(nch, 128).T)


def pad128(vec):
    v = np.zeros((128, 1), np.float32)
    v[:len(vec), 0] = vec
    return v


VEC_SPEC = []


def _vs(name, n):
    VEC_SPEC.append((name, n))


for _i in range(2):
    _vs(f"norm_mix{_i}", 8)
    _vs(f"norm_cross{_i}", 8)
    _vs(f"norm_mlp{_i}", 8)
_vs("final_norm", 8)
_vs("mem_norm", 8)
for _k in range(3):
    _vs(f"conv_w{_k}", 12)
_vs("conv_b", 12)
_vs("hy_skip0", 4)
_vs("hy_skip1", 4)
_vs("q_norm", 2)
_vs("kv_norm", 1)
_vs("hy_b1", 1)
_vs("hy_b2", 1)
_vs("hy_freq", 1)
_vs("subln", 128)
_vs("lqk", 256)
VEC_OFF = {}
_o = 0
for _n, _c in VEC_SPEC:
    VEC_OFF[_n] = (_o, _c)
    _o += _c
NV = _o


def pack_vecs(inp):
    cols = {}
    for i in range(2):
        cols[f"norm_mix{i}"] = fm(inp["norm_mix"][i], 8)
        cols[f"norm_cross{i}"] = fm(inp["norm_cross"][i], 8)
        cols[f"norm_mlp{i}"] = fm(inp["norm_mlp"][i], 8)
    cols["final_norm"] = fm(inp["final_norm"], 8)
    cols["mem_norm"] = fm(inp["mem_norm"], 8)
    for k in range(3):
        cols[f"conv_w{k}"] = fm(inp["ev_conv_w"][0, k], 12)
    cols["conv_b"] = fm(inp["ev_conv_b"][0], 12)
    cols["hy_skip0"] = fm(inp["hy_skip"][0, 0], 4)
    cols["hy_skip1"] = fm(inp["hy_skip"][0, 1], 4)
    cols["q_norm"] = fm(inp["mla_q_norm"][0], 2)
    cols["kv_norm"] = fm(inp["mla_kv_norm"][0], 1)
    cols["hy_b1"] = pad128(inp["hy_b1"][0])
    cols["hy_b2"] = pad128(inp["hy_b2"][0])
    cols["hy_freq"] = pad128(inp["hy_freq"][0])
    cols["subln"] = np.broadcast_to(np.asarray(inp["dif_subln"][0], np.float32)[None, :], (128, 128))
    lqk = np.concatenate([inp["dif_lq1"][0], inp["dif_lk1"][0], inp["dif_lq2"][0], inp["dif_lk2"][0]]).astype(np.float32)
    cols["lqk"] = np.broadcast_to(lqk[None, :], (128, 256))
    out = np.concatenate([cols[n] for n, _ in VEC_SPEC], axis=1).astype(np.float32)
    assert out.shape == (128, NV)
    return np.ascontiguousarray(out)


def fw_mark_of(grid, fw):
    cell = grid.cells[0][0]
    for i, (lo, hi, t) in enumerate(fw.live):
        if t is cell:
            return (lo, i)
    raise KeyError


class K:
    def __init__(self, nseq, stages, dbg=None):
        self.nseq = nseq
        self.stages = stages
        nc = bass.Bass("TRN2", target_bir_lowering=False)
        self.nc = nc
        fw = FW(nc)
        self.fw = fw
        self.d = {}
        self.in_names = []

        def din(name, shape, dt=F32):
            ap = nc.dram_tensor(name, list(shape), dt, kind="ExternalInput").ap()
            self.d[name] = fw.dram_tile(ap, name)
            self.in_names.append(name)

        din("x", [nseq, S, D])
        din("mem", [nseq, NMEM, D])
        din("vecs", [128, NV])
        din("ident", [128, 128])
        for i in range(2):
            din(f"xa_wq{i}", [D, D])
            din(f"xa_wkv{i}", [D, 2 * D])
            din(f"xa_wo{i}", [D, D])
            din(f"mlp_up{i}", [D, 4 * D])
            din(f"mlp_down{i}", [4 * D, D])
        din("ev_w_in", [D, 1952])
        din("ev_w_out", [D, D])
        din("mla_w_uq", [256, 768])
        din("mla_w_ukv", [128, 1024])
        din("hy_w1p", [128, 128])
        din("hy_w2p", [128, 128])
        din("hy_w3p", [128, 2048])
        din("featsT", [128, S])
        din("win_f", [S, 512])
        din("win_b", [S, 512])
        din("dftF", [16, 128, 2, 16, 128], BF16)
        din("dftI", [4, 4, 128, 4, 2, 512], BF16)
        din("ropeR_cos", [128, S])
        din("ropeR_sin", [128, S])
        ks = nc.dram_tensor("kspec", [2, 16, 128, 2, 512], BF16, kind="Internal").ap()
        self.kspec = [[fw.dram_tile(ks[o, fc], f"kspec{o}_{fc}") for fc in range(16)] for o in range(2)]
        din("od_w_qkv", [D, 3 * D])
        din("od_w_out", [D, D])
        din("ropeD_cos", [128, S])
        din("ropeD_sin", [128, S])
        yap = nc.dram_tensor("y", [nseq, S, D], F32, kind="ExternalOutput").ap()
        self.y = fw.dram_tile(yap, "y")

        self.setup_consts()
        self._precast_done = False
        if "mix0" in stages:
            self.filter_phase()
        else:
            self.precast()
        self.setup()
        for s in range(nseq):
            self.cur_seq = s
            self.load_seq(s)
            for st in stages:
                if st.startswith("mlp"):
                    self.mlp(int(st[3:]))
                elif st.startswith("cross"):
                    self.cross(int(st[5:]))
                elif st == "mix1":
                    self.mix1()
                elif st == "mix0":
                    self.mix0()
                elif st == "final":
                    pass
                else:
                    raise ValueError(st)
            self.store_seq(s, final=("final" in stages))
        fw.finish([self.y])

    def vec(self, name, c0=0, c1=None):
        off, n = VEC_OFF[name]
        if c1 is None:
            c1 = n
        return self.vecs[:, off + c0:off + c1]

    def setup(self):
        fw = self.fw
        self.xT = fw.alloc_grid("xT", NCH, S, F32)

    def precast(self):
        fw = self.fw
        nc = self.nc
        self._precast_done = True
        need = []
        for st in self.stages:
            if st.startswith("mlp"):
                i = st[3:]
                need += [f"mlp_up{i}", f"mlp_down{i}"]
        sem = fw.sem("precast")
        for name in need:
            src = self.d[name]
            shp = list(src.ap.shape)
            ap = nc.dram_tensor(name + "_bf", shp, BF16, kind="Internal").ap()
            dst = fw.dram_tile(ap, name + "_bf")
            rows = shp[0]
            step = max(128, rows // 2)
            for r0 in range(0, rows, step):
                fw.dma("pool", dst[r0:r0 + step, :], src[r0:r0 + step, :], sem, max_dma_last_dim=8192)
            if name == "ev_w_in":
                self.d["ev_w_in_bf"] = dst
            else:
                self.d[name] = dst

    def setup_consts(self):
        fw = self.fw
        self.vecs = fw.alloc("vecs", [NV], F32)
        self.ident = fw.alloc("ident", [128], F32)
        self.ones_bf = fw.alloc("ones_bf", [128], BF16)
        s_c = fw.sem("const")
        fw.dma("sp", self.vecs.v, self.d["vecs"].v, s_c)
        fw.dma("sp", self.ident.v, self.d["ident"].v, s_c)
        fw.memset(self.ones_bf.v, 1.0, eng="dve")
        self.ident_bf = fw.alloc("ident_bf", [128], BF16)
        fw.copy(self.ident_bf.v, self.ident.v, eng="dve")
        self.maskA = fw.alloc("maskA", [128], BF16)
        self.maskB = fw.alloc("maskB", [128], BF16)
        self.mcol = fw.alloc("mcol", [2], F32)
        fw.memset(self.maskA.v, 0.0, eng="dve")
        fw.memset(self.maskB.v, 0.0, eng="dve")
        fw.memset(self.mcol.v, 0.0, eng="dve")
        fw.memset(self.maskA[0:64, :], 1.0, eng="dve")
        fw.memset(self.maskB[64:128, :], 1.0, eng="dve")
        fw.memset(self.mcol[0:64, 0:1], 1.0, eng="dve")
        fw.memset(self.mcol[64:128, 1:2], 1.0, eng="dve")
        self._eps = {}
        for ev in (EPS, 1e-5):
            t = fw.alloc(f"eps{len(self._eps)}", [1], F32)
            fw.memset(t.v, float(ev), eng="dve")
            self._eps[ev] = t

    def rmsnorm_T(self, src_fn, gain, dst_fn, ntok, nch=NCH, eps=EPS, blk=TB):
        fw = self.fw
        m = fw.mark()
        dtot = nch * 128
        sq = [fw.alloc(f"nsq{i}", [nch, blk], BF16) for i in range(2)]
        rs = [fw.alloc(f"nrs{i}", [blk], F32) for i in range(2)]
        for bi, t0 in enumerate(range(0, ntok, blk)):
            w = min(blk, ntok - t0)
            q = sq[bi % 2]
            r = rs[bi % 2]
            for c in range(nch):
                xs = src_fn(c, t0, t0 + w)
                fw.act(q[:, c, 0:w], xs, AF.Square)
            bank = fw.bank()
            for c in range(nch):
                fw.matmul(bank[:, 0:w], self.ones_bf.v, q[:, c, 0:w], c == 0, c == nch - 1)
            fw.act(r[:, 0:w], bank[:, 0:w], AF.Ln, bias=self.eps_col(eps), scale=1.0 / dtot)
            fw.act(r[:, 0:w], r[:, 0:w], AF.Exp, scale=-0.5)
            for c in range(nch):
                fw.stt(dst_fn(c, t0, t0 + w), src_fn(c, t0, t0 + w), gain[:, c:c + 1], r[:, 0:w], ALU.mult, ALU.mult)
        fw.release(m)

    def eps_col(self, eps):
        return self._eps[eps].v

    def load_seq(self, s):
        fw = self.fw
        m = fw.mark()
        stg = [fw.alloc(f"xstg{i}", [D], F32) for i in range(2)]
        sems = [fw.sem(f"xstg{i}") for i in range(2)]
        xd = self.d["x"]
        for tt in range(S // 128):
            st = stg[tt % 2]
            fw.dma("sp", st.v, xd[s, tt * 128:(tt + 1) * 128, :], sems[tt % 2])
            for half in range(2):
                bank = fw.bank()
                for q in range(4):
                    c = half * 4 + q
                    fw.transpose(bank[:, q * 128:(q + 1) * 128], st[:, c * 128:(c + 1) * 128], self.ident.v)
                dst = self.xT.sl(half * 4, half * 4 + 4, tt * 128, (tt + 1) * 128)
                fw.copy(dst, bank.v.re("p (a b) -> p a b", a=4), eng="act" if half else "dve")
        fw.release(m)

    def load_mem(self, s):
        fw = self.fw
        self.memnT = fw.alloc("memnT", [NCH, NMEM], BF16)
        m = fw.mark()
        stg = [fw.alloc(f"mstg{i}", [D], F32) for i in range(2)]
        sems = [fw.sem(f"mstg{i}") for i in range(2)]
        memT = fw.alloc("memT", [NCH, NMEM], F32)
        md = self.d["mem"]
        for mt in range(NMEM // 128):
            st = stg[mt % 2]
            fw.dma("sp", st.v, md[s, mt * 128:(mt + 1) * 128, :], sems[mt % 2])
            for half in range(2):
                bank = fw.bank()
                for q in range(4):
                    c = half * 4 + q
                    fw.transpose(bank[:, q * 128:(q + 1) * 128], st[:, c * 128:(c + 1) * 128], self.ident.v)
                fw.copy(memT[:, half * 4:half * 4 + 4, mt * 128:(mt + 1) * 128],
                        bank.v.re("p (a b) -> p a b", a=4), eng="act" if half else "dve")
        self.rmsnorm_T(lambda c, a, b: memT[:, c, a:b], self.vec("mem_norm"),
                       lambda c, a, b: self.memnT[:, c, a:b], NMEM, blk=NMEM)
        fw.release(m)

    def store_seq(self, s, final):
        fw = self.fw
        m = fw.mark()
        if final:
            src = fw.alloc_grid("xfin", NCH, S, F32)
            self.rmsnorm_T(lambda c, a, b: self.xT.c(c, a, b), self.vec("final_norm"),
                           lambda c, a, b: src.c(c, a, b), S)
        else:
            src = self.xT
        stg = [fw.alloc(f"ostg{i}", [D], F32) for i in range(2)]
        sems = [fw.sem(f"ostg{i}") for i in range(2)]
        for tt in range(S // 128):
            st = stg[tt % 2]
            for half in range(2):
                bank = fw.bank()
                for q in range(4):
                    c = half * 4 + q
                    fw.transpose(bank[:, q * 128:(q + 1) * 128], src.c(c, tt * 128, (tt + 1) * 128), self.ident.v)
                fw.copy(st[:, half * 512:(half + 1) * 512], bank.v, eng="act" if half else "dve")
            fw.dma("sp", self.y[s, tt * 128:(tt + 1) * 128, :], st.v, sems[tt % 2])
        fw.release(m)

    def wstream(self, name, free_shape, nslots=2):
        fw = self.fw
        slots = [fw.alloc(f"{name}{i}", free_shape, BF16) for i in range(nslots)]
        sems = [fw.sem(f"{name}{i}") for i in range(nslots)]
        state = {"i": 0}

        def load(src_v, dst_idx=None):
            i = state["i"] % nslots
            state["i"] += 1
            dst = slots[i].v if dst_idx is None else slots[i][dst_idx]
            q = "sp" if src_v.ap.dtype == BF16 else "pool"
            fw.dma(q, dst, src_v, sems[i])
            slots[i]._sem = sems[i]
            return slots[i]
        return load

    @staticmethod
    def wsrc(dt, c0, c1):
        return V([dt], dt.ap.rearrange("(c p) n -> p c n", p=128)[:, :, c0:c1])

    def mlp(self, i):
        fw = self.fw
        m = fw.mark()
        hT = fw.alloc_grid("hT", NCH, S, BF16)
        wup = self.d[f"mlp_up{i}"]
        wdn = self.d[f"mlp_down{i}"]
        ld_up = self.wstream("wup", [NCH, 512])
        ld_dn = self.wstream("wdn", [4, D])
        hid = fw.alloc("hid", [32, TB], BF16)
        rl = [fw.alloc(f"rl{k}", [TB], F32) for k in range(3)]
        pre_up = ld_up(self.wsrc(wup, 0, 512))
        self.rmsnorm_T(lambda c, a, b: self.xT.c(c, a, b), self.vec(f"norm_mlp{i}"),
                       lambda c, a, b: hT.c(c, a, b), S)
        for tb in range(NTB):
            t0, t1 = tb * TB, (tb + 1) * TB
            nxt = pre_up if tb == 0 else ld_up(self.wsrc(wup, 0, 512))
            for g in range(8):
                cur = nxt
                if g + 1 < 8:
                    nxt = ld_up(self.wsrc(wup, (g + 1) * 512, (g + 2) * 512))
                else:
                    nxt_dn = ld_dn(V([wdn], wdn.ap.rearrange("(j p) n -> p j n", p=128)[:, 0:4, :]))
                for jj in range(4):
                    j = g * 4 + jj
                    bank = fw.bank()
                    for c in range(NCH):
                        fw.matmul(bank.v, cur[:, c, jj * 128:(jj + 1) * 128], hT.c(c, t0, t1), c == 0, c == NCH - 1)
                    r = rl[j % 3]
                    fw.act(r.v, bank.v, AF.Relu)
                    fw.tt(hid[:, j, :], r.v, r.v, ALU.mult)
            nxt = nxt_dn
            for g in range(8):
                cur = nxt
                if g + 1 < 8:
                    nxt = ld_dn(V([wdn], wdn.ap.rearrange("(j p) n -> p j n", p=128)[:, (g + 1) * 4:(g + 2) * 4, :]))
                for jj in range(4):
                    j = g * 4 + jj
                    for c in range(NCH):
                        fw.matmul(fw.banks[c].v, cur[:, jj, c * 128:(c + 1) * 128], hid[:, j, :], j == 0, j == 31)
            for c in range(NCH):
                xs = self.xT.c(c, t0, t1)
                fw.tt(xs, fw.banks[c].v, xs, ALU.add)
        fw.release(m)


    def proj_residual(self, wd, src, kch):
        fw = self.fw
        m = fw.mark()
        ld = self.wstream("wpo", [kch, 512])
        for g in range(2):
            w = ld(self.wsrc(wd, g * 512, (g + 1) * 512))
            for mm in range(4):
                c_out = g * 4 + mm
                for tb in range(NTB):
                    t0, t1 = tb * TB, (tb + 1) * TB
                    bank = fw.bank()
                    for c in range(kch):
                        fw.matmul(bank.v, w[:, c, mm * 128:(mm + 1) * 128], src.c(c, t0, t1), c == 0, c == kch - 1)
                    xs = self.xT.c(c_out, t0, t1)
                    fw.tt(xs, bank.v, xs, ALU.add)
        fw.release(m)

    def tab_stream(self, name, dcos, dsin, nslots=2):
        fw = self.fw
        slots = [fw.alloc(f"{name}{i}", [2, TB], F32) for i in range(nslots)]
        sems = [fw.sem(f"{name}{i}") for i in range(nslots)]
        st = {"i": 0, "pend": None}

        def issue(tb):
            i = st["i"] % nslots
            st["i"] += 1
            fw.dma("sp", slots[i][:, 0, :], dcos[:, tb * TB:(tb + 1) * TB], sems[i])
            fw.dma("sp", slots[i][:, 1, :], dsin[:, tb * TB:(tb + 1) * TB], sems[i])
            return (tb, slots[i])

        def get(tb):
            if st["pend"] is None:
                st["pend"] = issue(tb)
            ptb, slot = st["pend"]
            assert ptb == tb
            st["pend"] = issue((tb + 1) % NTB)
            return slot[:, 0, :], slot[:, 1, :]
        return get

    def rope_T(self, dst, ps, cosv, sinv, tmp, u, groups):
        fw = self.fw
        fw.tt(tmp, ps, cosv, ALU.mult)
        for (lo, plo, n) in groups:
            fw.tt(u[lo:lo + n], ps[plo:plo + n], sinv[lo:lo + n], ALU.mult)
        fw.tt(dst, tmp, u, ALU.add, eng="pool")

    def transpose_tm_to_fm(self, src_fn, dst, nch):
        fw = self.fw
        k = 0
        for tt in range(S // 128):
            for c0 in range(0, nch, 4):
                bank = fw.bank()
                bb = V(bank.v.ts, bank.ap.bitcast(BF16))
                for q in range(4):
                    fw.transpose(bb[:, q * 128:(q + 1) * 128], src_fn(tt, c0 + q), self.ident_bf.v)
                fw.copy(dst.sl(c0, c0 + 4, tt * 128, (tt + 1) * 128),
                        bb[:, 0:512].re("p (a b) -> p a b", a=4), eng="act" if k % 2 else "dve")
                k += 1

    def attn_core(self, qr, Ks, negbs, vtm, dv, scale, finalize, LA=2, mid_hook=None, bg=None, bg_every=4,
                  sbanks=(0, 1, 2), one_bank_acc=False):
        fw = self.fw
        nm = len(Ks)
        ee = self._ee
        vts = vtm.ts if isinstance(vtm, V) else vtm.v.ts
        steps = [(qb, jm, kt) for qb in range(NTB) for jm in range(nm) for kt in range(S // 128)]
        n = len(steps)
        Et = {}

        def acc_of(qb, jm):
            if one_bank_acc:
                b0 = fw.banks[4 + (self._acc_par + qb) % 2]
                w = dv + 1
                return [b0[:, k * w:(k + 1) * w] for k in range(4)]
            if nm == 1:
                bi = 4 + 2 * ((self._acc_par + qb) % 2)
            else:
                bi = 4 + 2 * jm
            b0, b1 = fw.banks[bi], fw.banks[bi + 1]
            return [b0[:, 0:dv + 1], b0[:, 256:256 + dv + 1], b1[:, 0:dv + 1], b1[:, 256:256 + dv + 1]]

        def qk(i):
            qb, jm, kt = steps[i]
            sb = fw.banks[sbanks[self._sbk % len(sbanks)]]
            self._sbk += 1
            fw.matmul(sb.v, Ks[jm][:, kt * 128:(kt + 1) * 128], qr[:, qb * TB:(qb + 1) * TB], True, True)
            e = ee[self._eit % len(ee)]
            self._eit += 1
            fw.act(e.v, sb.v, AF.Exp, bias=negbs[jm], scale=scale)
            Et[i] = e

        def pv(i):
            qb, jm, kt = steps[i]
            e = Et.pop(i)
            acc = acc_of(qb, jm)
            for qt in range(4):
                o, l, r = acc[qt].ap, e[:, qt * 128:(qt + 1) * 128].ap, vtm[:, kt, :].ap
                st = (kt == 0 and (qt == 0 if one_bank_acc else qt % 2 == 0))
                fw.emit("pe", lambda en, o=o, l=l, r=r, st=st, sp=(kt == 15): en.matmul(
                    o, l, r, start=st, stop=sp, skip_group_check=True),
                    e.v.ts + vts, acc[qt].ts, pe_acc=not st)

        deferred = []
        DEFER = 8
        for i in range(min(LA, n)):
            qk(i)
        for i in range(n):
            if i + LA < n:
                qk(i + LA)
            pv(i)
            while deferred and deferred[0][0] <= i:
                deferred.pop(0)[1]()
            qb, jm, kt = steps[i]
            if bg is not None and i % bg_every == bg_every - 1:
                next(bg, None)
            if jm == nm - 1 and kt == S // 128 - 1:
                cont = finalize(qb, [acc_of(qb, jj) for jj in range(nm)])
                if cont is not None:
                    deferred.append((i + DEFER, cont))
                if qb == 1 and mid_hook is not None:
                    mid_hook()
        while deferred:
            deferred.pop(0)[1]()
        if bg is not None:
            for _ in bg:
                pass
        if nm == 1:
            self._acc_par += NTB

    def mix1(self):
        fw = self.fw
        m = fw.mark()
        lam_init = 0.8 - 0.6 * math.exp(-0.3 * 1)
        scale = 64 ** -0.5
        hT = fw.alloc_grid("hT", NCH, S, BF16)
        self.rmsnorm_T(lambda c, a, b: self.xT.c(c, a, b), self.vec("norm_mix1"),
                       lambda c, a, b: hT.c(c, a, b), S)
        ao = fw.alloc("ao", [S // 128, D], BF16)
        m2 = fw.mark()
        tabD = self.tab_stream("tabD", self.d["ropeD_cos"], self.d["ropeD_sin"])
        lq = self.vec("lqk")
        sm = fw.alloc("lam_sm", [8], F32)
        prod = fw.alloc("lam_prod", [128], F32)
        fw.tt(prod[:, 0:64], lq[:, 0:64], lq[:, 64:128], ALU.mult)
        fw.tt(prod[:, 64:128], lq[:, 128:192], lq[:, 192:256], ALU.mult)
        fw.reduce(sm[:, 0:1], prod[:, 0:64], ALU.add)
        fw.reduce(sm[:, 1:2], prod[:, 64:128], ALU.add)
        fw.act(sm[:, 2:4], sm[:, 0:2], AF.Exp)
        fw.tt(sm[:, 4:5], sm[:, 3:4], sm[:, 2:3], ALU.subtract)
        fw.ts(sm[:, 5:6], sm[:, 4:5], -lam_init, None, op0=ALU.add)
        neglam = sm[:, 5:6]
        sub = fw.alloc("subln_s", [128], F32)
        fw.ts(sub.v, self.vec("subln"), 1.0 - lam_init, None, op0=ALU.mult)

        wqkv = self.d["od_w_qkv"]
        ldw = self.wstream("wqkv", [NCH, 384])
        src3 = wqkv.ap.rearrange("(c p) n -> p c n", p=128)
        kr = fw.alloc("kr", [S], BF16)
        bufs = []
        for bi in range(2):
            B = dict(qr=fw.alloc(f"qr{bi}", [S], BF16), kA=fw.alloc(f"kA{bi}", [S], BF16),
                     kB=fw.alloc(f"kB{bi}", [S], BF16), vtm=fw.alloc(f"vtm{bi}", [S // 128, 129], BF16),
                     negb=fw.alloc(f"negb{bi}", [2], F32), mx=fw.alloc(f"mx{bi}", [16], F32))
            fw.memset(B["vtm"][:, :, 128:129], 1.0, eng="dve")
            bufs.append(B)
        rtmp = [fw.alloc("rtmp0", [TB], F32)] * 2
        ru = [fw.alloc("ru0", [TB], F32)] * 2
        self._ee = [fw.alloc(f"ee{k}", [TB], BF16) for k in range(3)]
        self._eit = 0
        self._sbk = 0
        self._acc_par = 0
        fin = fw.alloc("fin", [4, 8], F32)
        ot = [fw.alloc("ot0", [128], F32)] * 2
        oo = [fw.alloc(f"oo{k}", [128], F32) for k in range(4)]
        junk32 = fw.alloc("junk32", [128], F32)
        groups = [(0, 32, 32), (32, 0, 32), (64, 96, 32), (96, 64, 32)]
        rkc = [0]

        PBs = [fw.banks[2], fw.banks[3]]
        pbc = [0]

        def nextpb():
            pbc[0] += 1
            return PBs[pbc[0] % len(PBs)]

        def proj_rope_unit(slot, which, dst, tb):
            t0, t1 = tb * TB, (tb + 1) * TB
            PB = nextpb()
            for c in range(NCH):
                fw.matmul(PB.v, slot[:, c, which * 128:(which + 1) * 128], hT.c(c, t0, t1), c == 0, c == NCH - 1)
            cv, sv = tabD(tb)
            self.rope_T(dst[:, t0:t1], PB.v, cv, sv,
                        rtmp[rkc[0] % 2].v, ru[rkc[0] % 2].v, groups)
            rkc[0] += 1

        def bounds_unit(B, wi, hj, tb):
            mx = B["mx"]
            mk = (self.maskA, self.maskB)[hj]
            PB = nextpb()
            fw.matmul(PB.v, mk.v, kr[:, tb * TB:(tb + 1) * TB], True, True)
            fw.reduce(mx[:, tb + 8 * hj:tb + 8 * hj + 1], PB.v, ALU.max)
            if tb == NTB - 1:
                fw.reduce(mx[:, 4 + 8 * hj + wi:5 + 8 * hj + wi], mx[:, 8 * hj:8 * hj + 4], ALU.max)

        def prologue_gen(hp, B):
            slot = ldw(V([wqkv], src3[:, :, hp * 128:(hp + 1) * 128]), (slice(None), slice(None), slice(0, 128)))
            self._dma_same_slot(slot, (slice(None), slice(None), slice(128, 256)),
                                V([wqkv], src3[:, :, D + hp * 128:D + (hp + 1) * 128]))
            self._dma_same_slot(slot, (slice(None), slice(None), slice(256, 384)),
                                V([wqkv], src3[:, :, 2 * D + hp * 128:2 * D + (hp + 1) * 128]))
            yield
            for tb in range(NTB):
                proj_rope_unit(slot, 0, B["qr"], tb)
                yield
                yield
            for tb in range(NTB):
                proj_rope_unit(slot, 1, kr, tb)
                yield
                yield
            fw.act(B["kA"].v, kr.v, AF.Copy, scale=self.mcol[:, 0:1])
            fw.act(B["kB"].v, kr.v, AF.Copy, scale=self.mcol[:, 1:2])
            vtm = B["vtm"]
            for t4 in range(4):
                PB = nextpb()
                for q in range(4):
                    tt = t4 * 4 + q
                    for c in range(NCH):
                        fw.matmul(PB[:, q * 128:(q + 1) * 128], hT.c(c, tt * 128, (tt + 1) * 128),
                                  slot[:, c, 256:384], c == 0, c == NCH - 1)
                fw.copy(vtm[:, t4 * 4:(t4 + 1) * 4, 0:128], PB.v.re("p (a b) -> p a b", a=4), eng="dve")
                yield
            fw.tt(kr.v, kr.v, kr.v, ALU.mult, eng="pool")
            for hj in range(2):
                for tb in range(NTB):
                    bounds_unit(B, 1, hj, tb)
                    yield
            fw.tt(kr.v, B["qr"].v, B["qr"].v, ALU.mult, eng="pool")
            for hj in range(2):
                for tb in range(NTB):
                    bounds_unit(B, 0, hj, tb)
                    yield
            mx = B["mx"]
            for hj in range(2):
                fw.tt(mx[:, 6 + 8 * hj:7 + 8 * hj], mx[:, 4 + 8 * hj:5 + 8 * hj], mx[:, 5 + 8 * hj:6 + 8 * hj], ALU.add)
                fw.ts(B["negb"][:, hj:hj + 1], mx[:, 6 + 8 * hj:7 + 8 * hj], -0.5 * scale, None, op0=ALU.mult)

        accsb = fw.alloc("accsb", [4, 512], F32)
        rden = fw.alloc("rden", [4, 2], F32)
        ssq = fw.alloc("ssq", [8], F32)

        def make_finalize(hp):
            def finalize(qb, accs):
                for b in range(4):
                    fw.copy(accsb[:, b, :], fw.banks[4 + b].v, eng="act" if b % 2 else "dve")
                fw.recip(rden.v, accsb[:, :, 128:512:256])
                fw.ts(rden[:, 2:4, :], rden[:, 2:4, :], neglam, None, op0=ALU.mult)
                for qt in range(4):
                    b, off = qt // 2, (qt % 2) * 256
                    o_o = oo[qt]
                    fw.ts(ot[0].v, accsb[:, b, off:off + 128], rden[:, b, qt % 2:qt % 2 + 1], None, op0=ALU.mult)
                    fw.stt(o_o.v, accsb[:, 2 + b, off:off + 128], rden[:, 2 + b, qt % 2:qt % 2 + 1], ot[0].v, ALU.mult, ALU.add)
                    oa, ja, sa = o_o.ap, junk32.ap, ssq[:, qt:qt + 1].ap
                    fw.emit("dve", lambda e, oa=oa, ja=ja, sa=sa: e.scalar_tensor_tensor(
                        ja, oa, 1.0, oa, ALU.mult, ALU.mult, accum_out=sa), o_o.v.ts, junk32.v.ts + ssq.v.ts)

                def part2():
                    fw.act(ssq[:, 4:8], ssq[:, 0:4], AF.Ln, bias=self.eps_col(1e-5), scale=1.0 / 128)
                    fw.act(ssq[:, 4:8], ssq[:, 4:8], AF.Exp, scale=-0.5)
                    for qt in range(4):
                        tt = qb * 4 + qt
                        fw.stt(ao[:, tt, hp * 128:(hp + 1) * 128], oo[qt].v, ssq[:, 4 + qt:5 + qt], sub.v, ALU.mult, ALU.mult)
                return part2
            return finalize

        for _ in prologue_gen(0, bufs[0]):
            pass
        for hp in range(8):
            B = bufs[hp % 2]
            bg = prologue_gen(hp + 1, bufs[(hp + 1) % 2]) if hp + 1 < 8 else None
            self.attn_core(B["qr"], [B["kA"], B["kB"]], [B["negb"][:, 0:1], B["negb"][:, 1:2]],
                           B["vtm"], 128, scale, make_finalize(hp), bg=bg, bg_every=3, LA=1, sbanks=(0, 1))

        fw.release(m2)
        self.transpose_tm_to_fm(lambda tt, c: ao[:, tt, c * 128:(c + 1) * 128], hT, NCH)
        self.proj_residual(self.d["od_w_out"], hT, NCH)
        fw.release(m)

    def _dma_same_slot(self, slot, idx, src_v):
        fw = self.fw
        sem = slot._sem
        q = "sp" if src_v.ap.dtype == BF16 else "pool"
        fw.dma(q, slot[idx], src_v, sem)

    def filter_phase(self):
        fw = self.fw
        m = fw.mark()
        N2 = 2.0 / 4096.0
        s_c = fw.sem("filt")
        w1 = fw.alloc("fw1", [128], F32)
        w2 = fw.alloc("fw2", [128], F32)
        w3 = fw.alloc("fw3", [2048], F32)
        ft = fw.alloc("feats", [S], F32)
        winf = fw.alloc("winf", [16, 512], F32)
        winb = fw.alloc("winb", [16, 512], F32)
        fw.dma("sp", w1.v, self.d["hy_w1p"].v, s_c)
        fw.dma("sp", w2.v, self.d["hy_w2p"].v, s_c)
        fw.dma("sp", w3.v, self.d["hy_w3p"].v, s_c)
        fw.dma("sp", ft.v, self.d["featsT"].v, s_c)
        fw.dma("sp", winf.v, V([self.d["win_f"]], self.d["win_f"].ap.rearrange("(t p) c -> p t c", p=128)), s_c)
        fw.dma("sp", winb.v, V([self.d["win_b"]], self.d["win_b"].ap.rearrange("(t p) c -> p t c", p=128)), s_c)
        a1 = fw.alloc("a1T", [S], F32)
        a2 = fw.alloc("a2T", [S], F32)
        pre = [fw.alloc(f"pre{k}", [TB], F32) for k in range(2)]
        wr = [fw.alloc(f"wr{k}", [TB], F32) for k in range(2)]
        PI = math.pi
        for (wt, src, dst, bn) in ((w1, ft, a1, "hy_b1"), (w2, a1, a2, "hy_b2")):
            for tb in range(NTB):
                bank = fw.bank()
                fw.matmul(bank.v, wt.v, src[:, tb * TB:(tb + 1) * TB], True, True)
                p = pre[tb % 2]
                fw.ts(p.v, bank.v, self.vec(bn), self.vec("hy_freq"), op0=ALU.add, op1=ALU.mult)
                w1_, w2_ = wr
                fw.ts(w1_.v, p.v, PI, -2 * PI, op0=ALU.is_gt, op1=ALU.mult)
                fw.ts(w2_.v, p.v, -PI, 2 * PI, op0=ALU.is_lt, op1=ALU.mult)
                fw.tt(p.v, p.v, w1_.v, ALU.add)
                fw.tt(p.v, p.v, w2_.v, ALU.add)
                fw.act(dst[:, tb * TB:(tb + 1) * TB], p.v, AF.Sin)
        ed = [[fw.alloc(f"ed{o}{k}", [16, 512], BF16) for k in range(2)] for o in range(2)]
        t12 = [fw.alloc(f"t12{k}", [512], F32) for k in range(4)]
        k = 0
        for pt in range(16):
            for o in range(2):
                bf = fw.bank()
                fw.matmul(bf.v, a2[:, pt * 128:(pt + 1) * 128], w3[:, (2 * o) * 512:(2 * o + 1) * 512], True, True)
                bb = fw.bank()
                fw.matmul(bb.v, a2[:, pt * 128:(pt + 1) * 128], w3[:, (2 * o + 1) * 512:(2 * o + 2) * 512], True, True)
                t1, t2 = t12[(k * 2) % 4], t12[(k * 2 + 1) % 4]
                k += 1
                fw.tt(t1.v, bf.v, winf[:, pt, :], ALU.mult)
                fw.tt(t2.v, bb.v, winb[:, pt, :], ALU.mult)
                fw.tt(ed[o][0][:, pt, :], t1.v, t2.v, ALU.add, eng="dve")
                fw.tt(ed[o][1][:, pt, :], t1.v, t2.v, ALU.subtract, eng="pool")
        ldF = self.dft_stream()
        kst = [fw.alloc(f"kst{k}", [2, 512], BF16) for k in range(2)]
        ksem = [fw.sem(f"kst{k}") for k in range(2)]
        it = 0
        for fc in range(16):
            slot = ldF(self.d["dftF"][fc])
            for o in range(2):
                be = fw.bank()
                for st in range(16):
                    fw.matmul(be.v, slot[:, 0, st, :], ed[o][0][:, st, :], st == 0, st == 15)
                bd = fw.bank()
                for st in range(16):
                    fw.matmul(bd.v, slot[:, 1, st, :], ed[o][1][:, st, :], st == 0, st == 15)
                kt_ = kst[it % 2]
                fw.act(kt_[:, 0, :], be.v, AF.Copy, scale=N2)
                fw.copy(kt_[:, 1, :], bd.v, eng="dve") if False else fw.ts(kt_[:, 1, :], bd.v, N2, None, op0=ALU.mult)
                fw.dma("sp", self.kspec[o][fc].v, kt_.v, ksem[it % 2])
                it += 1
        fw.release(m)

    def dft_stream(self):
        fw = self.fw
        slots = [fw.alloc(f"dft{i}", [2, 16, 128], BF16) for i in range(2)]
        sems = [fw.sem(f"dft{i}") for i in range(2)]
        st = {"i": 0}

        def load(src_v, shape4=None):
            i = st["i"] % 2
            st["i"] += 1
            t = slots[i]
            dst = t.v if shape4 is None else t.v.re("p a b c -> p (a b c)").re("p (a b c) -> p a b c", a=shape4[0], b=shape4[1])
            fw.dma("sp", dst, src_v, sems[i])
            return V([t], dst.ap)
        return load

    def mix0(self):
        fw = self.fw
        m = fw.mark()
        omlaT = fw.alloc_grid("omlaT", 4, S, BF16)
        m1 = fw.mark()
        hT = fw.alloc_grid("hT", NCH, S, BF16)
        self.rmsnorm_T(lambda c, a, b: self.xT.c(c, a, b), self.vec("norm_mix0"),
                       lambda c, a, b: hT.c(c, a, b), S)
        self.mla(hT, omlaT)
        fw.release(m1)
        vx = fw.alloc_grid("vx", 12, S, BF16)
        m1 = fw.mark()
        hT = fw.alloc_grid("hT", NCH, S, BF16)
        self.rmsnorm_T(lambda c, a, b: self.xT.c(c, a, b), self.vec("norm_mix0"),
                       lambda c, a, b: hT.c(c, a, b), S)
        self.hyena_inproj(hT, vx)
        fw.release(m1)
        self.hyena_conv(vx)
        class Cat:
            def c(_, c, t0, t1):
                return vx.c(c, t0, t1) if c < 4 else omlaT.c(c - 4, t0, t1)
        self.proj_residual(self.d["ev_w_out"], Cat(), NCH)
        fw.release(m)

    def mla(self, hT, omlaT):
        fw = self.fw
        m = fw.mark()
        scale = 96 ** -0.5
        win = self.d["ev_w_in"]
        src3 = win.ap.rearrange("(c p) n -> p c n", p=128)
        s_w = fw.sem("mlaw")
        wm = fw.alloc("wm", [NCH, 640], BF16)
        wuq = fw.alloc("wuq", [2, 8, 128], BF16)
        wuqs = fw.alloc("wuqs", [2, 8, 128], BF16)
        wkk = fw.alloc("wkk", [8, 128], BF16)
        wkv = fw.alloc("wkv", [8, 64], BF16)
        for t in (wm, wuq, wuqs, wkk):
            fw.memset(t.v, 0.0, eng="pool")
        fw.dma("pool", wm[:, :, 0:384], V([win], src3[:, :, 1536:1920]), s_w)
        fw.dma("pool", wm[:, :, 448:480], V([win], src3[:, :, 1920:1952]), s_w)
        fw.dma("pool", wm[:, :, 576:592], V([win], src3[:, :, 1936:1952]), s_w)
        fw.dma("pool", wm[:, :, 592:608], V([win], src3[:, :, 1920:1936]), s_w)
        uq = self.d["mla_w_uq"]
        uq4 = uq.ap.rearrange("(c p) (h d) -> p c h d", p=128, d=96)
        for kc in range(2):
            fw.dma("pool", wuq[:, kc, :, 0:96], V([uq], uq4[:, kc, :, :]), s_w)
            fw.dma("pool", wuqs[:, kc, :, 64:80], V([uq], uq4[:, kc, :, 80:96]), s_w)
            fw.dma("pool", wuqs[:, kc, :, 80:96], V([uq], uq4[:, kc, :, 64:80]), s_w)
        ukv = self.d["mla_w_ukv"]
        ukv3 = ukv.ap.rearrange("p (h d) -> p h d", d=128)
        fw.dma("pool", wkk[:, :, 0:64], V([ukv], ukv3[:, :, 0:64]), s_w)
        fw.dma("pool", wkv.v, V([ukv], ukv3[:, :, 64:128]), s_w)
        tabR = self.tab_stream("tabR", self.d["ropeR_cos"], self.d["ropeR_sin"])
        cqn = fw.alloc("cqn", [2, S], BF16)
        ckvn = fw.alloc("ckvn", [1, S], BF16)
        kpe = fw.alloc("kpe", [S], BF16)
        rtmp = [fw.alloc("rtmp0", [TB], F32)] * 2
        ru = [fw.alloc("ru0", [TB], F32)] * 2
        for tb in range(NTB):
            t0, t1 = tb * TB, (tb + 1) * TB
            b1 = fw.bank()
            for c in range(NCH):
                fw.matmul(b1.v, wm[:, c, 384:512], hT.c(c, t0, t1), c == 0, c == NCH - 1)
            b2 = fw.bank()
            for c in range(NCH):
                fw.matmul(b2.v, wm[:, c, 512:640], hT.c(c, t0, t1), c == 0, c == NCH - 1)
            tm, uu = rtmp[tb % 2], ru[tb % 2]
            cv, sv = tabR(tb)
            fw.tt(tm[64:96, :], b1[64:96, :], cv[64:96, :], ALU.mult)
            fw.tt(uu[64:96, :], b2[64:96, :], sv[64:96, :], ALU.mult)
            fw.tt(kpe[64:96, t0:t1], tm[64:96, :], uu[64:96, :], ALU.add, eng="pool")
        for (blks, gname, dstt) in (((0, 1), "q_norm", cqn), ((2,), "kv_norm", ckvn)):
            m2 = fw.mark()
            cq = fw.alloc("cq", [len(blks), S], F32)
            for bi, blk in enumerate(blks):
                for tb in range(NTB):
                    t0, t1 = tb * TB, (tb + 1) * TB
                    bank = fw.bank()
                    for c in range(NCH):
                        fw.matmul(bank.v, wm[:, c, blk * 128:(blk + 1) * 128], hT.c(c, t0, t1), c == 0, c == NCH - 1)
                    fw.copy(cq[:, bi, t0:t1], bank.v, eng="act")
            self.rmsnorm_T(lambda c, a, b: cq[:, c, a:b], self.vec(gname), lambda c, a, b: dstt[:, c, a:b], S, nch=len(blks))
            fw.release(m2)
        sq = fw.alloc("sq", [S], BF16)
        bufs = []
        for bi in range(2):
            B = dict(qh=fw.alloc(f"qh{bi}", [S], BF16), kh=fw.alloc(f"kh{bi}", [S], BF16),
                     negb=fw.alloc(f"negb{bi}", [1], F32), mx=fw.alloc(f"mx{bi}", [8], F32),
                     vt2=fw.alloc(f"vt2{bi}", [S // 128, 2, 65], BF16))
            fw.memset(B["qh"].v, 0.0, eng="pool")
            fw.memset(B["kh"].v, 0.0, eng="pool")
            fw.memset(B["vt2"][:, :, :, 64:65], 1.0, eng="dve")
            bufs.append(B)
        ao2 = fw.alloc("ao2", [S // 128, 128], BF16)
        self._ee = [fw.alloc(f"ee{k}", [TB], BF16) for k in range(3)]
        self._eit = 0
        self._sbk = 0
        self._acc_par = 0
        fin = fw.alloc("fin", [4], F32)
        PBs = [fw.banks[3], fw.banks[6], fw.banks[7]]
        pbc = [0]

        def nextpb():
            pbc[0] += 1
            return PBs[pbc[0] % len(PBs)]

        def prologue_gen(h):
            hp, hj = h // 2, h % 2
            B = bufs[h % 2]
            qh, kh, mx = B["qh"], B["kh"], B["mx"]
            if hj == 0:
                vt2 = bufs[hp % 2]["vt2"]
                for t4 in range(4):
                    PB = nextpb()
                    for q in range(4):
                        tt = t4 * 4 + q
                        fw.matmul(PB[:, q * 128:(q + 1) * 128], ckvn[:, 0, tt * 128:(tt + 1) * 128],
                                  wkv[:, 2 * hp:2 * hp + 2, :], True, True)
                    fw.copy(vt2[:, t4 * 4:(t4 + 1) * 4, :, 0:64],
                            PB.v.re("p (a b c) -> p a b c", a=4, b=2), eng="dve")
                    yield
            for tb in range(NTB):
                t0, t1 = tb * TB, (tb + 1) * TB
                cv, sv = tabR(tb)
                PB = nextpb()
                for kc in range(2):
                    fw.matmul(PB.v, wuq[:, kc, h, :], cqn[:, kc, t0:t1], kc == 0, kc == 1)
                fw.copy(qh[0:64, t0:t1], PB[0:64, :], eng="dve")
                tm, uu = rtmp[tb % 2], ru[tb % 2]
                fw.tt(tm[64:96, :], PB[64:96, :], cv[64:96, :], ALU.mult)
                yield
                PB = nextpb()
                for kc in range(2):
                    fw.matmul(PB.v, wuqs[:, kc, h, :], cqn[:, kc, t0:t1], kc == 0, kc == 1)
                fw.tt(uu[64:96, :], PB[64:96, :], sv[64:96, :], ALU.mult)
                fw.tt(qh[64:96, t0:t1], tm[64:96, :], uu[64:96, :], ALU.add, eng="pool")
                yield
                PB = nextpb()
                fw.matmul(PB.v, wkk[:, h, :], ckvn[:, 0, t0:t1], True, True)
                fw.copy(kh[0:64, t0:t1], PB[0:64, :], eng="dve")
                yield
            fw.copy(kh[64:96, :], kpe[64:96, :], eng="pool")
            for wi, srct in enumerate((qh, kh)):
                fw.tt(sq.v, srct.v, srct.v, ALU.mult, eng="pool")
                for tb in range(NTB):
                    PB = nextpb()
                    fw.matmul(PB.v, self.ones_bf.v, sq[:, tb * TB:(tb + 1) * TB], True, True)
                    fw.reduce(mx[:, tb:tb + 1], PB.v, ALU.max)
                    yield
                fw.reduce(mx[:, 4 + wi:5 + wi], mx[:, 0:4], ALU.max)
            fw.tt(mx[:, 6:7], mx[:, 4:5], mx[:, 5:6], ALU.add)
            fw.ts(B["negb"].v, mx[:, 6:7], -0.5 * scale, None, op0=ALU.mult)

        def make_finalize(hj):
            def finalize(qb, accs):
                for qt in range(4):
                    tt = qb * 4 + qt
                    a1 = accs[0][qt]
                    fw.recip(fin[:, qt:qt + 1], a1[:, 64:65])
                    fw.ts(ao2[:, tt, hj * 64:(hj + 1) * 64], a1[:, 0:64], fin[:, qt:qt + 1], None, op0=ALU.mult)
                return None
            return finalize

        for _ in prologue_gen(0):
            pass
        if not self._precast_done:
            self.precast()
        for h in range(8):
            hp, hj = h // 2, h % 2
            B = bufs[h % 2]
            bg = prologue_gen(h + 1) if h + 1 < 8 else None
            self.attn_core(B["qh"], [B["kh"]], [B["negb"].v], bufs[hp % 2]["vt2"][:, :, hj, :], 64, scale,
                           make_finalize(hj), bg=bg, bg_every=2, sbanks=(0, 1, 2), one_bank_acc=True)
            if hj == 1:
                for t4 in range(4):
                    PB = nextpb()
                    bb = V(PB.v.ts, PB.ap.bitcast(BF16))
                    for q in range(4):
                        fw.transpose(bb[:, q * 128:(q + 1) * 128], ao2[:, t4 * 4 + q, :], self.ident_bf.v)
                    fw.copy(omlaT.c(hp, t4 * 512, (t4 + 1) * 512), bb[:, 0:512], eng="dve")
        fw.release(m)

    def hyena_inproj(self, hT, vx):
        fw = self.fw
        m = fw.mark()
        win = self.d["ev_w_in"]
        ld = self.wstream("whin", [NCH, 512])
        ub = [fw.alloc("ub0", [S + 2], F32)] * 2
        ac = [fw.alloc("uacc0", [S], F32)] * 2
        for u in ub[:1]:
            fw.memset(u[:, 0:1], 0.0, eng="dve")
            fw.memset(u[:, S + 1:S + 2], 0.0, eng="dve")
        for g in range(3):
            w = ld(self.wsrc(win, g * 512, (g + 1) * 512))
            for mm in range(4):
                hc = g * 4 + mm
                u, a = ub[hc % 2], ac[hc % 2]
                for tb in range(NTB):
                    bank = fw.bank()
                    for c in range(NCH):
                        fw.matmul(bank.v, w[:, c, mm * 128:(mm + 1) * 128], hT.c(c, tb * TB, (tb + 1) * TB), c == 0, c == NCH - 1)
                    fw.copy(u[:, 1 + tb * TB:1 + (tb + 1) * TB], bank.v, eng="act")
                fw.ts(a.v, u[:, 0:S], self.vec("conv_w0", hc, hc + 1), self.vec("conv_b", hc, hc + 1), op0=ALU.mult, op1=ALU.add)
                fw.stt(a.v, u[:, 1:S + 1], self.vec("conv_w1", hc, hc + 1), a.v, ALU.mult, ALU.add)
                fw.stt(vx.sl(hc, hc + 1, 0, S), u[:, 2:S + 2], self.vec("conv_w2", hc, hc + 1), a.v, ALU.mult, ALU.add)
        fw.release(m)

    def hyena_conv(self, vx):
        fw = self.fw
        m = fw.mark()
        zT = fw.alloc("zT", [16, 512], BF16)
        Ut = fw.alloc("Ut", [16, 512], BF16)
        Vt = fw.alloc("Vt", [16, 512], BF16)
        ldF = self.dft_stream()
        ksl = [fw.alloc("ksl0", [2, 512], BF16)] * 2
        ksem = [fw.sem("ksl0")] * 2
        tq = [fw.alloc(f"tq{k}", [512], F32) for k in range(2)] * 2
        kit = 0
        for o in range(2):
            for tt in range(16):
                bank = fw.bank(0, 4)
                bb = V(bank.v.ts, bank.ap.bitcast(BF16))
                for c in range(4):
                    fw.transpose(bb[:, c * 128:(c + 1) * 128], vx.c(c, tt * 128, (tt + 1) * 128), self.ident_bf.v)
                fw.copy(zT[:, tt, :], bb[:, 0:512], eng="act" if tt % 2 else "dve")
            for fc in range(16):
                slot = ldF(self.d["dftF"][fc])
                ks = ksl[kit % 2]
                fw.dma("sp", ks.v, self.kspec[o][fc].v, ksem[kit % 2])
                kit += 1
                ba = fw.bank(0, 4)
                for st in range(16):
                    fw.matmul(ba.v, slot[:, 0, st, :], zT[:, st, :], st == 0, st == 15)
                bb_ = fw.bank(0, 4)
                for st in range(16):
                    fw.matmul(bb_.v, slot[:, 1, st, :], zT[:, st, :], st == 0, st == 15)
                t1, t2 = tq[0], tq[1]
                fw.tt(t1.v, ba.v, ks[:, 0, :], ALU.mult)
                fw.tt(t2.v, bb_.v, ks[:, 1, :], ALU.mult)
                fw.tt(Ut[:, fc, :], t1.v, t2.v, ALU.subtract, eng="pool")
                fw.tt(t1.v, ba.v, ks[:, 1, :], ALU.mult)
                fw.tt(t2.v, bb_.v, ks[:, 0, :], ALU.mult)
                fw.tt(Vt[:, fc, :], t1.v, t2.v, ALU.add, eng="pool")
            gate0 = 4 + 4 * o
            for tb in range(NTB):
                t0, t1_ = tb * TB, (tb + 1) * TB
                for fg in range(4):
                    slot = ldF(self.d["dftI"][tb, fg], shape4=(4, 2))
                    for fci in range(4):
                        fc = fg * 4 + fci
                        for cc in range(4):
                            fw.matmul(fw.banks[4 + cc].v, Ut[:, fc, cc * 128:(cc + 1) * 128], slot[:, fci, 0, :], fc == 0, False)
                            fw.matmul(fw.banks[4 + cc].v, Vt[:, fc, cc * 128:(cc + 1) * 128], slot[:, fci, 1, :], False, fc == 15)
                for cc in range(4):
                    zc = vx.c(cc, t0, t1_)
                    tmp = tq[cc % 2]
                    fw.stt(tmp.v, zc, self.vec(f"hy_skip{o}", cc, cc + 1), fw.banks[4 + cc].v, ALU.mult, ALU.add)
                    fw.tt(zc, tmp.v, vx.c(gate0 + cc, t0, t1_), ALU.mult, eng="pool")
        fw.release(m)

    def cross(self, i):
        fw = self.fw
        m = fw.mark()
        scale = 256 ** -0.5
        wq, wkv, wo = self.d[f"xa_wq{i}"], self.d[f"xa_wkv{i}"], self.d[f"xa_wo{i}"]
        ld = self.wstream("wx", [NCH, 512])
        pre_k = [ld(self.wsrc(wkv, g * 512, (g + 1) * 512)) for g in range(2)]
        self.load_mem(self.cur_seq)
        hT = fw.alloc_grid("hT", NCH, S, BF16)
        self.rmsnorm_T(lambda c, a, b: self.xT.c(c, a, b), self.vec(f"norm_cross{i}"),
                       lambda c, a, b: hT.c(c, a, b), S)
        kT = fw.alloc("kT", [NCH, NMEM], BF16)
        vtm = fw.alloc("vtm", [2, D], BF16)
        qT = fw.alloc_grid("qT", NCH, S, BF16)
        for g in range(2):
            w = pre_k[g]
            for mm in range(4):
                cc = g * 4 + mm
                bank = fw.bank()
                for c in range(NCH):
                    fw.matmul(bank[:, 0:NMEM], w[:, c, mm * 128:(mm + 1) * 128], self.memnT[:, c, :], c == 0, c == NCH - 1)
                fw.copy(kT[:, cc, :], bank[:, 0:NMEM], eng="act")
        for g in range(2):
            w = ld(self.wsrc(wkv, D + g * 512, D + (g + 1) * 512))
            for mt in range(2):
                bank = fw.bank()
                for c in range(NCH):
                    fw.matmul(bank.v, self.memnT[:, c, mt * 128:(mt + 1) * 128], w[:, c, :], c == 0, c == NCH - 1)
                fw.copy(vtm[:, mt, g * 512:(g + 1) * 512], bank.v, eng="act")
        for g in range(2):
            w = ld(self.wsrc(wq, g * 512, (g + 1) * 512))
            for mm in range(4):
                cc = g * 4 + mm
                for tb in range(NTB):
                    bank = fw.bank()
                    for c in range(NCH):
                        fw.matmul(bank.v, w[:, c, mm * 128:(mm + 1) * 128], hT.c(c, tb * TB, (tb + 1) * TB), c == 0, c == NCH - 1)
                    fw.copy(qT.c(cc, tb * TB, (tb + 1) * TB), bank.v, eng="act" if tb % 2 else "dve")
        negb = fw.alloc("negb", [4], F32)
        sqb = [fw.alloc(f"sqb{k}", [2, TB], BF16) for k in range(2)]
        mx = fw.alloc("mx", [8], F32)
        for h in range(4):
            for tb in range(NTB):
                sq = sqb[tb % 2]
                for u in range(2):
                    qs = qT.c(2 * h + u, tb * TB, (tb + 1) * TB)
                    fw.act(sq[:, u, :], qs, AF.Square)
                bank = fw.bank()
                for u in range(2):
                    fw.matmul(bank.v, self.ones_bf.v, sq[:, u, :], u == 0, u == 1)
                fw.reduce(mx[:, tb:tb + 1], bank.v, ALU.max)
            sq = sqb[0]
            for u in range(2):
                ks = kT[:, 2 * h + u, :]
                fw.act(sq[:, u, 0:NMEM], ks, AF.Square)
            bank = fw.bank()
            for u in range(2):
                fw.matmul(bank[:, 0:NMEM], self.ones_bf.v, sq[:, u, 0:NMEM], u == 0, u == 1)
            fw.reduce(mx[:, 4:5], bank[:, 0:NMEM], ALU.max)
            fw.reduce(mx[:, 5:6], mx[:, 0:4], ALU.max)
            fw.tt(mx[:, 6:7], mx[:, 4:5], mx[:, 5:6], ALU.add)
            fw.ts(negb[:, h:h + 1], mx[:, 6:7], -0.5 * scale, None, op0=ALU.mult)
        oT = hT
        ee = [fw.alloc(f"ee{k}", [TB], BF16) for k in range(4)]
        rd = [fw.alloc(f"rd{k}", [TB], F32) for k in range(2)]
        its = [(h, tb) for h in range(4) for tb in range(NTB)]
        ee = ee + [fw.alloc(f"ee{k}", [TB], BF16) for k in range(4, 6)]

        def qk_exp(n):
            h, tb = its[n]
            t0, t1 = tb * TB, (tb + 1) * TB
            es = []
            for kt in range(2):
                bank = fw.banks[(n * 2 + kt) % 4]
                for u in range(2):
                    fw.matmul(bank.v, kT[:, 2 * h + u, kt * 128:(kt + 1) * 128], qT.c(2 * h + u, t0, t1), u == 0, u == 1)
                e = ee[(n * 2 + kt) % 6]
                fw.act(e.v, bank.v, AF.Exp, bias=negb[:, h:h + 1], scale=scale)
                es.append(e)
            return es

        pend = qk_exp(0)
        for n, (h, tb) in enumerate(its):
            t0, t1 = tb * TB, (tb + 1) * TB
            es = pend
            if n + 1 < len(its):
                pend = qk_exp(n + 1)
            bd = fw.bank(4, 8)
            for kt in range(2):
                fw.matmul(bd.v, self.ones_bf.v, es[kt].v, kt == 0, kt == 1)
            r = rd[n % 2]
            fw.act(r.v, bd.v, AF.Ln)
            fw.act(r.v, r.v, AF.Exp, scale=-1.0)
            for dc in range(2):
                bo = fw.bank(4, 8)
                for kt in range(2):
                    fw.matmul(bo.v, vtm[:, kt, h * 256 + dc * 128:h * 256 + (dc + 1) * 128], es[kt].v, kt == 0, kt == 1)
                fw.tt(oT.c(2 * h + dc, t0, t1), bo.v, r.v, ALU.mult)
        for g in range(2):
            w = ld(self.wsrc(wo, g * 512, (g + 1) * 512))
            for mm in range(4):
                c_out = g * 4 + mm
                for tb in range(NTB):
                    t0, t1 = tb * TB, (tb + 1) * TB
                    bank = fw.bank()
                    for c in range(NCH):
                        fw.matmul(bank.v, w[:, c, mm * 128:(mm + 1) * 128], oT.c(c, t0, t1), c == 0, c == NCH - 1)
                    xs = self.xT.c(c_out, t0, t1)
                    fw.tt(xs, bank.v, xs, ALU.add)
        fw.release(m)


ALL_STAGES = ["mix0", "cross0", "mlp0", "mix1", "cross1", "mlp1", "final"]


def rope_consts():
    out = {}
    inv = (10000.0 ** (-np.arange(0, 64, 2, dtype=np.float32) / 64)).astype(np.float32)
    ang = (np.arange(S, dtype=np.float32)[None, :] * inv[:, None]).astype(np.float32)
    cos, sin = np.cos(ang).astype(np.float32), np.sin(ang).astype(np.float32)
    p = np.arange(128)
    out["ropeD_cos"] = np.ascontiguousarray(cos[p % 32])
    sgn = np.where((p % 64) < 32, -1.0, 1.0).astype(np.float32)[:, None]
    out["ropeD_sin"] = np.ascontiguousarray(sin[p % 32] * sgn)
    return out


_CONST_CACHE = {}


def hyena_consts():
    if "c" in _CONST_CACHE:
        return _CONST_CACHE["c"]
    out = {}
    L = S
    N = 2 * L
    t = np.linspace(0.0, 1.0, L, dtype=np.float32)[:, None]
    w = ((2.0 * math.pi / L) * np.arange(L, dtype=np.float32))[:, None].astype(np.float32)
    bands = np.linspace(1e-4, 15, 16, dtype=np.float32)[None, :]
    feats = np.concatenate([t, np.cos(bands * w), -np.sin(bands * w)], axis=-1).astype(np.float32)
    ft = np.zeros((128, L), np.float32)
    ft[:33] = feats.T
    out["featsT"] = ft
    deltas = np.linspace(math.log(1e-2) / 1.5, math.log(1e-2) / 0.3, 512, dtype=np.float32)
    window = np.exp(-t * np.abs(deltas)[None, :]).astype(np.float32)
    out["win_f"] = np.ascontiguousarray(window)
    wb = window.copy()
    wb[0] = 0.0
    out["win_b"] = wb
    idx = np.arange(L, dtype=np.float64)
    ang = 2.0 * np.pi * np.outer(idx, idx + 0.5) / N
    C = np.cos(ang)
    Sn = np.sin(ang)
    bf = ml_dtypes.bfloat16
    F_ = np.stack([C, Sn], 0).reshape(2, 16, 128, 16, 128)
    out["dftF"] = np.ascontiguousarray(F_.transpose(3, 2, 0, 1, 4)).astype(bf)
    I_ = np.stack([C, Sn], 0).reshape(2, 4, 512, 4, 4, 128)
    out["dftI"] = np.ascontiguousarray(I_.transpose(1, 3, 5, 4, 0, 2)).astype(bf)
    inv = (10000.0 ** (-np.arange(0, 32, 2, dtype=np.float32) / 32)).astype(np.float32)
    angr = (np.arange(S, dtype=np.float32)[None, :] * inv[:, None]).astype(np.float32)
    cr, sr = np.cos(angr).astype(np.float32), np.sin(angr).astype(np.float32)
    rc = np.zeros((128, S), np.float32)
    rs = np.zeros((128, S), np.float32)
    rc[64:80] = cr
    rc[80:96] = cr
    rs[64:80] = -sr
    rs[80:96] = sr
    out["ropeR_cos"] = rc
    out["ropeR_sin"] = rs
    _CONST_CACHE["c"] = out
    return out


def make_in_map(inp, x, mem):
    mp = {
        "x": np.ascontiguousarray(x, np.float32),
        "mem": np.ascontiguousarray(mem, np.float32),
        "vecs": pack_vecs(inp),
        "ident": np.eye(128, dtype=np.float32),
    }
    mp["od_w_qkv"] = np.ascontiguousarray(inp["od_w_qkv"][0], np.float32)
    mp["od_w_out"] = np.ascontiguousarray(inp["od_w_out"][0], np.float32)
    mp.update(rope_consts())
    mp.update(hyena_consts())
    mp["ev_w_in"] = np.ascontiguousarray(inp["ev_w_in"][0], np.float32)
    mp["ev_w_out"] = np.ascontiguousarray(inp["ev_w_out"][0], np.float32)
    mp["mla_w_uq"] = np.ascontiguousarray(inp["mla_w_uq"][0], np.float32)
    mp["mla_w_ukv"] = np.ascontiguousarray(inp["mla_w_ukv"][0], np.float32)
    w1p = np.zeros((128, 128), np.float32)
    w1p[:33, :64] = inp["hy_w1"][0]
    w2p = np.zeros((128, 128), np.float32)
    w2p[:64, :64] = inp["hy_w2"][0]
    w3p = np.zeros((128, 2048), np.float32)
    w3p[:64] = inp["hy_w3"][0]
    mp["hy_w1p"], mp["hy_w2p"], mp["hy_w3p"] = w1p, w2p, w3p
    for i in range(2):
        mp[f"xa_wq{i}"] = np.ascontiguousarray(inp["xa_wq"][i], np.float32)
        mp[f"xa_wkv{i}"] = np.ascontiguousarray(inp["xa_wkv"][i], np.float32)
        mp[f"xa_wo{i}"] = np.ascontiguousarray(inp["xa_wo"][i], np.float32)
        mp[f"mlp_up{i}"] = np.ascontiguousarray(inp["mlp_up"][i], np.float32)
        mp[f"mlp_down{i}"] = np.ascontiguousarray(inp["mlp_down"][i], np.float32)
    return mp


def run_stages(inp, x, stages, nseq, n_cores, trace=False):
    k = K(nseq, stages)
    in_maps = []
    for c in range(n_cores):
        mp = make_in_map(inp, x[c * nseq:(c + 1) * nseq], inp["mem"][c * nseq:(c + 1) * nseq])
        in_maps.append({n: mp[n] for n in k.in_names})
    res = run_bass_kernel_spmd(k.nc, in_maps, core_ids=list(range(n_cores)), trace=trace)
    out = np.concatenate([np.asarray(r["y"]) for r in res.results], axis=0)
    return out, res


def kernel(**inputs):
    inp = {k: np.asarray(v) for k, v in inputs.items()}
    out, _ = run_stages(inp, inp["x"], ALL_STAGES, SEQ_PER_CORE, N_CORES)
    return out.astype(np.float32)
```

```python
import math
import numpy as np
import ml_dtypes
import concourse.bass as bass
import concourse.mybir as mybir
from concourse.bass_utils import run_bass_kernel_spmd

DT = mybir.dt
F32, BF16, U8 = DT.float32, DT.bfloat16, DT.uint8
ALU = mybir.AluOpType
AF = mybir.ActivationFunctionType
AX = mybir.AxisListType
DSIZE = {F32: 4, BF16: 2, U8: 1}

D = 1024
S = 2048
NMEM = 256
NCH = 8
TB = 512
NTB = 4
EPS = 1e-6
N_CORES = 8
SEQ_PER_CORE = 2
SBUF_BYTES = 212736


class Sem:
    def __init__(self, nc, name):
        self.h = nc.alloc_semaphore(name=name)
        self.count = 0
        self.name = name


class Tile:
    def __init__(self, ap, name, space):
        self.ap = ap
        self.name = name
        self.space = space
        self.w = None
        self.r = {}

    def __getitem__(self, idx):
        return V([self], self.ap[idx])

    @property
    def v(self):
        return V([self], self.ap)


class V:
    def __init__(self, ts, ap):
        self.ts = ts
        self.ap = ap

    def __getitem__(self, idx):
        return V(self.ts, self.ap[idx])

    def re(self, pat, **kw):
        return V(self.ts, self.ap.rearrange(pat, **kw))


class Grid:
    def __init__(self, ap, name, nch, ntok, tbw):
        self.ap = ap
        self.tbw = tbw
        self.nch = nch
        self.cells = [[Tile(ap[:, c, tb * tbw:(tb + 1) * tbw], f"{name}_{c}_{tb}", "sbuf")
                       for tb in range(ntok // tbw)] for c in range(nch)]

    def all_tiles(self):
        return [t for row in self.cells for t in row]

    def sl(self, c0, c1, t0, t1):
        tiles = [self.cells[c][tb] for c in range(c0, c1)
                 for tb in range(t0 // self.tbw, (t1 - 1) // self.tbw + 1)]
        if c1 - c0 == 1:
            return V(tiles, self.ap[:, c0, t0:t1])
        return V(tiles, self.ap[:, c0:c1, t0:t1])

    def c(self, c, t0, t1):
        return self.sl(c, c + 1, t0, t1)


class Eng:
    def __init__(self, nc, name):
        self.name = name
        self.sem = Sem(nc, "s_" + name)
        self.known = {}
        self.ops = []


class FW:
    def __init__(self, nc, sbuf_bytes=SBUF_BYTES):
        self.nc = nc
        self.eng = {k: Eng(nc, k) for k in ("pe", "dve", "act", "pool", "sp")}
        self._cms = []
        cm = nc.sbuf_tensor("arena", [128, sbuf_bytes], U8)
        self.arena = cm.__enter__()
        self._cms.append(cm)
        cm = nc.psum_tensor("psum", [128, 8 * 512], F32)
        self.psum = cm.__enter__()
        self._cms.append(cm)
        self.sbuf_bytes = sbuf_bytes
        self.top = 0
        self.peak = 0
        self.freed = []
        self.live = []
        self.banks = [Tile(self.psum[:, i * 512:(i + 1) * 512], f"bank{i}", "psum") for i in range(8)]
        self.rot = 0
        self.sems = {}
        self.n_instr = 0
        self.self_sync = True

    def bank(self, lo=0, hi=8):
        n = hi - lo
        b = self.banks[lo + self.rot % n]
        self.rot += 1
        return b

    def sem(self, name):
        if name not in self.sems:
            self.sems[name] = Sem(self.nc, name)
        return self.sems[name]

    def _raw(self, name, free_shape, dtype, parts):
        n = int(np.prod(free_shape)) * DSIZE[dtype]
        n_al = (n + 63) // 64 * 64
        lo = self.top
        hi = lo + n_al
        assert hi <= self.sbuf_bytes, f"SBUF overflow allocating {name}: {hi} > {self.sbuf_bytes}"
        self.top = hi
        self.peak = max(self.peak, hi)
        ap = self.arena[0:parts, lo:lo + n].bitcast(dtype)
        if len(free_shape) == 2:
            ap = ap.rearrange("p (a b) -> p a b", a=free_shape[0])
        elif len(free_shape) == 3:
            ap = ap.rearrange("p (a b c) -> p a b c", a=free_shape[0], b=free_shape[1])
        return ap, lo, hi

    def _inherit(self, t, lo, hi):
        for (flo, fhi, deps) in self.freed:
            if flo < hi and fhi > lo:
                for s, v in deps.items():
                    self._merge(t.r, s, v)
        self.live.append((lo, hi, t))

    def alloc(self, name, free_shape, dtype, parts=128):
        ap, lo, hi = self._raw(name, free_shape, dtype, parts)
        t = Tile(ap, name, "sbuf")
        self._inherit(t, lo, hi)
        return t

    def alloc_grid(self, name, nch, ntok, dtype, tbw=TB):
        ap, lo, hi = self._raw(name, [nch, ntok], dtype, 128)
        g = Grid(ap, name, nch, ntok, tbw)
        for t in g.all_tiles():
            self._inherit(t, lo, hi)
        return g

    def mark(self):
        return (self.top, len(self.live))

    def release(self, mark):
        top, nlive = mark
        recs = {}
        while len(self.live) > nlive:
            lo, hi, t = self.live.pop()
            deps = recs.setdefault((lo, hi), {})
            if t.w is not None:
                self._merge(deps, t.w[0], self._resolve(*t.w))
            for s, v in t.r.items():
                self._merge(deps, s, self._resolve(s, v))
        for (lo, hi), deps in recs.items():
            keep = []
            for (a, b, d) in self.freed:
                if a >= lo and b <= hi:
                    for s, v in d.items():
                        self._merge(deps, s, v)
                else:
                    keep.append((a, b, d))
            self.freed = keep
            self.freed.append((lo, hi, deps))
        self.top = top

    @staticmethod
    def _resolve(s, v):
        return s.count if v is None else v

    @staticmethod
    def _merge(d, s, v):
        if s in d:
            if d[s] is None or v is None:
                d[s] = None
            else:
                d[s] = max(d[s], v)
        else:
            d[s] = v

    def dram_tile(self, ap, name):
        return Tile(ap, name, "dram")

    def _waits(self, E, reads, writes):
        deps = {}
        for t in reads:
            if t.w is not None:
                self._merge(deps, t.w[0], self._resolve(*t.w))
        for t in writes:
            if t.w is not None:
                self._merge(deps, t.w[0], self._resolve(*t.w))
            for s, v in t.r.items():
                self._merge(deps, s, self._resolve(s, v))
        for s, v in deps.items():
            if v <= 0:
                continue
            if s is E.sem and not self.self_sync:
                continue
            if E.known.get(s, 0) >= v:
                continue
            E.known[s] = v
            E.ops.append(lambda e, h=s.h, vv=v: e.wait_ge(h, vv))

    def emit(self, eng, fn, reads, writes, pe_acc=False):
        E = self.eng[eng]
        reads = list(dict.fromkeys(reads))
        writes = list(dict.fromkeys(writes))
        if pe_acc:
            self._waits(E, reads, [])
        else:
            self._waits(E, reads, writes)
        E.sem.count += 1
        n = E.sem.count
        sh = E.sem.h
        E.ops.append(lambda e: fn(e).then_inc(sh, 1))
        self.n_instr += 1
        for t in reads:
            self._merge(t.r, E.sem, n)
        for t in writes:
            t.w = (E.sem, n)
            t.r = {}

    def dma(self, queue, out, in_, sem, **kw):
        Q = self.eng[queue]
        self._waits(Q, in_.ts, out.ts)
        sem.count += 16
        sh = sem.h
        oa, ia = out.ap, in_.ap
        Q.ops.append(lambda e: e.dma_start(out=oa, in_=ia, **kw).then_inc(sh, 16))
        self.n_instr += 1
        for t in in_.ts:
            self._merge(t.r, sem, None)
        for t in out.ts:
            t.w = (sem, None)
            t.r = {}
            if t.space == "dram":
                if not hasattr(t, "wsems"):
                    t.wsems = set()
                t.wsems.add(sem)

    def matmul(self, out, lhsT, rhs, start, stop):
        o, l, r = out.ap, lhsT.ap, rhs.ap
        self.emit("pe", lambda e: e.matmul(o, l, r, start=start, stop=stop),
                  lhsT.ts + rhs.ts, out.ts, pe_acc=not start)

    def transpose(self, out, in_, ident):
        o, i, d = out.ap, in_.ap, ident.ap
        self.emit("pe", lambda e: e.transpose(o, i, d), in_.ts + ident.ts, out.ts)

    def act(self, out, in_, func, bias=None, scale=1.0, accum_out=None):
        o, i = out.ap, in_.ap
        reads = list(in_.ts)
        writes = list(out.ts)
        kw = {}
        if bias is not None:
            if isinstance(bias, V):
                reads += bias.ts
                kw["bias"] = bias.ap
            else:
                kw["bias"] = bias
        if isinstance(scale, V):
            reads += scale.ts
            kw["scale"] = scale.ap
        else:
            kw["scale"] = scale
        if accum_out is not None:
            writes += accum_out.ts
            kw["accum_out"] = accum_out.ap
        self.emit("act", lambda e: e.activation(o, i, func, **kw), reads, writes)

    def tt(self, out, in0, in1, op, eng="dve"):
        o, a, b = out.ap, in0.ap, in1.ap
        self.emit(eng, lambda e: e.tensor_tensor(o, a, b, op), in0.ts + in1.ts, out.ts)

    def ts(self, out, in0, s1, s2=None, op0=ALU.mult, op1=None, eng="dve", accum_out=None):
        o, a = out.ap, in0.ap
        reads = list(in0.ts)
        writes = list(out.ts)
        if isinstance(s1, V):
            reads += s1.ts
            s1 = s1.ap
        if isinstance(s2, V):
            reads += s2.ts
            s2 = s2.ap
        kw = {}
        if op1 is not None:
            kw["op1"] = op1
        if accum_out is not None:
            writes += accum_out.ts
            kw["accum_out"] = accum_out.ap
        self.emit(eng, lambda e: e.tensor_scalar(o, a, s1, s2, op0, **kw), reads, writes)

    def stt(self, out, in0, scalar, in1, op0, op1):
        o, a, b = out.ap, in0.ap, in1.ap
        reads = in0.ts + in1.ts
        if isinstance(scalar, V):
            reads = reads + scalar.ts
            scalar = scalar.ap
        self.emit("dve", lambda e: e.scalar_tensor_tensor(o, a, scalar, b, op0, op1), reads, out.ts)

    def copy(self, out, in_, eng="dve"):
        o, i = out.ap, in_.ap
        if eng == "act":
            self.emit(eng, lambda e: e.copy(o, i), in_.ts, out.ts)
        else:
            self.emit(eng, lambda e: e.tensor_copy(o, i), in_.ts, out.ts)

    def memset(self, out, val, eng="dve"):
        o = out.ap
        self.emit(eng, lambda e: e.memset(o, val), [], out.ts)

    def reduce(self, out, in_, op, axis=AX.X):
        o, i = out.ap, in_.ap
        self.emit("dve", lambda e: e.tensor_reduce(o, i, axis, op), in_.ts, out.ts)

    def recip(self, out, in_):
        o, i = out.ap, in_.ap
        self.emit("dve", lambda e: e.reciprocal(o, i), in_.ts, out.ts)

    def finish(self, final_tiles):
        SP = self.eng["sp"]
        self._waits(SP, [], final_tiles)
        for t in final_tiles:
            for sm in getattr(t, "wsems", ()):
                if SP.known.get(sm, 0) < sm.count:
                    SP.known[sm] = sm.count
                    SP.ops.append(lambda e, h=sm.h, v=sm.count: e.wait_ge(h, v))
        for k, E in self.eng.items():
            if k == "sp" or E.sem.count == 0:
                continue
            if SP.known.get(E.sem, 0) < E.sem.count:
                SP.ops.append(lambda e, h=E.sem.h, v=E.sem.count: e.wait_ge(h, v))
        nc = self.nc
        engs = self.eng
        with nc.Block() as block:
            @block.tensor
            def _(e):
                for f in engs["pe"].ops:
                    f(e)

            @block.vector
            def _(e):
                for f in engs["dve"].ops:
                    f(e)

            @block.scalar
            def _(e):
                for f in engs["act"].ops:
                    f(e)

            @block.gpsimd
            def _(e):
                for f in engs["pool"].ops:
                    f(e)

            @block.sync
            def _(e):
                for f in engs["sp"].ops:
                    f(e)
        for cm in reversed(self._cms):
            cm.__exit__(None, None, None)


def fm(vec, nch):
    return np.ascontiguousarray(np.asarray(vec, np.float32).reshape(nch, 128).T)


def pad128(vec):
    v = np.zeros((128, 1), np.float32)
    v[:len(vec), 0] = vec
    return v


VEC_SPEC = []


def _vs(name, n):
    VEC_SPEC.append((name, n))


for _i in range(2):
    _vs(f"norm_mix{_i}", 8)
    _vs(f"norm_cross{_i}", 8)
    _vs(f"norm_mlp{_i}", 8)
_vs("final_norm", 8)
_vs("mem_norm", 8)
for _k in range(3):
    _vs(f"conv_w{_k}", 12)
_vs("conv_b", 12)
_vs("hy_skip0", 4)
_vs("hy_skip1", 4)
_vs("q_norm", 2)
_vs("kv_norm", 1)
_vs("hy_b1", 1)
_vs("hy_b2", 1)
_vs("hy_freq", 1)
_vs("subln", 128)
_vs("lqk", 256)
VEC_OFF = {}
_o = 0
for _n, _c in VEC_SPEC:
    VEC_OFF[_n] = (_o, _c)
    _o += _c
NV = _o


def pack_vecs(inp):
    cols = {}
    for i in range(2):
        cols[f"norm_mix{i}"] = fm(inp["norm_mix"][i], 8)
        cols[f"norm_cross{i}"] = fm(inp["norm_cross"][i], 8)
        cols[f"norm_mlp{i}"] = fm(inp["norm_mlp"][i], 8)
    cols["final_norm"] = fm(inp["final_norm"], 8)
    cols["mem_norm"] = fm(inp["mem_norm"], 8)
    for k in range(3):
        cols[f"conv_w{k}"] = fm(inp["ev_conv_w"][0, k], 12)
    cols["conv_b"] = fm(inp["ev_conv_b"][0], 12)
    cols["hy_skip0"] = fm(inp["hy_skip"][0, 0], 4)
    cols["hy_skip1"] = fm(inp["hy_skip"][0, 1], 4)
    cols["q_norm"] = fm(inp["mla_q_norm"][0], 2)
    cols["kv_norm"] = fm(inp["mla_kv_norm"][0], 1)
    cols["hy_b1"] = pad128(inp["hy_b1"][0])
    cols["hy_b2"] = pad128(inp["hy_b2"][0])
    cols["hy_freq"] = pad128(inp["hy_freq"][0])
    cols["subln"] = np.broadcast_to(np.asarray(inp["dif_subln"][0], np.float32)[None, :], (128, 128))
    lqk = np.concatenate([inp["dif_lq1"][0], inp["dif_lk1"][0], inp["dif_lq2"][0], inp["dif_lk2"][0]]).astype(np.float32)
    cols["lqk"] = np.broadcast_to(lqk[None, :], (128, 256))
    out = np.concatenate([cols[n] for n, _ in VEC_SPEC], axis=1).astype(np.float32)
    assert out.shape == (128, NV)
    return np.ascontiguousarray(out)


def fw_mark_of(grid, fw):
    cell = grid.cells[0][0]
    for i, (lo, hi, t) in enumerate(fw.live):
        if t is cell:
            return (lo, i)
    raise KeyError


class K:
    def __init__(self, nseq, stages, dbg=None):
        self.nseq = nseq
        self.stages = stages
        nc = bass.Bass("TRN2", target_bir_lowering=False)
        self.nc = nc
        fw = FW(nc)
        self.fw = fw
        self.d = {}
        self.in_names = []

        def din(name, shape, dt=F32):
            ap = nc.dram_tensor(name, list(shape), dt, kind="ExternalInput").ap()
            self.d[name] = fw.dram_tile(ap, name)
            self.in_names.append(name)

        din("x", [nseq, S, D])
        din("mem", [nseq, NMEM, D])
        din("vecs", [128, NV])
        din("ident", [128, 128])
        for i in range(2):
            din(f"xa_wq{i}", [D, D])
            din(f"xa_wkv{i}", [D, 2 * D])
            din(f"xa_wo{i}", [D, D])
            din(f"mlp_up{i}", [D, 4 * D])
            din(f"mlp_down{i}", [4 * D, D])
        din("ev_w_in", [D, 1952])
        din("ev_w_out", [D, D])
        din("mla_w_uq", [256, 768])
        din("mla_w_ukv", [128, 1024])
        din("hy_w1p", [128, 128])
        din("hy_w2p", [128, 128])
        din("hy_w3p", [128, 2048])
        din("featsT", [128, S])
        din("win_f", [S, 512])
        din("win_b", [S, 512])
        din("dftF", [16, 128, 2, 16, 128], BF16)
        din("dftI", [4, 4, 128, 4, 2, 512], BF16)
        din("ropeR_cos", [128, S])
        din("ropeR_sin", [128, S])
        ks = nc.dram_tensor("kspec", [2, 16, 128, 2, 512], BF16, kind="Internal").ap()
        self.kspec = [[fw.dram_tile(ks[o, fc], f"kspec{o}_{fc}") for fc in range(16)] for o in range(2)]
        din("od_w_qkv", [D, 3 * D])
        din("od_w_out", [D, D])
        din("ropeD_cos", [128, S])
        din("ropeD_sin", [128, S])
        yap = nc.dram_tensor("y", [nseq, S, D], F32, kind="ExternalOutput").ap()
        self.y = fw.dram_tile(yap, "y")

        self.setup_consts()
        self._precast_done = False
        if "mix0" in stages:
            self.filter_phase()
        if not self._precast_done:
            self.precast()
        self.setup()
        for s in range(nseq):
            self.cur_seq = s
            self.load_seq(s)
            for st in stages:
                if st.startswith("mlp"):
                    self.mlp(int(st[3:]))
                elif st.startswith("cross"):
                    self.cross(int(st[5:]))
                elif st == "mix1":
                    self.mix1()
                elif st == "mix0":
                    self.mix0()
                elif st == "final":
                    pass
                else:
                    raise ValueError(st)
            self.store_seq(s, final=("final" in stages))
        fw.finish([self.y])

    def vec(self, name, c0=0, c1=None):
        off, n = VEC_OFF[name]
        if c1 is None:
            c1 = n
        return self.vecs[:, off + c0:off + c1]

    def setup(self):
        fw = self.fw
        self.xT = fw.alloc_grid("xT", NCH, S, F32)

    def precast(self):
        fw = self.fw
        nc = self.nc
        self._precast_done = True
        need = []
        for st in self.stages:
            if st.startswith("mlp"):
                i = st[3:]
                need += [f"mlp_up{i}", f"mlp_down{i}"]
        sem = fw.sem("precast")
        for name in need:
            src = self.d[name]
            shp = list(src.ap.shape)
            ap = nc.dram_tensor(name + "_bf", shp, BF16, kind="Internal").ap()
            dst = fw.dram_tile(ap, name + "_bf")
            rows = shp[0]
            step = max(128, rows // 2)
            for r0 in range(0, rows, step):
                fw.dma("pool", dst[r0:r0 + step, :], src[r0:r0 + step, :], sem, max_dma_last_dim=8192)
            if name == "ev_w_in":
                self.d["ev_w_in_bf"] = dst
            else:
                self.d[name] = dst

    def setup_consts(self):
        fw = self.fw
        self.vecs = fw.alloc("vecs", [NV], F32)
        self.ident = fw.alloc("ident", [128], F32)
        self.ones_bf = fw.alloc("ones_bf", [128], BF16)
        s_c = fw.sem("const")
        fw.dma("sp", self.vecs.v, self.d["vecs"].v, s_c)
        fw.dma("sp", self.ident.v, self.d["ident"].v, s_c)
        fw.memset(self.ones_bf.v, 1.0, eng="dve")
        self.ident_bf = fw.alloc("ident_bf", [128], BF16)
        fw.copy(self.ident_bf.v, self.ident.v, eng="dve")
        self.maskA = fw.alloc("maskA", [128], BF16)
        self.maskB = fw.alloc("maskB", [128], BF16)
        self.mcol = fw.alloc("mcol", [2], F32)
        fw.memset(self.maskA.v, 0.0, eng="dve")
        fw.memset(self.maskB.v, 0.0, eng="dve")
        fw.memset(self.mcol.v, 0.0, eng="dve")
        fw.memset(self.maskA[0:64, :], 1.0, eng="dve")
        fw.memset(self.maskB[64:128, :], 1.0, eng="dve")
        fw.memset(self.mcol[0:64, 0:1], 1.0, eng="dve")
        fw.memset(self.mcol[64:128, 1:2], 1.0, eng="dve")
        self._eps = {}
        for ev in (EPS, 1e-5):
            t = fw.alloc(f"eps{len(self._eps)}", [1], F32)
            fw.memset(t.v, float(ev), eng="dve")
            self._eps[ev] = t

    def rmsnorm_T(self, src_fn, gain, dst_fn, ntok, nch=NCH, eps=EPS, blk=TB):
        fw = self.fw
        m = fw.mark()
        dtot = nch * 128
        sq = [fw.alloc(f"nsq{i}", [nch, blk], BF16) for i in range(2)]
        rs = [fw.alloc(f"nrs{i}", [blk], F32) for i in range(2)]
        for bi, t0 in enumerate(range(0, ntok, blk)):
            w = min(blk, ntok - t0)
            q = sq[bi % 2]
            r = rs[bi % 2]
            for c in range(nch):
                xs = src_fn(c, t0, t0 + w)
                fw.act(q[:, c, 0:w], xs, AF.Square)
            bank = fw.bank()
            for c in range(nch):
                fw.matmul(bank[:, 0:w], self.ones_bf.v, q[:, c, 0:w], c == 0, c == nch - 1)
            fw.act(r[:, 0:w], bank[:, 0:w], AF.Ln, bias=self.eps_col(eps), scale=1.0 / dtot)
            fw.act(r[:, 0:w], r[:, 0:w], AF.Exp, scale=-0.5)
            for c in range(nch):
                fw.stt(dst_fn(c, t0, t0 + w), src_fn(c, t0, t0 + w), gain[:, c:c + 1], r[:, 0:w], ALU.mult, ALU.mult)
        fw.release(m)

    def eps_col(self, eps):
        return self._eps[eps].v

    def load_seq(self, s):
        fw = self.fw
        m = fw.mark()
        stg = [fw.alloc(f"xstg{i}", [D], F32) for i in range(2)]
        sems = [fw.sem(f"xstg{i}") for i in range(2)]
        xd = self.d["x"]
        for tt in range(S // 128):
            st = stg[tt % 2]
            fw.dma("sp", st.v, xd[s, tt * 128:(tt + 1) * 128, :], sems[tt % 2])
            for half in range(2):
                bank = fw.bank()
                for q in range(4):
                    c = half * 4 + q
                    fw.transpose(bank[:, q * 128:(q + 1) * 128], st[:, c * 128:(c + 1) * 128], self.ident.v)
                dst = self.xT.sl(half * 4, half * 4 + 4, tt * 128, (tt + 1) * 128)
                fw.copy(dst, bank.v.re("p (a b) -> p a b", a=4), eng="act" if half else "dve")
        fw.release(m)

    def load_mem(self, s):
        fw = self.fw
        self.memnT = fw.alloc("memnT", [NCH, NMEM], BF16)
        m = fw.mark()
        stg = [fw.alloc(f"mstg{i}", [D], F32) for i in range(2)]
        sems = [fw.sem(f"mstg{i}") for i in range(2)]
        memT = fw.alloc("memT", [NCH, NMEM], F32)
        md = self.d["mem"]
        for mt in range(NMEM // 128):
            st = stg[mt % 2]
            fw.dma("sp", st.v, md[s, mt * 128:(mt + 1) * 128, :], sems[mt % 2])
            for half in range(2):
                bank = fw.bank()
                for q in range(4):
                    c = half * 4 + q
                    fw.transpose(bank[:, q * 128:(q + 1) * 128], st[:, c * 128:(c + 1) * 128], self.ident.v)
                fw.copy(memT[:, half * 4:half * 4 + 4, mt * 128:(mt + 1) * 128],
                        bank.v.re("p (a b) -> p a b", a=4), eng="act" if half else "dve")
        self.rmsnorm_T(lambda c, a, b: memT[:, c, a:b], self.vec("mem_norm"),
                       lambda c, a, b: self.memnT[:, c, a:b], NMEM, blk=NMEM)
        fw.release(m)

    def store_seq(self, s, final):
        fw = self.fw
        m = fw.mark()
        if final:
            src = fw.alloc_grid("xfin", NCH, S, F32)
            self.rmsnorm_T(lambda c, a, b: self.xT.c(c, a, b), self.vec("final_norm"),
                           lambda c, a, b: src.c(c, a, b), S)
        else:
            src = self.xT
        stg = [fw.alloc(f"ostg{i}", [D], F32) for i in range(2)]
        sems = [fw.sem(f"ostg{i}") for i in range(2)]
        for tt in range(S // 128):
            st = stg[tt % 2]
            for half in range(2):
                bank = fw.bank()
                for q in range(4):
                    c = half * 4 + q
                    fw.transpose(bank[:, q * 128:(q + 1) * 128], src.c(c, tt * 128, (tt + 1) * 128), self.ident.v)
                fw.copy(st[:, half * 512:(half + 1) * 512], bank.v, eng="act" if half else "dve")
            fw.dma("act", self.y[s, tt * 128:(tt + 1) * 128, :], st.v, sems[tt % 2])
        fw.release(m)

    def wstream(self, name, free_shape, nslots=2):
        fw = self.fw
        slots = [fw.alloc(f"{name}{i}", free_shape, BF16) for i in range(nslots)]
        sems = [fw.sem(f"{name}{i}") for i in range(nslots)]
        state = {"i": 0}

        def load(src_v, dst_idx=None):
            i = state["i"] % nslots
            state["i"] += 1
            dst = slots[i].v if dst_idx is None else slots[i][dst_idx]
            q = "sp" if src_v.ap.dtype == BF16 else "pool"
            fw.dma(q, dst, src_v, sems[i])
            slots[i]._sem = sems[i]
            return slots[i]
        return load

    @staticmethod
    def wsrc(dt, c0, c1):
        return V([dt], dt.ap.rearrange("(c p) n -> p c n", p=128)[:, :, c0:c1])

    def mlp(self, i):
        fw = self.fw
        m = fw.mark()
        hT = fw.alloc_grid("hT", NCH, S, BF16)
        wup = self.d[f"mlp_up{i}"]
        wdn = self.d[f"mlp_down{i}"]
        ld_up = self.wstream("wup", [NCH, 512])
        ld_dn = self.wstream("wdn", [4, D])
        hid = fw.alloc("hid", [32, TB], BF16)
        rl = [fw.alloc(f"rl{k}", [TB], F32) for k in range(3)]
        pre_up = ld_up(self.wsrc(wup, 0, 512))
        self.rmsnorm_T(lambda c, a, b: self.xT.c(c, a, b), self.vec(f"norm_mlp{i}"),
                       lambda c, a, b: hT.c(c, a, b), S)
        for tb in range(NTB):
            t0, t1 = tb * TB, (tb + 1) * TB
            nxt = pre_up if tb == 0 else ld_up(self.wsrc(wup, 0, 512))
            for g in range(8):
                cur = nxt
                if g + 1 < 8:
                    nxt = ld_up(self.wsrc(wup, (g + 1) * 512, (g + 2) * 512))
                else:
                    nxt_dn = ld_dn(V([wdn], wdn.ap.rearrange("(j p) n -> p j n", p=128)[:, 0:4, :]))
                for jj in range(4):
                    j = g * 4 + jj
                    bank = fw.bank()
                    for c in range(NCH):
                        fw.matmul(bank.v, cur[:, c, jj * 128:(jj + 1) * 128], hT.c(c, t0, t1), c == 0, c == NCH - 1)
                    r = rl[j % 3]
                    fw.act(r.v, bank.v, AF.Relu)
                    fw.tt(hid[:, j, :], r.v, r.v, ALU.mult)
            nxt = nxt_dn
            for g in range(8):
                cur = nxt
                if g + 1 < 8:
                    nxt = ld_dn(V([wdn], wdn.ap.rearrange("(j p) n -> p j n", p=128)[:, (g + 1) * 4:(g + 2) * 4, :]))
                for jj in range(4):
                    j = g * 4 + jj
                    for c in range(NCH):
                        fw.matmul(fw.banks[c].v, cur[:, jj, c * 128:(c + 1) * 128], hid[:, j, :], j == 0, j == 31)
            for c in range(NCH):
                xs = self.xT.c(c, t0, t1)
                fw.tt(xs, fw.banks[c].v, xs, ALU.add)
        fw.release(m)


    def proj_residual(self, wd, src, kch):
        fw = self.fw
        m = fw.mark()
        ld = self.wstream("wpo", [kch, 512])
        for g in range(2):
            w = ld(self.wsrc(wd, g * 512, (g + 1) * 512))
            for mm in range(4):
                c_out = g * 4 + mm
                for tb in range(NTB):
                    t0, t1 = tb * TB, (tb + 1) * TB
                    bank = fw.bank()
                    for c in range(kch):
                        fw.matmul(bank.v, w[:, c, mm * 128:(mm + 1) * 128], src.c(c, t0, t1), c == 0, c == kch - 1)
                    xs = self.xT.c(c_out, t0, t1)
                    fw.tt(xs, bank.v, xs, ALU.add)
        fw.release(m)

    def tab_stream(self, name, dcos, dsin, nslots=2):
        fw = self.fw
        slots = [fw.alloc(f"{name}{i}", [2, TB], F32) for i in range(nslots)]
        sems = [fw.sem(f"{name}{i}") for i in range(nslots)]
        st = {"i": 0, "pend": None}

        def issue(tb):
            i = st["i"] % nslots
            st["i"] += 1
            fw.dma("sp", slots[i][:, 0, :], dcos[:, tb * TB:(tb + 1) * TB], sems[i])
            fw.dma("sp", slots[i][:, 1, :], dsin[:, tb * TB:(tb + 1) * TB], sems[i])
            return (tb, slots[i])

        def get(tb):
            if st["pend"] is None:
                st["pend"] = issue(tb)
            ptb, slot = st["pend"]
            assert ptb == tb
            st["pend"] = issue((tb + 1) % NTB)
            return slot[:, 0, :], slot[:, 1, :]
        return get

    def rope_T(self, dst, ps, cosv, sinv, tmp, u, groups):
        fw = self.fw
        fw.tt(tmp, ps, cosv, ALU.mult)
        for (lo, plo, n) in groups:
            fw.tt(u[lo:lo + n], ps[plo:plo + n], sinv[lo:lo + n], ALU.mult)
        fw.tt(dst, tmp, u, ALU.add, eng="pool")

    def transpose_tm_to_fm(self, src_fn, dst, nch):
        fw = self.fw
        k = 0
        for tt in range(S // 128):
            for c0 in range(0, nch, 4):
                bank = fw.bank()
                bb = V(bank.v.ts, bank.ap.bitcast(BF16))
                for q in range(4):
                    fw.transpose(bb[:, q * 128:(q + 1) * 128], src_fn(tt, c0 + q), self.ident_bf.v)
                fw.copy(dst.sl(c0, c0 + 4, tt * 128, (tt + 1) * 128),
                        bb[:, 0:512].re("p (a b) -> p a b", a=4), eng="act" if k % 2 else "dve")
                k += 1

    def attn_core(self, qr, Ks, negbs, vtm, dv, scale, finalize, LA=2, mid_hook=None, bg=None, bg_every=4,
                  sbanks=(0, 1, 2), one_bank_acc=False):
        fw = self.fw
        nm = len(Ks)
        ee = self._ee
        vts = vtm.ts if isinstance(vtm, V) else vtm.v.ts
        steps = [(qb, jm, kt) for qb in range(NTB) for jm in range(nm) for kt in range(S // 128)]
        n = len(steps)
        Et = {}

        def acc_of(qb, jm):
            if one_bank_acc:
                b0 = fw.banks[4 + (self._acc_par + qb) % 2]
                w = dv + 1
                return [b0[:, k * w:(k + 1) * w] for k in range(4)]
            if nm == 1:
                bi = 4 + 2 * ((self._acc_par + qb) % 2)
            else:
                bi = 4 + 2 * jm
            b0, b1 = fw.banks[bi], fw.banks[bi + 1]
            return [b0[:, 0:dv + 1], b0[:, 256:256 + dv + 1], b1[:, 0:dv + 1], b1[:, 256:256 + dv + 1]]

        def qk(i):
            qb, jm, kt = steps[i]
            sb = fw.banks[sbanks[self._sbk % len(sbanks)]]
            self._sbk += 1
            fw.matmul(sb.v, Ks[jm][:, kt * 128:(kt + 1) * 128], qr[:, qb * TB:(qb + 1) * TB], True, True)
            e = ee[self._eit % len(ee)]
            self._eit += 1
            fw.act(e.v, sb.v, AF.Exp, bias=negbs[jm], scale=scale)
            Et[i] = e

        def pv(i):
            qb, jm, kt = steps[i]
            e = Et.pop(i)
            acc = acc_of(qb, jm)
            for qt in range(4):
                o, l, r = acc[qt].ap, e[:, qt * 128:(qt + 1) * 128].ap, vtm[:, kt, :].ap
                st = (kt == 0 and (qt == 0 if one_bank_acc else qt % 2 == 0))
                fw.emit("pe", lambda en, o=o, l=l, r=r, st=st, sp=(kt == 15): en.matmul(
                    o, l, r, start=st, stop=sp, skip_group_check=True),
                    e.v.ts + vts, acc[qt].ts, pe_acc=not st)

        deferred = []
        DEFER = 8
        for i in range(min(LA, n)):
            qk(i)
        for i in range(n):
            if i + LA < n:
                qk(i + LA)
            pv(i)
            while deferred and deferred[0][0] <= i:
                deferred.pop(0)[1]()
            qb, jm, kt = steps[i]
            if bg is not None and i % bg_every == bg_every - 1:
                next(bg, None)
            if jm == nm - 1 and kt == S // 128 - 1:
                cont = finalize(qb, [acc_of(qb, jj) for jj in range(nm)])
                if cont is not None:
                    deferred.append((i + DEFER, cont))
                if qb == 1 and mid_hook is not None:
                    mid_hook()
        while deferred:
            deferred.pop(0)[1]()
        if bg is not None:
            for _ in bg:
                pass
        if nm == 1:
            self._acc_par += NTB

    def mix1(self):
        fw = self.fw
        m = fw.mark()
        lam_init = 0.8 - 0.6 * math.exp(-0.3 * 1)
        scale = 64 ** -0.5
        hT = fw.alloc_grid("hT", NCH, S, BF16)
        self.rmsnorm_T(lambda c, a, b: self.xT.c(c, a, b), self.vec("norm_mix1"),
                       lambda c, a, b: hT.c(c, a, b), S)
        ao = fw.alloc("ao", [S // 128, D], BF16)
        m2 = fw.mark()
        tabD = self.tab_stream("tabD", self.d["ropeD_cos"], self.d["ropeD_sin"])
        lq = self.vec("lqk")
        sm = fw.alloc("lam_sm", [8], F32)
        prod = fw.alloc("lam_prod", [128], F32)
        fw.tt(prod[:, 0:64], lq[:, 0:64], lq[:, 64:128], ALU.mult)
        fw.tt(prod[:, 64:128], lq[:, 128:192], lq[:, 192:256], ALU.mult)
        fw.reduce(sm[:, 0:1], prod[:, 0:64], ALU.add)
        fw.reduce(sm[:, 1:2], prod[:, 64:128], ALU.add)
        fw.act(sm[:, 2:4], sm[:, 0:2], AF.Exp)
        fw.tt(sm[:, 4:5], sm[:, 3:4], sm[:, 2:3], ALU.subtract)
        fw.ts(sm[:, 5:6], sm[:, 4:5], -lam_init, None, op0=ALU.add)
        neglam = sm[:, 5:6]
        sub = fw.alloc("subln_s", [128], F32)
        fw.ts(sub.v, self.vec("subln"), 1.0 - lam_init, None, op0=ALU.mult)

        wqkv = self.d["od_w_qkv"]
        ldw = self.wstream("wqkv", [NCH, 384])
        src3 = wqkv.ap.rearrange("(c p) n -> p c n", p=128)
        kr = fw.alloc("kr", [S], BF16)
        bufs = []
        for bi in range(2):
            B = dict(qr=fw.alloc(f"qr{bi}", [S], BF16), kA=fw.alloc(f"kA{bi}", [S], BF16),
                     kB=fw.alloc(f"kB{bi}", [S], BF16), vtm=fw.alloc(f"vtm{bi}", [S // 128, 129], BF16),
                     negb=fw.alloc(f"negb{bi}", [2], F32), mx=fw.alloc(f"mx{bi}", [16], F32))
            fw.memset(B["vtm"][:, :, 128:129], 1.0, eng="dve")
            bufs.append(B)
        rtmp = [fw.alloc("rtmp0", [TB], F32)] * 2
        ru = [fw.alloc("ru0", [TB], F32)] * 2
        self._ee = [fw.alloc(f"ee{k}", [TB], BF16) for k in range(3)]
        self._eit = 0
        self._sbk = 0
        self._acc_par = 0
        fin = fw.alloc("fin", [4, 8], F32)
        ot = [fw.alloc("ot0", [128], F32)] * 2
        oo = [fw.alloc(f"oo{k}", [128], F32) for k in range(4)]
        junk32 = fw.alloc("junk32", [128], F32)
        groups = [(0, 32, 32), (32, 0, 32), (64, 96, 32), (96, 64, 32)]
        rkc = [0]

        PBs = [fw.banks[2], fw.banks[3]]
        pbc = [0]

        def nextpb():
            pbc[0] += 1
            return PBs[pbc[0] % len(PBs)]

        def proj_rope_unit(slot, which, dst, tb):
            t0, t1 = tb * TB, (tb + 1) * TB
            PB = nextpb()
            for c in range(NCH):
                fw.matmul(PB.v, slot[:, c, which * 128:(which + 1) * 128], hT.c(c, t0, t1), c == 0, c == NCH - 1)
            cv, sv = tabD(tb)
            self.rope_T(dst[:, t0:t1], PB.v, cv, sv,
                        rtmp[rkc[0] % 2].v, ru[rkc[0] % 2].v, groups)
            rkc[0] += 1

        def bounds_unit(B, wi, hj, tb):
            mx = B["mx"]
            mk = (self.maskA, self.maskB)[hj]
            PB = nextpb()
            fw.matmul(PB.v, mk.v, kr[:, tb * TB:(tb + 1) * TB], True, True)
            fw.reduce(mx[:, tb + 8 * hj:tb + 8 * hj + 1], PB.v, ALU.max)
            if tb == NTB - 1:
                fw.reduce(mx[:, 4 + 8 * hj + wi:5 + 8 * hj + wi], mx[:, 8 * hj:8 * hj + 4], ALU.max)

        def prologue_gen(hp, B):
            slot = ldw(V([wqkv], src3[:, :, hp * 128:(hp + 1) * 128]), (slice(None), slice(None), slice(0, 128)))
            self._dma_same_slot(slot, (slice(None), slice(None), slice(128, 256)),
                                V([wqkv], src3[:, :, D + hp * 128:D + (hp + 1) * 128]))
            self._dma_same_slot(slot, (slice(None), slice(None), slice(256, 384)),
                                V([wqkv], src3[:, :, 2 * D + hp * 128:2 * D + (hp + 1) * 128]))
            yield
            for tb in range(NTB):
                proj_rope_unit(slot, 0, B["qr"], tb)
                yield
                yield
            for tb in range(NTB):
                proj_rope_unit(slot, 1, kr, tb)
                yield
                yield
            fw.act(B["kA"].v, kr.v, AF.Copy, scale=self.mcol[:, 0:1])
            fw.act(B["kB"].v, kr.v, AF.Copy, scale=self.mcol[:, 1:2])
            vtm = B["vtm"]
            for t4 in range(4):
                PB = nextpb()
                for q in range(4):
                    tt = t4 * 4 + q
                    for c in range(NCH):
                        fw.matmul(PB[:, q * 128:(q + 1) * 128], hT.c(c, tt * 128, (tt + 1) * 128),
                                  slot[:, c, 256:384], c == 0, c == NCH - 1)
                fw.copy(vtm[:, t4 * 4:(t4 + 1) * 4, 0:128], PB.v.re("p (a b) -> p a b", a=4), eng="dve")
                yield
            fw.tt(kr.v, kr.v, kr.v, ALU.mult, eng="pool")
            for hj in range(2):
                for tb in range(NTB):
                    bounds_unit(B, 1, hj, tb)
                    yield
            fw.tt(kr.v, B["qr"].v, B["qr"].v, ALU.mult, eng="pool")
            for hj in range(2):
                for tb in range(NTB):
                    bounds_unit(B, 0, hj, tb)
                    yield
            mx = B["mx"]
            for hj in range(2):
                fw.tt(mx[:, 6 + 8 * hj:7 + 8 * hj], mx[:, 4 + 8 * hj:5 + 8 * hj], mx[:, 5 + 8 * hj:6 + 8 * hj], ALU.add)
                fw.ts(B["negb"][:, hj:hj + 1], mx[:, 6 + 8 * hj:7 + 8 * hj], -0.5 * scale, None, op0=ALU.mult)

        accsb = fw.alloc("accsb", [4, 512], F32)
        rden = fw.alloc("rden", [4, 2], F32)
        ssq = fw.alloc("ssq", [8], F32)

        def make_finalize(hp):
            def finalize(qb, accs):
                for b in range(4):
                    fw.copy(accsb[:, b, :], fw.banks[4 + b].v, eng="act" if b % 2 else "dve")
                fw.recip(rden.v, accsb[:, :, 128:512:256])
                fw.ts(rden[:, 2:4, :], rden[:, 2:4, :], neglam, None, op0=ALU.mult)
                for qt in range(4):
                    b, off = qt // 2, (qt % 2) * 256
                    o_o = oo[qt]
                    fw.ts(ot[0].v, accsb[:, b, off:off + 128], rden[:, b, qt % 2:qt % 2 + 1], None, op0=ALU.mult)
                    fw.stt(o_o.v, accsb[:, 2 + b, off:off + 128], rden[:, 2 + b, qt % 2:qt % 2 + 1], ot[0].v, ALU.mult, ALU.add)
                    oa, ja, sa = o_o.ap, junk32.ap, ssq[:, qt:qt + 1].ap
                    fw.emit("dve", lambda e, oa=oa, ja=ja, sa=sa: e.scalar_tensor_tensor(
                        ja, oa, 1.0, oa, ALU.mult, ALU.mult, accum_out=sa), o_o.v.ts, junk32.v.ts + ssq.v.ts)

                def part2():
                    fw.act(ssq[:, 4:8], ssq[:, 0:4], AF.Ln, bias=self.eps_col(1e-5), scale=1.0 / 128)
                    fw.act(ssq[:, 4:8], ssq[:, 4:8], AF.Exp, scale=-0.5)
                    for qt in range(4):
                        tt = qb * 4 + qt
                        fw.stt(ao[:, tt, hp * 128:(hp + 1) * 128], oo[qt].v, ssq[:, 4 + qt:5 + qt], sub.v, ALU.mult, ALU.mult)
                return part2
            return finalize

        for _ in prologue_gen(0, bufs[0]):
            pass
        for hp in range(8):
            B = bufs[hp % 2]
            bg = prologue_gen(hp + 1, bufs[(hp + 1) % 2]) if hp + 1 < 8 else None
            self.attn_core(B["qr"], [B["kA"], B["kB"]], [B["negb"][:, 0:1], B["negb"][:, 1:2]],
                           B["vtm"], 128, scale, make_finalize(hp), bg=bg, bg_every=3, LA=1, sbanks=(0, 1))

        fw.release(m2)
        self.transpose_tm_to_fm(lambda tt, c: ao[:, tt, c * 128:(c + 1) * 128], hT, NCH)
        self.proj_residual(self.d["od_w_out"], hT, NCH)
        fw.release(m)

    def _dma_same_slot(self, slot, idx, src_v):
        fw = self.fw
        sem = slot._sem
        q = "sp" if src_v.ap.dtype == BF16 else "pool"
        fw.dma(q, slot[idx], src_v, sem)

    def filter_phase(self):
        fw = self.fw
        m = fw.mark()
        N2 = 2.0 / 4096.0
        s_c = fw.sem("filt")
        w1 = fw.alloc("fw1", [128], F32)
        w2 = fw.alloc("fw2", [128], F32)
        w3 = fw.alloc("fw3", [2048], F32)
        ft = fw.alloc("feats", [S], F32)
        winf = fw.alloc("winf", [16, 512], F32)
        winb = fw.alloc("winb", [16, 512], F32)
        fw.dma("sp", w1.v, self.d["hy_w1p"].v, s_c)
        fw.dma("sp", w2.v, self.d["hy_w2p"].v, s_c)
        fw.dma("sp", w3.v, self.d["hy_w3p"].v, s_c)
        fw.dma("sp", ft.v, self.d["featsT"].v, s_c)
        fw.dma("sp", winf.v, V([self.d["win_f"]], self.d["win_f"].ap.rearrange("(t p) c -> p t c", p=128)), s_c)
        fw.dma("sp", winb.v, V([self.d["win_b"]], self.d["win_b"].ap.rearrange("(t p) c -> p t c", p=128)), s_c)
        a1 = fw.alloc("a1T", [S], F32)
        a2 = fw.alloc("a2T", [S], F32)
        pre = [fw.alloc(f"pre{k}", [TB], F32) for k in range(2)]
        wr = [fw.alloc(f"wr{k}", [TB], F32) for k in range(2)]
        PI = math.pi
        for (wt, src, dst, bn) in ((w1, ft, a1, "hy_b1"), (w2, a1, a2, "hy_b2")):
            for tb in range(NTB):
                bank = fw.bank()
                fw.matmul(bank.v, wt.v, src[:, tb * TB:(tb + 1) * TB], True, True)
                p = pre[tb % 2]
                fw.ts(p.v, bank.v, self.vec(bn), self.vec("hy_freq"), op0=ALU.add, op1=ALU.mult)
                w1_, w2_ = wr
                fw.ts(w1_.v, p.v, PI, -2 * PI, op0=ALU.is_gt, op1=ALU.mult)
                fw.ts(w2_.v, p.v, -PI, 2 * PI, op0=ALU.is_lt, op1=ALU.mult)
                fw.tt(p.v, p.v, w1_.v, ALU.add)
                fw.tt(p.v, p.v, w2_.v, ALU.add)
                fw.act(dst[:, tb * TB:(tb + 1) * TB], p.v, AF.Sin)
        ed = [[fw.alloc(f"ed{o}{k}", [16, 512], BF16) for k in range(2)] for o in range(2)]
        t12 = [fw.alloc(f"t12{k}", [512], F32) for k in range(4)]
        k = 0
        for pt in range(16):
            for o in range(2):
                bf = fw.bank()
                fw.matmul(bf.v, a2[:, pt * 128:(pt + 1) * 128], w3[:, (2 * o) * 512:(2 * o + 1) * 512], True, True)
                bb = fw.bank()
                fw.matmul(bb.v, a2[:, pt * 128:(pt + 1) * 128], w3[:, (2 * o + 1) * 512:(2 * o + 2) * 512], True, True)
                t1, t2 = t12[(k * 2) % 4], t12[(k * 2 + 1) % 4]
                k += 1
                fw.tt(t1.v, bf.v, winf[:, pt, :], ALU.mult)
                fw.tt(t2.v, bb.v, winb[:, pt, :], ALU.mult)
                fw.tt(ed[o][0][:, pt, :], t1.v, t2.v, ALU.add, eng="dve")
                fw.tt(ed[o][1][:, pt, :], t1.v, t2.v, ALU.subtract, eng="pool")
        ldF = self.dft_stream()
        kst = [fw.alloc(f"kst{k}", [2, 512], BF16) for k in range(2)]
        ksem = [fw.sem(f"kst{k}") for k in range(2)]
        it = 0
        for fc in range(16):
            slot = ldF(self.d["dftF"][fc])
            for o in range(2):
                be = fw.bank()
                for st in range(16):
                    fw.matmul(be.v, slot[:, 0, st, :], ed[o][0][:, st, :], st == 0, st == 15)
                bd = fw.bank()
                for st in range(16):
                    fw.matmul(bd.v, slot[:, 1, st, :], ed[o][1][:, st, :], st == 0, st == 15)
                kt_ = kst[it % 2]
                fw.act(kt_[:, 0, :], be.v, AF.Copy, scale=N2)
                fw.copy(kt_[:, 1, :], bd.v, eng="dve") if False else fw.ts(kt_[:, 1, :], bd.v, N2, None, op0=ALU.mult)
                fw.dma("act", self.kspec[o][fc].v, kt_.v, ksem[it % 2])
                it += 1
        self.precast()
        fw.release(m)

    def dft_stream(self):
        fw = self.fw
        slots = [fw.alloc(f"dft{i}", [2, 16, 128], BF16) for i in range(2)]
        sems = [fw.sem(f"dft{i}") for i in range(2)]
        st = {"i": 0}

        def load(src_v, shape4=None):
            i = st["i"] % 2
            st["i"] += 1
            t = slots[i]
            dst = t.v if shape4 is None else t.v.re("p a b c -> p (a b c)").re("p (a b c) -> p a b c", a=shape4[0], b=shape4[1])
            fw.dma("sp", dst, src_v, sems[i])
            return V([t], dst.ap)
        return load

    def mix0(self):
        fw = self.fw
        m = fw.mark()
        omlaT = fw.alloc_grid("omlaT", 4, S, BF16)
        m1 = fw.mark()
        hT = fw.alloc_grid("hT", NCH, S, BF16)
        self.rmsnorm_T(lambda c, a, b: self.xT.c(c, a, b), self.vec("norm_mix0"),
                       lambda c, a, b: hT.c(c, a, b), S)
        self.mla(hT, omlaT)
        fw.release(m1)
        vx = fw.alloc_grid("vx", 12, S, BF16)
        m1 = fw.mark()
        hT = fw.alloc_grid("hT", NCH, S, BF16)
        self.rmsnorm_T(lambda c, a, b: self.xT.c(c, a, b), self.vec("norm_mix0"),
                       lambda c, a, b: hT.c(c, a, b), S)
        self.hyena_inproj(hT, vx)
        fw.release(m1)
        self.hyena_conv(vx)
        class Cat:
            def c(_, c, t0, t1):
                return vx.c(c, t0, t1) if c < 4 else omlaT.c(c - 4, t0, t1)
        self.proj_residual(self.d["ev_w_out"], Cat(), NCH)
        fw.release(m)

    def mla(self, hT, omlaT):
        fw = self.fw
        m = fw.mark()
        scale = 96 ** -0.5
        win = self.d["ev_w_in"]
        src3 = win.ap.rearrange("(c p) n -> p c n", p=128)
        s_w = fw.sem("mlaw")
        wm = fw.alloc("wm", [NCH, 640], BF16)
        wuq = fw.alloc("wuq", [2, 8, 128], BF16)
        wuqs = fw.alloc("wuqs", [2, 8, 128], BF16)
        wkk = fw.alloc("wkk", [8, 128], BF16)
        wkv = fw.alloc("wkv", [8, 64], BF16)
        for t in (wm, wuq, wuqs, wkk):
            fw.memset(t.v, 0.0, eng="pool")
        fw.dma("pool", wm[:, :, 0:384], V([win], src3[:, :, 1536:1920]), s_w)
        fw.dma("pool", wm[:, :, 448:480], V([win], src3[:, :, 1920:1952]), s_w)
        fw.dma("pool", wm[:, :, 576:592], V([win], src3[:, :, 1936:1952]), s_w)
        fw.dma("pool", wm[:, :, 592:608], V([win], src3[:, :, 1920:1936]), s_w)
        uq = self.d["mla_w_uq"]
        uq4 = uq.ap.rearrange("(c p) (h d) -> p c h d", p=128, d=96)
        for kc in range(2):
            fw.dma("pool", wuq[:, kc, :, 0:96], V([uq], uq4[:, kc, :, :]), s_w)
            fw.dma("pool", wuqs[:, kc, :, 64:80], V([uq], uq4[:, kc, :, 80:96]), s_w)
            fw.dma("pool", wuqs[:, kc, :, 80:96], V([uq], uq4[:, kc, :, 64:80]), s_w)
        ukv = self.d["mla_w_ukv"]
        ukv3 = ukv.ap.rearrange("p (h d) -> p h d", d=128)
        fw.dma("pool", wkk[:, :, 0:64], V([ukv], ukv3[:, :, 0:64]), s_w)
        fw.dma("pool", wkv.v, V([ukv], ukv3[:, :, 64:128]), s_w)
        tabR = self.tab_stream("tabR", self.d["ropeR_cos"], self.d["ropeR_sin"])
        cqn = fw.alloc("cqn", [2, S], BF16)
        ckvn = fw.alloc("ckvn", [1, S], BF16)
        kpe = fw.alloc("kpe", [S], BF16)
        rtmp = [fw.alloc("rtmp0", [TB], F32)] * 2
        ru = [fw.alloc("ru0", [TB], F32)] * 2
        for tb in range(NTB):
            t0, t1 = tb * TB, (tb + 1) * TB
            b1 = fw.bank()
            for c in range(NCH):
                fw.matmul(b1.v, wm[:, c, 384:512], hT.c(c, t0, t1), c == 0, c == NCH - 1)
            b2 = fw.bank()
            for c in range(NCH):
                fw.matmul(b2.v, wm[:, c, 512:640], hT.c(c, t0, t1), c == 0, c == NCH - 1)
            tm, uu = rtmp[tb % 2], ru[tb % 2]
            cv, sv = tabR(tb)
            fw.tt(tm[64:96, :], b1[64:96, :], cv[64:96, :], ALU.mult)
            fw.tt(uu[64:96, :], b2[64:96, :], sv[64:96, :], ALU.mult)
            fw.tt(kpe[64:96, t0:t1], tm[64:96, :], uu[64:96, :], ALU.add, eng="pool")
        for (blks, gname, dstt) in (((0, 1), "q_norm", cqn), ((2,), "kv_norm", ckvn)):
            m2 = fw.mark()
            cq = fw.alloc("cq", [len(blks), S], F32)
            for bi, blk in enumerate(blks):
                for tb in range(NTB):
                    t0, t1 = tb * TB, (tb + 1) * TB
                    bank = fw.bank()
                    for c in range(NCH):
                        fw.matmul(bank.v, wm[:, c, blk * 128:(blk + 1) * 128], hT.c(c, t0, t1), c == 0, c == NCH - 1)
                    fw.copy(cq[:, bi, t0:t1], bank.v, eng="act")
            self.rmsnorm_T(lambda c, a, b: cq[:, c, a:b], self.vec(gname), lambda c, a, b: dstt[:, c, a:b], S, nch=len(blks))
            fw.release(m2)
        sq = fw.alloc("sq", [S], BF16)
        bufs = []
        for bi in range(2):
            B = dict(qh=fw.alloc(f"qh{bi}", [S], BF16), kh=fw.alloc(f"kh{bi}", [S], BF16),
                     negb=fw.alloc(f"negb{bi}", [1], F32), mx=fw.alloc(f"mx{bi}", [8], F32),
                     vt2=fw.alloc(f"vt2{bi}", [S // 128, 2, 65], BF16))
            fw.memset(B["qh"].v, 0.0, eng="pool")
            fw.memset(B["kh"].v, 0.0, eng="pool")
            fw.memset(B["vt2"][:, :, :, 64:65], 1.0, eng="dve")
            bufs.append(B)
        ao2 = fw.alloc("ao2", [S // 128, 128], BF16)
        self._ee = [fw.alloc(f"ee{k}", [TB], BF16) for k in range(3)]
        self._eit = 0
        self._sbk = 0
        self._acc_par = 0
        fin = fw.alloc("fin", [4], F32)
        PBs = [fw.banks[3], fw.banks[6], fw.banks[7]]
        pbc = [0]

        def nextpb():
            pbc[0] += 1
            return PBs[pbc[0] % len(PBs)]

        def prologue_gen(h):
            hp, hj = h // 2, h % 2
            B = bufs[h % 2]
            qh, kh, mx = B["qh"], B["kh"], B["mx"]
            if hj == 0:
                vt2 = bufs[hp % 2]["vt2"]
                for t4 in range(4):
                    PB = nextpb()
                    for q in range(4):
                        tt = t4 * 4 + q
                        fw.matmul(PB[:, q * 128:(q + 1) * 128], ckvn[:, 0, tt * 128:(tt + 1) * 128],
                                  wkv[:, 2 * hp:2 * hp + 2, :], True, True)
                    fw.copy(vt2[:, t4 * 4:(t4 + 1) * 4, :, 0:64],
                            PB.v.re("p (a b c) -> p a b c", a=4, b=2), eng="dve")
                    yield
            for tb in range(NTB):
                t0, t1 = tb * TB, (tb + 1) * TB
                cv, sv = tabR(tb)
                PB = nextpb()
                for kc in range(2):
                    fw.matmul(PB.v, wuq[:, kc, h, :], cqn[:, kc, t0:t1], kc == 0, kc == 1)
                fw.copy(qh[0:64, t0:t1], PB[0:64, :], eng="dve")
                tm, uu = rtmp[tb % 2], ru[tb % 2]
                fw.tt(tm[64:96, :], PB[64:96, :], cv[64:96, :], ALU.mult)
                yield
                PB = nextpb()
                for kc in range(2):
                    fw.matmul(PB.v, wuqs[:, kc, h, :], cqn[:, kc, t0:t1], kc == 0, kc == 1)
                fw.tt(uu[64:96, :], PB[64:96, :], sv[64:96, :], ALU.mult)
                fw.tt(qh[64:96, t0:t1], tm[64:96, :], uu[64:96, :], ALU.add, eng="pool")
                yield
                PB = nextpb()
                fw.matmul(PB.v, wkk[:, h, :], ckvn[:, 0, t0:t1], True, True)
                fw.copy(kh[0:64, t0:t1], PB[0:64, :], eng="dve")
                yield
            fw.copy(kh[64:96, :], kpe[64:96, :], eng="pool")
            for wi, srct in enumerate((qh, kh)):
                fw.tt(sq.v, srct.v, srct.v, ALU.mult, eng="pool")
                for tb in range(NTB):
                    PB = nextpb()
                    fw.matmul(PB.v, self.ones_bf.v, sq[:, tb * TB:(tb + 1) * TB], True, True)
                    fw.reduce(mx[:, tb:tb + 1], PB.v, ALU.max)
                    yield
                fw.reduce(mx[:, 4 + wi:5 + wi], mx[:, 0:4], ALU.max)
            fw.tt(mx[:, 6:7], mx[:, 4:5], mx[:, 5:6], ALU.add)
            fw.ts(B["negb"].v, mx[:, 6:7], -0.5 * scale, None, op0=ALU.mult)

        def make_finalize(hj):
            def finalize(qb, accs):
                for qt in range(4):
                    tt = qb * 4 + qt
                    a1 = accs[0][qt]
                    fw.recip(fin[:, qt:qt + 1], a1[:, 64:65])
                    fw.ts(ao2[:, tt, hj * 64:(hj + 1) * 64], a1[:, 0:64], fin[:, qt:qt + 1], None, op0=ALU.mult)
                return None
            return finalize

        for _ in prologue_gen(0):
            pass
        for h in range(8):
            hp, hj = h // 2, h % 2
            B = bufs[h % 2]
            bg = prologue_gen(h + 1) if h + 1 < 8 else None
            self.attn_core(B["qh"], [B["kh"]], [B["negb"].v], bufs[hp % 2]["vt2"][:, :, hj, :], 64, scale,
                           make_finalize(hj), bg=bg, bg_every=2, sbanks=(0, 1, 2), one_bank_acc=True)
            if hj == 1:
                for t4 in range(4):
                    PB = nextpb()
                    bb = V(PB.v.ts, PB.ap.bitcast(BF16))
                    for q in range(4):
                        fw.transpose(bb[:, q * 128:(q + 1) * 128], ao2[:, t4 * 4 + q, :], self.ident_bf.v)
                    fw.copy(omlaT.c(hp, t4 * 512, (t4 + 1) * 512), bb[:, 0:512], eng="dve")
        fw.release(m)

    def hyena_inproj(self, hT, vx):
        fw = self.fw
        m = fw.mark()
        win = self.d["ev_w_in"]
        ld = self.wstream("whin", [NCH, 512])
        ub = [fw.alloc("ub0", [S + 2], F32)] * 2
        ac = [fw.alloc("uacc0", [S], F32)] * 2
        for u in ub[:1]:
            fw.memset(u[:, 0:1], 0.0, eng="dve")
            fw.memset(u[:, S + 1:S + 2], 0.0, eng="dve")
        for g in range(3):
            w = ld(self.wsrc(win, g * 512, (g + 1) * 512))
            for mm in range(4):
                hc = g * 4 + mm
                u, a = ub[hc % 2], ac[hc % 2]
                for tb in range(NTB):
                    bank = fw.bank()
                    for c in range(NCH):
                        fw.matmul(bank.v, w[:, c, mm * 128:(mm + 1) * 128], hT.c(c, tb * TB, (tb + 1) * TB), c == 0, c == NCH - 1)
                    fw.copy(u[:, 1 + tb * TB:1 + (tb + 1) * TB], bank.v, eng="act")
                fw.ts(a.v, u[:, 0:S], self.vec("conv_w0", hc, hc + 1), self.vec("conv_b", hc, hc + 1), op0=ALU.mult, op1=ALU.add)
                fw.stt(a.v, u[:, 1:S + 1], self.vec("conv_w1", hc, hc + 1), a.v, ALU.mult, ALU.add)
                fw.stt(vx.sl(hc, hc + 1, 0, S), u[:, 2:S + 2], self.vec("conv_w2", hc, hc + 1), a.v, ALU.mult, ALU.add)
        fw.release(m)

    def hyena_conv(self, vx):
        fw = self.fw
        m = fw.mark()
        zT = fw.alloc("zT", [16, 512], BF16)
        Ut = fw.alloc("Ut", [16, 512], BF16)
        Vt = fw.alloc("Vt", [16, 512], BF16)
        ldF = self.dft_stream()
        ksl = [fw.alloc("ksl0", [2, 512], BF16)] * 2
        ksem = [fw.sem("ksl0")] * 2
        tq = [fw.alloc(f"tq{k}", [512], F32) for k in range(2)] * 2
        kit = 0
        for o in range(2):
            for tt in range(16):
                bank = fw.bank(0, 4)
                bb = V(bank.v.ts, bank.ap.bitcast(BF16))
                for c in range(4):
                    fw.transpose(bb[:, c * 128:(c + 1) * 128], vx.c(c, tt * 128, (tt + 1) * 128), self.ident_bf.v)
                fw.copy(zT[:, tt, :], bb[:, 0:512], eng="act" if tt % 2 else "dve")
            for fc in range(16):
                slot = ldF(self.d["dftF"][fc])
                ks = ksl[kit % 2]
                fw.dma("act", ks.v, self.kspec[o][fc].v, ksem[kit % 2])
                kit += 1
                ba = fw.bank(0, 4)
                for st in range(16):
                    fw.matmul(ba.v, slot[:, 0, st, :], zT[:, st, :], st == 0, st == 15)
                bb_ = fw.bank(0, 4)
                for st in range(16):
                    fw.matmul(bb_.v, slot[:, 1, st, :], zT[:, st, :], st == 0, st == 15)
                t1, t2 = tq[0], tq[1]
                fw.tt(t1.v, ba.v, ks[:, 0, :], ALU.mult)
                fw.tt(t2.v, bb_.v, ks[:, 1, :], ALU.mult)
                fw.tt(Ut[:, fc, :], t1.v, t2.v, ALU.subtract, eng="pool")
                fw.tt(t1.v, ba.v, ks[:, 1, :], ALU.mult)
                fw.tt(t2.v, bb_.v, ks[:, 0, :], ALU.mult)
                fw.tt(Vt[:, fc, :], t1.v, t2.v, ALU.add, eng="pool")
            gate0 = 4 + 4 * o
            for tb in range(NTB):
                t0, t1_ = tb * TB, (tb + 1) * TB
                for fg in range(4):
                    slot = ldF(self.d["dftI"][tb, fg], shape4=(4, 2))
                    for fci in range(4):
                        fc = fg * 4 + fci
                        for cc in range(4):
                            fw.matmul(fw.banks[4 + cc].v, Ut[:, fc, cc * 128:(cc + 1) * 128], slot[:, fci, 0, :], fc == 0, False)
                            fw.matmul(fw.banks[4 + cc].v, Vt[:, fc, cc * 128:(cc + 1) * 128], slot[:, fci, 1, :], False, fc == 15)
                for cc in range(4):
                    zc = vx.c(cc, t0, t1_)
                    tmp = tq[cc % 2]
                    fw.stt(tmp.v, zc, self.vec(f"hy_skip{o}", cc, cc + 1), fw.banks[4 + cc].v, ALU.mult, ALU.add)
                    fw.tt(zc, tmp.v, vx.c(gate0 + cc, t0, t1_), ALU.mult, eng="pool")
        fw.release(m)

    def cross(self, i):
        fw = self.fw
        m = fw.mark()
        scale = 256 ** -0.5
        wq, wkv, wo = self.d[f"xa_wq{i}"], self.d[f"xa_wkv{i}"], self.d[f"xa_wo{i}"]
        ld = self.wstream("wx", [NCH, 512])
        pre_k = [ld(self.wsrc(wkv, g * 512, (g + 1) * 512)) for g in range(2)]
        self.load_mem(self.cur_seq)
        hT = fw.alloc_grid("hT", NCH, S, BF16)
        self.rmsnorm_T(lambda c, a, b: self.xT.c(c, a, b), self.vec(f"norm_cross{i}"),
                       lambda c, a, b: hT.c(c, a, b), S)
        kT = fw.alloc("kT", [NCH, NMEM], BF16)
        vtm = fw.alloc("vtm", [2, D], BF16)
        qT = fw.alloc_grid("qT", NCH, S, BF16)
        for g in range(2):
            w = pre_k[g]
            for mm in range(4):
                cc = g * 4 + mm
                bank = fw.bank()
                for c in range(NCH):
                    fw.matmul(bank[:, 0:NMEM], w[:, c, mm * 128:(mm + 1) * 128], self.memnT[:, c, :], c == 0, c == NCH - 1)
                fw.copy(kT[:, cc, :], bank[:, 0:NMEM], eng="act")
        for g in range(2):
            w = ld(self.wsrc(wkv, D + g * 512, D + (g + 1) * 512))
            for mt in range(2):
                bank = fw.bank()
                for c in range(NCH):
                    fw.matmul(bank.v, self.memnT[:, c, mt * 128:(mt + 1) * 128], w[:, c, :], c == 0, c == NCH - 1)
                fw.copy(vtm[:, mt, g * 512:(g + 1) * 512], bank.v, eng="act")
        for g in range(2):
            w = ld(self.wsrc(wq, g * 512, (g + 1) * 512))
            for mm in range(4):
                cc = g * 4 + mm
                for tb in range(NTB):
                    bank = fw.bank()
                    for c in range(NCH):
                        fw.matmul(bank.v, w[:, c, mm * 128:(mm + 1) * 128], hT.c(c, tb * TB, (tb + 1) * TB), c == 0, c == NCH - 1)
                    fw.copy(qT.c(cc, tb * TB, (tb + 1) * TB), bank.v, eng="act" if tb % 2 else "dve")
        negb = fw.alloc("negb", [4], F32)
        sqb = [fw.alloc(f"sqb{k}", [2, TB], BF16) for k in range(2)]
        mx = fw.alloc("mx", [8], F32)
        for h in range(4):
            for tb in range(NTB):
                sq = sqb[tb % 2]
                for u in range(2):
                    qs = qT.c(2 * h + u, tb * TB, (tb + 1) * TB)
                    fw.act(sq[:, u, :], qs, AF.Square)
                bank = fw.bank()
                for u in range(2):
                    fw.matmul(bank.v, self.ones_bf.v, sq[:, u, :], u == 0, u == 1)
                fw.reduce(mx[:, tb:tb + 1], bank.v, ALU.max)
            sq = sqb[0]
            for u in range(2):
                ks = kT[:, 2 * h + u, :]
                fw.act(sq[:, u, 0:NMEM], ks, AF.Square)
            bank = fw.bank()
            for u in range(2):
                fw.matmul(bank[:, 0:NMEM], self.ones_bf.v, sq[:, u, 0:NMEM], u == 0, u == 1)
            fw.reduce(mx[:, 4:5], bank[:, 0:NMEM], ALU.max)
            fw.reduce(mx[:, 5:6], mx[:, 0:4], ALU.max)
            fw.tt(mx[:, 6:7], mx[:, 4:5], mx[:, 5:6], ALU.add)
            fw.ts(negb[:, h:h + 1], mx[:, 6:7], -0.5 * scale, None, op0=ALU.mult)
        oT = hT
        ee = [fw.alloc(f"ee{k}", [TB], BF16) for k in range(4)]
        rd = [fw.alloc(f"rd{k}", [TB], F32) for k in range(2)]
        its = [(h, tb) for h in range(4) for tb in range(NTB)]
        ee = ee + [fw.alloc(f"ee{k}", [TB], BF16) for k in range(4, 6)]

        def qk_exp(n):
            h, tb = its[n]
            t0, t1 = tb * TB, (tb + 1) * TB
            es = []
            for kt in range(2):
                bank = fw.banks[(n * 2 + kt) % 4]
                for u in range(2):
                    fw.matmul(bank.v, kT[:, 2 * h + u, kt * 128:(kt + 1) * 128], qT.c(2 * h + u, t0, t1), u == 0, u == 1)
                e = ee[(n * 2 + kt) % 6]
                fw.act(e.v, bank.v, AF.Exp, bias=negb[:, h:h + 1], scale=scale)
                es.append(e)
            return es

        pend = qk_exp(0)
        for n, (h, tb) in enumerate(its):
            t0, t1 = tb * TB, (tb + 1) * TB
            es = pend
            if n + 1 < len(its):
                pend = qk_exp(n + 1)
            bd = fw.bank(4, 8)
            for kt in range(2):
                fw.matmul(bd.v, self.ones_bf.v, es[kt].v, kt == 0, kt == 1)
            r = rd[n % 2]
            fw.act(r.v, bd.v, AF.Ln)
            fw.act(r.v, r.v, AF.Exp, scale=-1.0)
            for dc in range(2):
                bo = fw.bank(4, 8)
                for kt in range(2):
                    fw.matmul(bo.v, vtm[:, kt, h * 256 + dc * 128:h * 256 + (dc + 1) * 128], es[kt].v, kt == 0, kt == 1)
                fw.tt(oT.c(2 * h + dc, t0, t1), bo.v, r.v, ALU.mult)
        for g in range(2):
            w = ld(self.wsrc(wo, g * 512, (g + 1) * 512))
            for mm in range(4):
                c_out = g * 4 + mm
                for tb in range(NTB):
                    t0, t1 = tb * TB, (tb + 1) * TB
                    bank = fw.bank()
                    for c in range(NCH):
                        fw.matmul(bank.v, w[:, c, mm * 128:(mm + 1) * 128], oT.c(c, t0, t1), c == 0, c == NCH - 1)
                    xs = self.xT.c(c_out, t0, t1)
                    fw.tt(xs, bank.v, xs, ALU.add)
        fw.release(m)


ALL_STAGES = ["mix0", "cross0", "mlp0", "mix1", "cross1", "mlp1", "final"]


def rope_consts():
    out = {}
    inv = (10000.0 ** (-np.arange(0, 64, 2, dtype=np.float32) / 64)).astype(np.float32)
    ang = (np.arange(S, dtype=np.float32)[None, :] * inv[:, None]).astype(np.float32)
    cos, sin = np.cos(ang).astype(np.float32), np.sin(ang).astype(np.float32)
    p = np.arange(128)
    out["ropeD_cos"] = np.ascontiguousarray(cos[p % 32])
    sgn = np.where((p % 64) < 32, -1.0, 1.0).astype(np.float32)[:, None]
    out["ropeD_sin"] = np.ascontiguousarray(sin[p % 32] * sgn)
    return out


_CONST_CACHE = {}


def hyena_consts():
    if "c" in _CONST_CACHE:
        return _CONST_CACHE["c"]
    out = {}
    L = S
    N = 2 * L
    t = np.linspace(0.0, 1.0, L, dtype=np.float32)[:, None]
    w = ((2.0 * math.pi / L) * np.arange(L, dtype=np.float32))[:, None].astype(np.float32)
    bands = np.linspace(1e-4, 15, 16, dtype=np.float32)[None, :]
    feats = np.concatenate([t, np.cos(bands * w), -np.sin(bands * w)], axis=-1).astype(np.float32)
    ft = np.zeros((128, L), np.float32)
    ft[:33] = feats.T
    out["featsT"] = ft
    deltas = np.linspace(math.log(1e-2) / 1.5, math.log(1e-2) / 0.3, 512, dtype=np.float32)
    window = np.exp(-t * np.abs(deltas)[None, :]).astype(np.float32)
    out["win_f"] = np.ascontiguousarray(window)
    wb = window.copy()
    wb[0] = 0.0
    out["win_b"] = wb
    idx = np.arange(L, dtype=np.float64)
    ang = 2.0 * np.pi * np.outer(idx, idx + 0.5) / N
    C = np.cos(ang)
    Sn = np.sin(ang)
    bf = ml_dtypes.bfloat16
    F_ = np.stack([C, Sn], 0).reshape(2, 16, 128, 16, 128)
    out["dftF"] = np.ascontiguousarray(F_.transpose(3, 2, 0, 1, 4)).astype(bf)
    I_ = np.stack([C, Sn], 0).reshape(2, 4, 512, 4, 4, 128)
    out["dftI"] = np.ascontiguousarray(I_.transpose(1, 3, 5, 4, 0, 2)).astype(bf)
    inv = (10000.0 ** (-np.arange(0, 32, 2, dtype=np.float32) / 32)).astype(np.float32)
    angr = (np.arange(S, dtype=np.float32)[None, :] * inv[:, None]).astype(np.float32)
    cr, sr = np.cos(angr).astype(np.float32), np.sin(angr).astype(np.float32)
    rc = np.zeros((128, S), np.float32)
    rs = np.zeros((128, S), np.float32)
    rc[64:80] = cr
    rc[80:96] = cr
    rs[64:80] = -sr
    rs[80:96] = sr
    out["ropeR_cos"] = rc
    out["ropeR_sin"] = rs
    _CONST_CACHE["c"] = out
    return out


def make_in_map(inp, x, mem):
    mp = {
        "x": np.ascontiguousarray(x, np.float32),
        "mem": np.ascontiguousarray(mem, np.float32),
        "vecs": pack_vecs(inp),
        "ident": np.eye(128, dtype=np.float32),
    }
    mp["od_w_qkv"] = np.ascontiguousarray(inp["od_w_qkv"][0], np.float32)
    mp["od_w_out"] = np.ascontiguousarray(inp["od_w_out"][0], np.float32)
    mp.update(rope_consts())
    mp.update(hyena_consts())
    mp["ev_w_in"] = np.ascontiguousarray(inp["ev_w_in"][0], np.float32)
    mp["ev_w_out"] = np.ascontiguousarray(inp["ev_w_out"][0], np.float32)
    mp["mla_w_uq"] = np.ascontiguousarray(inp["mla_w_uq"][0], np.float32)
    mp["mla_w_ukv"] = np.ascontiguousarray(inp["mla_w_ukv"][0], np.float32)
    w1p = np.zeros((128, 128), np.float32)
    w1p[:33, :64] = inp["hy_w1"][0]
    w2p = np.zeros((128, 128), np.float32)
    w2p[:64, :64] = inp["hy_w2"][0]
    w3p = np.zeros((128, 2048), np.float32)
    w3p[:64] = inp["hy_w3"][0]
    mp["hy_w1p"], mp["hy_w2p"], mp["hy_w3p"] = w1p, w2p, w3p
    for i in range(2):
        mp[f"xa_wq{i}"] = np.ascontiguousarray(inp["xa_wq"][i], np.float32)
        mp[f"xa_wkv{i}"] = np.ascontiguousarray(inp["xa_wkv"][i], np.float32)
        mp[f"xa_wo{i}"] = np.ascontiguousarray(inp["xa_wo"][i], np.float32)
        mp[f"mlp_up{i}"] = np.ascontiguousarray(inp["mlp_up"][i], np.float32)
        mp[f"mlp_down{i}"] = np.ascontiguousarray(inp["mlp_down"][i], np.float32)
    return mp


def run_stages(inp, x, stages, nseq, n_cores, trace=False):
    k = K(nseq, stages)
    in_maps = []
    for c in range(n_cores):
        mp = make_in_map(inp, x[c * nseq:(c + 1) * nseq], inp["mem"][c * nseq:(c + 1) * nseq])
        in_maps.append({n: mp[n] for n in k.in_names})
    res = run_bass_kernel_spmd(k.nc, in_maps, core_ids=list(range(n_cores)), trace=trace)
    out = np.concatenate([np.asarray(r["y"]) for r in res.results], axis=0)
    return out, res


def kernel(**inputs):
    inp = {k: np.asarray(v) for k, v in inputs.items()}
    out, _ = run_stages(inp, inp["x"], ALL_STAGES, SEQ_PER_CORE, N_CORES)
    return out.astype(np.float32)
```

```python
import math
import numpy as np
import ml_dtypes
import concourse.bass as bass
import concourse.mybir as mybir
from concourse.bass_utils import run_bass_kernel_spmd

DT = mybir.dt
F32, BF16, U8 = DT.float32, DT.bfloat16, DT.uint8
ALU = mybir.AluOpType
AF = mybir.ActivationFunctionType
AX = mybir.AxisListType
DSIZE = {F32: 4, BF16: 2, U8: 1}

D = 1024
S = 2048
NMEM = 256
NCH = 8
TB = 512
NTB = 4
EPS = 1e-6
N_CORES = 8
SEQ_PER_CORE = 2
SBUF_BYTES = 212736


class Sem:
    def __init__(self, nc, name):
        self.h = nc.alloc_semaphore(name=name)
        self.count = 0
        self.name = name


class Tile:
    def __init__(self, ap, name, space):
        self.ap = ap
        self.name = name
        self.space = space
        self.w = None
        self.r = {}

    def __getitem__(self, idx):
        return V([self], self.ap[idx])

    @property
    def v(self):
        return V([self], self.ap)


class V:
    def __init__(self, ts, ap):
        self.ts = ts
        self.ap = ap

    def __getitem__(self, idx):
        return V(self.ts, self.ap[idx])

    def re(self, pat, **kw):
        return V(self.ts, self.ap.rearrange(pat, **kw))


class Grid:
    def __init__(self, ap, name, nch, ntok, tbw):
        self.ap = ap
        self.tbw = tbw
        self.nch = nch
        self.cells = [[Tile(ap[:, c, tb * tbw:(tb + 1) * tbw], f"{name}_{c}_{tb}", "sbuf")
                       for tb in range(ntok // tbw)] for c in range(nch)]

    def all_tiles(self):
        return [t for row in self.cells for t in row]

    def sl(self, c0, c1, t0, t1):
        tiles = [self.cells[c][tb] for c in range(c0, c1)
                 for tb in range(t0 // self.tbw, (t1 - 1) // self.tbw + 1)]
        if c1 - c0 == 1:
            return V(tiles, self.ap[:, c0, t0:t1])
        return V(tiles, self.ap[:, c0:c1, t0:t1])

    def c(self, c, t0, t1):
        return self.sl(c, c + 1, t0, t1)


class Eng:
    def __init__(self, nc, name):
        self.name = name
        self.sem = Sem(nc, "s_" + name)
        self.known = {}
        self.ops = []


class FW:
    def __init__(self, nc, sbuf_bytes=SBUF_BYTES):
        self.nc = nc
        self.eng = {k: Eng(nc, k) for k in ("pe", "dve", "act", "pool", "sp")}
        self._cms = []
        cm = nc.sbuf_tensor("arena", [128, sbuf_bytes], U8)
        self.arena = cm.__enter__()
        self._cms.append(cm)
        cm = nc.psum_tensor("psum", [128, 8 * 512], F32)
        self.psum = cm.__enter__()
        self._cms.append(cm)
        self.sbuf_bytes = sbuf_bytes
        self.top = 0
        self.peak = 0
        self.freed = []
        self.live = []
        self.banks = [Tile(self.psum[:, i * 512:(i + 1) * 512], f"bank{i}", "psum") for i in range(8)]
        self.rot = 0
        self.sems = {}
        self.n_instr = 0
        self.self_sync = True

    def bank(self, lo=0, hi=8):
        n = hi - lo
        b = self.banks[lo + self.rot % n]
        self.rot += 1
        return b

    def sem(self, name):
        if name not in self.sems:
            self.sems[name] = Sem(self.nc, name)
        return self.sems[name]

    def _raw(self, name, free_shape, dtype, parts):
        n = int(np.prod(free_shape)) * DSIZE[dtype]
        n_al = (n + 63) // 64 * 64
        lo = self.top
        hi = lo + n_al
        assert hi <= self.sbuf_bytes, f"SBUF overflow allocating {name}: {hi} > {self.sbuf_bytes}"
        self.top = hi
        self.peak = max(self.peak, hi)
        ap = self.arena[0:parts, lo:lo + n].bitcast(dtype)
        if len(free_shape) == 2:
            ap = ap.rearrange("p (a b) -> p a b", a=free_shape[0])
        elif len(free_shape) == 3:
            ap = ap.rearrange("p (a b c) -> p a b c", a=free_shape[0], b=free_shape[1])
        return ap, lo, hi

    def _inherit(self, t, lo, hi):
        for (flo, fhi, deps) in self.freed:
            if flo < hi and fhi > lo:
                for s, v in deps.items():
                    self._merge(t.r, s, v)
        self.live.append((lo, hi, t))

    def alloc(self, name, free_shape, dtype, parts=128):
        ap, lo, hi = self._raw(name, free_shape, dtype, parts)
        t = Tile(ap, name, "sbuf")
        self._inherit(t, lo, hi)
        return t

    def alloc_grid(self, name, nch, ntok, dtype, tbw=TB):
        ap, lo, hi = self._raw(name, [nch, ntok], dtype, 128)
        g = Grid(ap, name, nch, ntok, tbw)
        for t in g.all_tiles():
            self._inherit(t, lo, hi)
        return g

    def mark(self):
        return (self.top, len(self.live))

    def release(self, mark):
        top, nlive = mark
        recs = {}
        while len(self.live) > nlive:
            lo, hi, t = self.live.pop()
            deps = recs.setdefault((lo, hi), {})
            if t.w is not None:
                self._merge(deps, t.w[0], self._resolve(*t.w))
            for s, v in t.r.items():
                self._merge(deps, s, self._resolve(s, v))
        for (lo, hi), deps in recs.items():
            keep = []
            for (a, b, d) in self.freed:
                if a >= lo and b <= hi:
                    for s, v in d.items():
                        self._merge(deps, s, v)
                else:
                    keep.append((a, b, d))
            self.freed = keep
            self.freed.append((lo, hi, deps))
        self.top = top

    @staticmethod
    def _resolve(s, v):
        return s.count if v is None else v

    @staticmethod
    def _merge(d, s, v):
        if s in d:
            if d[s] is None or v is None:
                d[s] = None
            else:
                d[s] = max(d[s], v)
        else:
            d[s] = v

    def dram_tile(self, ap, name):
        return Tile(ap, name, "dram")

    def _waits(self, E, reads, writes):
        deps = {}
        for t in reads:
            if t.w is not None:
                self._merge(deps, t.w[0], self._resolve(*t.w))
        for t in writes:
            if t.w is not None:
                self._merge(deps, t.w[0], self._resolve(*t.w))
            for s, v in t.r.items():
                self._merge(deps, s, self._resolve(s, v))
        for s, v in deps.items():
            if v <= 0:
                continue
            if s is E.sem and not self.self_sync:
                continue
            if E.known.get(s, 0) >= v:
                continue
            E.known[s] = v
            E.ops.append(lambda e, h=s.h, vv=v: e.wait_ge(h, vv))

    def emit(self, eng, fn, reads, writes, pe_acc=False):
        E = self.eng[eng]
        reads = list(dict.fromkeys(reads))
        writes = list(dict.fromkeys(writes))
        if pe_acc:
            self._waits(E, reads, [])
        else:
            self._waits(E, reads, writes)
        E.sem.count += 1
        n = E.sem.count
        sh = E.sem.h
        E.ops.append(lambda e: fn(e).then_inc(sh, 1))
        self.n_instr += 1
        for t in reads:
            self._merge(t.r, E.sem, n)
        for t in writes:
            t.w = (E.sem, n)
            t.r = {}

    def dma(self, queue, out, in_, sem, **kw):
        Q = self.eng[queue]
        self._waits(Q, in_.ts, out.ts)
        sem.count += 16
        sh = sem.h
        oa, ia = out.ap, in_.ap
        Q.ops.append(lambda e: e.dma_start(out=oa, in_=ia, **kw).then_inc(sh, 16))
        self.n_instr += 1
        for t in in_.ts:
            self._merge(t.r, sem, None)
        for t in out.ts:
            t.w = (sem, None)
            t.r = {}
            if t.space == "dram":
                if not hasattr(t, "wsems"):
                    t.wsems = set()
                t.wsems.add(sem)

    def matmul(self, out, lhsT, rhs, start, stop):
        o, l, r = out.ap, lhsT.ap, rhs.ap
        self.emit("pe", lambda e: e.matmul(o, l, r, start=start, stop=stop),
                  lhsT.ts + rhs.ts, out.ts, pe_acc=not start)

    def transpose(self, out, in_, ident):
        o, i, d = out.ap, in_.ap, ident.ap
        self.emit("pe", lambda e: e.transpose(o, i, d), in_.ts + ident.ts, out.ts)

    def act(self, out, in_, func, bias=None, scale=1.0, accum_out=None):
        o, i = out.ap, in_.ap
        reads = list(in_.ts)
        writes = list(out.ts)
        kw = {}
        if bias is not None:
            if isinstance(bias, V):
                reads += bias.ts
                kw["bias"] = bias.ap
            else:
                kw["bias"] = bias
        if isinstance(scale, V):
            reads += scale.ts
            kw["scale"] = scale.ap
        else:
            kw["scale"] = scale
        if accum_out is not None:
            writes += accum_out.ts
            kw["accum_out"] = accum_out.ap
        self.emit("act", lambda e: e.activation(o, i, func, **kw), reads, writes)

    def tt(self, out, in0, in1, op, eng="dve"):
        o, a, b = out.ap, in0.ap, in1.ap
        self.emit(eng, lambda e: e.tensor_tensor(o, a, b, op), in0.ts + in1.ts, out.ts)

    def ts(self, out, in0, s1, s2=None, op0=ALU.mult, op1=None, eng="dve", accum_out=None):
        o, a = out.ap, in0.ap
        reads = list(in0.ts)
        writes = list(out.ts)
        if isinstance(s1, V):
            reads += s1.ts
            s1 = s1.ap
        if isinstance(s2, V):
            reads += s2.ts
            s2 = s2.ap
        kw = {}
        if op1 is not None:
            kw["op1"] = op1
        if accum_out is not None:
            writes += accum_out.ts
            kw["accum_out"] = accum_out.ap
        self.emit(eng, lambda e: e.tensor_scalar(o, a, s1, s2, op0, **kw), reads, writes)

    def stt(self, out, in0, scalar, in1, op0, op1):
        o, a, b = out.ap, in0.ap, in1.ap
        reads = in0.ts + in1.ts
        if isinstance(scalar, V):
            reads = reads + scalar.ts
            scalar = scalar.ap
        self.emit("dve", lambda e: e.scalar_tensor_tensor(o, a, scalar, b, op0, op1), reads, out.ts)

    def copy(self, out, in_, eng="dve"):
        o, i = out.ap, in_.ap
        if eng == "act":
            self.emit(eng, lambda e: e.copy(o, i), in_.ts, out.ts)
        else:
            self.emit(eng, lambda e: e.tensor_copy(o, i), in_.ts, out.ts)

    def memset(self, out, val, eng="dve"):
        o = out.ap
        self.emit(eng, lambda e: e.memset(o, val), [], out.ts)

    def reduce(self, out, in_, op, axis=AX.X):
        o, i = out.ap, in_.ap
        self.emit("dve", lambda e: e.tensor_reduce(o, i, axis, op), in_.ts, out.ts)

    def recip(self, out, in_):
        o, i = out.ap, in_.ap
        self.emit("dve", lambda e: e.reciprocal(o, i), in_.ts, out.ts)

    def finish(self, final_tiles):
        SP = self.eng["sp"]
        self._waits(SP, [], final_tiles)
        for t in final_tiles:
            for sm in getattr(t, "wsems", ()):
                if SP.known.get(sm, 0) < sm.count:
                    SP.known[sm] = sm.count
                    SP.ops.append(lambda e, h=sm.h, v=sm.count: e.wait_ge(h, v))
        for k, E in self.eng.items():
            if k == "sp" or E.sem.count == 0:
                continue
            if SP.known.get(E.sem, 0) < E.sem.count:
                SP.ops.append(lambda e, h=E.sem.h, v=E.sem.count: e.wait_ge(h, v))
        nc = self.nc
        engs = self.eng
        with nc.Block() as block:
            @block.tensor
            def _(e):
                for f in engs["pe"].ops:
                    f(e)

            @block.vector
            def _(e):
                for f in engs["dve"].ops:
                    f(e)

            @block.scalar
            def _(e):
                for f in engs["act"].ops:
                    f(e)

            @block.gpsimd
            def _(e):
                for f in engs["pool"].ops:
                    f(e)

            @block.sync
            def _(e):
                for f in engs["sp"].ops:
                    f(e)
        for cm in reversed(self._cms):
            cm.__exit__(None, None, None)


def fm(vec, nch):
    return np.ascontiguousarray(np.asarray(vec, np.float32).reshape(nch, 128).T)


def pad128(vec):
    v = np.zeros((128, 1), np.float32)
    v[:len(vec), 0] = vec
    return v


VEC_SPEC = []


def _vs(name, n):
    VEC_SPEC.append((name, n))


for _i in range(2):
    _vs(f"norm_mix{_i}", 8)
    _vs(f"norm_cross{_i}", 8)
    _vs(f"norm_mlp{_i}", 8)
_vs("final_norm", 8)
_vs("mem_norm", 8)
for _k in range(3):
    _vs(f"conv_w{_k}", 12)
_vs("conv_b", 12)
_vs("hy_skip0", 4)
_vs("hy_skip1", 4)
_vs("q_norm", 2)
_vs("kv_norm", 1)
_vs("hy_b1", 1)
_vs("hy_b2", 1)
_vs("hy_freq", 1)
_vs("subln", 128)
_vs("lqk", 256)
VEC_OFF = {}
_o = 0
for _n, _c in VEC_SPEC:
    VEC_OFF[_n] = (_o, _c)
    _o += _c
NV = _o


def pack_vecs(inp):
    cols = {}
    for i in range(2):
        cols[f"norm_mix{i}"] = fm(inp["norm_mix"][i], 8)
        cols[f"norm_cross{i}"] = fm(inp["norm_cross"][i], 8)
        cols[f"norm_mlp{i}"] = fm(inp["norm_mlp"][i], 8)
    cols["final_norm"] = fm(inp["final_norm"], 8)
    cols["mem_norm"] = fm(inp["mem_norm"], 8)
    for k in range(3):
        cols[f"conv_w{k}"] = fm(inp["ev_conv_w"][0, k], 12)
    cols["conv_b"] = fm(inp["ev_conv_b"][0], 12)
    cols["hy_skip0"] = fm(inp["hy_skip"][0, 0], 4)
    cols["hy_skip1"] = fm(inp["hy_skip"][0, 1], 4)
    cols["q_norm"] = fm(inp["mla_q_norm"][0], 2)
    cols["kv_norm"] = fm(inp["mla_kv_norm"][0], 1)
    cols["hy_b1"] = pad128(inp["hy_b1"][0])
    cols["hy_b2"] = pad128(inp["hy_b2"][0])
    cols["hy_freq"] = pad128(inp["hy_freq"][0])
    cols["subln"] = np.broadcast_to(np.asarray(inp["dif_subln"][0], np.float32)[None, :], (128, 128))
    lqk = np.concatenate([inp["dif_lq1"][0], inp["dif_lk1"][0], inp["dif_lq2"][0], inp["dif_lk2"][0]]).astype(np.float32)
    cols["lqk"] = np.broadcast_to(lqk[None, :], (128, 256))
    out = np.concatenate([cols[n] for n, _ in VEC_SPEC], axis=1).astype(np.float32)
    assert out.shape == (128, NV)
    return np.ascontiguousarray(out)


def fw_mark_of(grid, fw):
    cell = grid.cells[0][0]
    for i, (lo, hi, t) in enumerate(fw.live):
        if t is cell:
            return (lo, i)
    raise KeyError


class K:
    def __init__(self, nseq, stages, dbg=None):
        self.nseq = nseq
        self.stages = stages
        nc = bass.Bass("TRN2", target_bir_lowering=False)
        self.nc = nc
        fw = FW(nc)
        self.fw = fw
        self.d = {}
        self.in_names = []

        def din(name, shape, dt=F32):
            ap = nc.dram_tensor(name, list(shape), dt, kind="ExternalInput").ap()
            self.d[name] = fw.dram_tile(ap, name)
            self.in_names.append(name)

        din("x", [nseq, S, D])
        din("mem", [nseq, NMEM, D])
        din("vecs", [128, NV])
        din("ident", [128, 128])
        for i in range(2):
            din(f"xa_wq{i}", [D, D])
            din(f"xa_wkv{i}", [D, 2 * D])
            din(f"xa_wo{i}", [D, D])
            din(f"mlp_up{i}", [D, 4 * D])
            din(f"mlp_down{i}", [4 * D, D])
        din("ev_w_in", [D, 1952])
        din("ev_w_out", [D, D])
        din("mla_w_uq", [256, 768])
        din("mla_w_ukv", [128, 1024])
        din("hy_w1p", [128, 128])
        din("hy_w2p", [128, 128])
        din("hy_w3p", [128, 2048])
        din("featsT", [128, S])
        din("win_f", [S, 512])
        din("win_b", [S, 512])
        din("dftF", [16, 128, 2, 16, 128], BF16)
        din("dftI", [4, 4, 128, 4, 2, 512], BF16)
        din("ropeR_cos", [128, S])
        din("ropeR_sin", [128, S])
        ks = nc.dram_tensor("kspec", [2, 16, 128, 2, 512], BF16, kind="Internal").ap()
        self.kspec = [[fw.dram_tile(ks[o, fc], f"kspec{o}_{fc}") for fc in range(16)] for o in range(2)]
        din("od_w_qkv", [D, 3 * D])
        din("od_w_out", [D, D])
        din("ropeD_cos", [128, S])
        din("ropeD_sin", [128, S])
        yap = nc.dram_tensor("y", [nseq, S, D], F32, kind="ExternalOutput").ap()
        self.y = fw.dram_tile(yap, "y")

        self.setup_consts()
        self._precast_done = False
        if "mix0" in stages:
            self.filter_phase()
        if not self._precast_done:
            self.precast()
        self.setup()
        for s in range(nseq):
            self.cur_seq = s
            self.load_seq(s)
            for st in stages:
                if st.startswith("mlp"):
                    self.mlp(int(st[3:]))
                elif st.startswith("cross"):
                    self.cross(int(st[5:]))
                elif st == "mix1":
                    self.mix1()
                elif st == "mix0":
                    self.mix0()
                elif st == "final":
                    pass
                else:
                    raise ValueError(st)
            self.store_seq(s, final=("final" in stages))
        fw.finish([self.y])

    def vec(self, name, c0=0, c1=None):
        off, n = VEC_OFF[name]
        if c1 is None:
            c1 = n
        return self.vecs[:, off + c0:off + c1]

    def setup(self):
        fw = self.fw
        self.xT = fw.alloc_grid("xT", NCH, S, F32)

    def precast(self):
        fw = self.fw
        nc = self.nc
        self._precast_done = True
        need = []
        for st in self.stages:
            if st.startswith("mlp"):
                i = st[3:]
                need += [f"mlp_up{i}", f"mlp_down{i}"]
        sem = fw.sem("precast")
        for name in need:
            src = self.d[name]
            shp = list(src.ap.shape)
            ap = nc.dram_tensor(name + "_bf", shp, BF16, kind="Internal").ap()
            dst = fw.dram_tile(ap, name + "_bf")
            rows = shp[0]
            step = max(128, rows // 2)
            for r0 in range(0, rows, step):
                fw.dma("pool", dst[r0:r0 + step, :], src[r0:r0 + step, :], sem, max_dma_last_dim=8192)
            if name == "ev_w_in":
                self.d["ev_w_in_bf"] = dst
            else:
                self.d[name] = dst

    def setup_consts(self):
        fw = self.fw
        self.vecs = fw.alloc("vecs", [NV], F32)
        self.ident = fw.alloc("ident", [128], F32)
        self.ones_bf = fw.alloc("ones_bf", [128], BF16)
        s_c = fw.sem("const")
        fw.dma("sp", self.vecs.v, self.d["vecs"].v, s_c)
        fw.dma("sp", self.ident.v, self.d["ident"].v, s_c)
        fw.memset(self.ones_bf.v, 1.0, eng="dve")
        self.ident_bf = fw.alloc("ident_bf", [128], BF16)
        fw.copy(self.ident_bf.v, self.ident.v, eng="dve")
        self.maskA = fw.alloc("maskA", [128], BF16)
        self.maskB = fw.alloc("maskB", [128], BF16)
        self.mcol = fw.alloc("mcol", [2], F32)
        fw.memset(self.maskA.v, 0.0, eng="dve")
        fw.memset(self.maskB.v, 0.0, eng="dve")
        fw.memset(self.mcol.v, 0.0, eng="dve")
        fw.memset(self.maskA[0:64, :], 1.0, eng="dve")
        fw.memset(self.maskB[64:128, :], 1.0, eng="dve")
        fw.memset(self.mcol[0:64, 0:1], 1.0, eng="dve")
        fw.memset(self.mcol[64:128, 1:2], 1.0, eng="dve")
        self._eps = {}
        for ev in (EPS, 1e-5):
            t = fw.alloc(f"eps{len(self._eps)}", [1], F32)
            fw.memset(t.v, float(ev), eng="dve")
            self._eps[ev] = t

    def rmsnorm_T(self, src_fn, gain, dst_fn, ntok, nch=NCH, eps=EPS, blk=TB):
        fw = self.fw
        m = fw.mark()
        dtot = nch * 128
        sq = [fw.alloc(f"nsq{i}", [nch, blk], BF16) for i in range(2)]
        rs = [fw.alloc(f"nrs{i}", [blk], F32) for i in range(2)]
        for bi, t0 in enumerate(range(0, ntok, blk)):
            w = min(blk, ntok - t0)
            q = sq[bi % 2]
            r = rs[bi % 2]
            for c in range(nch):
                xs = src_fn(c, t0, t0 + w)
                fw.act(q[:, c, 0:w], xs, AF.Square)
            bank = fw.bank()
            for c in range(nch):
                fw.matmul(bank[:, 0:w], self.ones_bf.v, q[:, c, 0:w], c == 0, c == nch - 1)
            fw.act(r[:, 0:w], bank[:, 0:w], AF.Ln, bias=self.eps_col(eps), scale=1.0 / dtot)
            fw.act(r[:, 0:w], r[:, 0:w], AF.Exp, scale=-0.5)
            for c in range(nch):
                fw.stt(dst_fn(c, t0, t0 + w), src_fn(c, t0, t0 + w), gain[:, c:c + 1], r[:, 0:w], ALU.mult, ALU.mult)
        fw.release(m)

    def eps_col(self, eps):
        return self._eps[eps].v

    def load_seq(self, s):
        fw = self.fw
        m = fw.mark()
        stg = [fw.alloc(f"xstg{i}", [D], F32) for i in range(2)]
        sems = [fw.sem(f"xstg{i}") for i in range(2)]
        xd = self.d["x"]
        for tt in range(S // 128):
            st = stg[tt % 2]
            fw.dma("sp", st.v, xd[s, tt * 128:(tt + 1) * 128, :], sems[tt % 2])
            for half in range(2):
                bank = fw.bank()
                for q in range(4):
                    c = half * 4 + q
                    fw.transpose(bank[:, q * 128:(q + 1) * 128], st[:, c * 128:(c + 1) * 128], self.ident.v)
                dst = self.xT.sl(half * 4, half * 4 + 4, tt * 128, (tt + 1) * 128)
                fw.copy(dst, bank.v.re("p (a b) -> p a b", a=4), eng="act" if half else "dve")
        fw.release(m)

    def load_mem(self, s):
        fw = self.fw
        self.memnT = fw.alloc("memnT", [NCH, NMEM], BF16)
        m = fw.mark()
        stg = [fw.alloc(f"mstg{i}", [D], F32) for i in range(2)]
        sems = [fw.sem(f"mstg{i}") for i in range(2)]
        memT = fw.alloc("memT", [NCH, NMEM], F32)
        md = self.d["mem"]
        for mt in range(NMEM // 128):
            st = stg[mt % 2]
            fw.dma("sp", st.v, md[s, mt * 128:(mt + 1) * 128, :], sems[mt % 2])
            for half in range(2):
                bank = fw.bank()
                for q in range(4):
                    c = half * 4 + q
                    fw.transpose(bank[:, q * 128:(q + 1) * 128], st[:, c * 128:(c + 1) * 128], self.ident.v)
                fw.copy(memT[:, half * 4:half * 4 + 4, mt * 128:(mt + 1) * 128],
                        bank.v.re("p (a b) -> p a b", a=4), eng="act" if half else "dve")
        self.rmsnorm_T(lambda c, a, b: memT[:, c, a:b], self.vec("mem_norm"),
                       lambda c, a, b: self.memnT[:, c, a:b], NMEM, blk=NMEM)
        fw.release(m)

    def store_seq(self, s, final):
        fw = self.fw
        m = fw.mark()
        if final:
            src = fw.alloc_grid("xfin", NCH, S, F32)
            self.rmsnorm_T(lambda c, a, b: self.xT.c(c, a, b), self.vec("final_norm"),
                           lambda c, a, b: src.c(c, a, b), S)
        else:
            src = self.xT
        stg = [fw.alloc(f"ostg{i}", [D], F32) for i in range(2)]
        sems = [fw.sem(f"ostg{i}") for i in range(2)]
        for tt in range(S // 128):
            st = stg[tt % 2]
            for half in range(2):
                bank = fw.bank()
                for q in range(4):
                    c = half * 4 + q
                    fw.transpose(bank[:, q * 128:(q + 1) * 128], src.c(c, tt * 128, (tt + 1) * 128), self.ident.v)
                fw.copy(st[:, half * 512:(half + 1) * 512], bank.v, eng="act" if half else "dve")
            fw.dma("act", self.y[s, tt * 128:(tt + 1) * 128, :], st.v, sems[tt % 2])
        fw.release(m)

    def wstream(self, name, free_shape, nslots=2):
        fw = self.fw
        slots = [fw.alloc(f"{name}{i}", free_shape, BF16) for i in range(nslots)]
        sems = [fw.sem(f"{name}{i}") for i in range(nslots)]
        state = {"i": 0}

        def load(src_v, dst_idx=None):
            i = state["i"] % nslots
            state["i"] += 1
            dst = slots[i].v if dst_idx is None else slots[i][dst_idx]
            q = "sp" if src_v.ap.dtype == BF16 else "pool"
            fw.dma(q, dst, src_v, sems[i])
            slots[i]._sem = sems[i]
            return slots[i]
        return load

    @staticmethod
    def wsrc(dt, c0, c1):
        return V([dt], dt.ap.rearrange("(c p) n -> p c n", p=128)[:, :, c0:c1])

    def mlp(self, i):
        fw = self.fw
        m = fw.mark()
        hT = fw.alloc_grid("hT", NCH, S, BF16)
        wup = self.d[f"mlp_up{i}"]
        wdn = self.d[f"mlp_down{i}"]
        ld_up = self.wstream("wup", [NCH, 512])
        ld_dn = self.wstream("wdn", [4, D])
        hid = fw.alloc("hid", [32, TB], BF16)
        rl = [fw.alloc(f"rl{k}", [TB], F32) for k in range(3)]
        pre_up = ld_up(self.wsrc(wup, 0, 512))
        self.rmsnorm_T(lambda c, a, b: self.xT.c(c, a, b), self.vec(f"norm_mlp{i}"),
                       lambda c, a, b: hT.c(c, a, b), S)
        for tb in range(NTB):
            t0, t1 = tb * TB, (tb + 1) * TB
            nxt = pre_up if tb == 0 else ld_up(self.wsrc(wup, 0, 512))
            for g in range(8):
                cur = nxt
                if g + 1 < 8:
                    nxt = ld_up(self.wsrc(wup, (g + 1) * 512, (g + 2) * 512))
                else:
                    nxt_dn = ld_dn(V([wdn], wdn.ap.rearrange("(j p) n -> p j n", p=128)[:, 0:4, :]))
                for jj in range(4):
                    j = g * 4 + jj
                    bank = fw.bank()
                    for c in range(NCH):
                        fw.matmul(bank.v, cur[:, c, jj * 128:(jj + 1) * 128], hT.c(c, t0, t1), c == 0, c == NCH - 1)
                    r = rl[j % 3]
                    fw.act(r.v, bank.v, AF.Relu)
                    fw.tt(hid[:, j, :], r.v, r.v, ALU.mult)
            nxt = nxt_dn
            for g in range(8):
                cur = nxt
                if g + 1 < 8:
                    nxt = ld_dn(V([wdn], wdn.ap.rearrange("(j p) n -> p j n", p=128)[:, (g + 1) * 4:(g + 2) * 4, :]))
                for jj in range(4):
                    j = g * 4 + jj
                    for c in range(NCH):
                        fw.matmul(fw.banks[c].v, cur[:, jj, c * 128:(c + 1) * 128], hid[:, j, :], j == 0, j == 31)
            for c in range(NCH):
                xs = self.xT.c(c, t0, t1)
                fw.tt(xs, fw.banks[c].v, xs, ALU.add)
        fw.release(m)


    def proj_residual(self, wd, src, kch):
        fw = self.fw
        m = fw.mark()
        ld = self.wstream("wpo", [kch, 512])
        for g in range(2):
            w = ld(self.wsrc(wd, g * 512, (g + 1) * 512))
            for mm in range(4):
                c_out = g * 4 + mm
                for tb in range(NTB):
                    t0, t1 = tb * TB, (tb + 1) * TB
                    bank = fw.bank()
                    for c in range(kch):
                        fw.matmul(bank.v, w[:, c, mm * 128:(mm + 1) * 128], src.c(c, t0, t1), c == 0, c == kch - 1)
                    xs = self.xT.c(c_out, t0, t1)
                    fw.tt(xs, bank.v, xs, ALU.add)
        fw.release(m)

    def tab_stream(self, name, dcos, dsin, nslots=2):
        fw = self.fw
        slots = [fw.alloc(f"{name}{i}", [2, TB], F32) for i in range(nslots)]
        sems = [fw.sem(f"{name}{i}") for i in range(nslots)]
        st = {"i": 0, "pend": None}

        def issue(tb):
            i = st["i"] % nslots
            st["i"] += 1
            fw.dma("sp", slots[i][:, 0, :], dcos[:, tb * TB:(tb + 1) * TB], sems[i])
            fw.dma("sp", slots[i][:, 1, :], dsin[:, tb * TB:(tb + 1) * TB], sems[i])
            return (tb, slots[i])

        def get(tb):
            if st["pend"] is None:
                st["pend"] = issue(tb)
            ptb, slot = st["pend"]
            assert ptb == tb
            st["pend"] = issue((tb + 1) % NTB)
            return slot[:, 0, :], slot[:, 1, :]
        return get

    def rope_T(self, dst, ps, cosv, sinv, tmp, u, groups):
        fw = self.fw
        fw.tt(tmp, ps, cosv, ALU.mult)
        for (lo, plo, n) in groups:
            fw.tt(u[lo:lo + n], ps[plo:plo + n], sinv[lo:lo + n], ALU.mult)
        fw.tt(dst, tmp, u, ALU.add, eng="pool")

    def transpose_tm_to_fm(self, src_fn, dst, nch):
        fw = self.fw
        k = 0
        for tt in range(S // 128):
            for c0 in range(0, nch, 4):
                bank = fw.bank()
                bb = V(bank.v.ts, bank.ap.bitcast(BF16))
                for q in range(4):
                    fw.transpose(bb[:, q * 128:(q + 1) * 128], src_fn(tt, c0 + q), self.ident_bf.v)
                fw.copy(dst.sl(c0, c0 + 4, tt * 128, (tt + 1) * 128),
                        bb[:, 0:512].re("p (a b) -> p a b", a=4), eng="act" if k % 2 else "dve")
                k += 1

    def attn_core(self, qr, Ks, negbs, vtm, dv, scale, finalize, LA=2, mid_hook=None, bg=None, bg_every=4,
                  sbanks=(0, 1, 2), one_bank_acc=False):
        fw = self.fw
        nm = len(Ks)
        ee = self._ee
        vts = vtm.ts if isinstance(vtm, V) else vtm.v.ts
        steps = [(qb, jm, kt) for qb in range(NTB) for jm in range(nm) for kt in range(S // 128)]
        n = len(steps)
        Et = {}

        def acc_of(qb, jm):
            if one_bank_acc:
                b0 = fw.banks[4 + (self._acc_par + qb) % 2]
                w = dv + 1
                return [b0[:, k * w:(k + 1) * w] for k in range(4)]
            if nm == 1:
                bi = 4 + 2 * ((self._acc_par + qb) % 2)
            else:
                bi = 4 + 2 * jm
            b0, b1 = fw.banks[bi], fw.banks[bi + 1]
            return [b0[:, 0:dv + 1], b0[:, 256:256 + dv + 1], b1[:, 0:dv + 1], b1[:, 256:256 + dv + 1]]

        def qk(i):
            qb, jm, kt = steps[i]
            sb = fw.banks[sbanks[self._sbk % len(sbanks)]]
            self._sbk += 1
            fw.matmul(sb.v, Ks[jm][:, kt * 128:(kt + 1) * 128], qr[:, qb * TB:(qb + 1) * TB], True, True)
            e = ee[self._eit % len(ee)]
            self._eit += 1
            fw.act(e.v, sb.v, AF.Exp, bias=negbs[jm], scale=scale)
            Et[i] = e

        def pv(i):
            qb, jm, kt = steps[i]
            e = Et.pop(i)
            acc = acc_of(qb, jm)
            for qt in range(4):
                o, l, r = acc[qt].ap, e[:, qt * 128:(qt + 1) * 128].ap, vtm[:, kt, :].ap
                st = (kt == 0 and (qt == 0 if one_bank_acc else qt % 2 == 0))
                fw.emit("pe", lambda en, o=o, l=l, r=r, st=st, sp=(kt == 15): en.matmul(
                    o, l, r, start=st, stop=sp, skip_group_check=True),
                    e.v.ts + vts, acc[qt].ts, pe_acc=not st)

        deferred = []
        DEFER = 8
        for i in range(min(LA, n)):
            qk(i)
        for i in range(n):
            if i + LA < n:
                qk(i + LA)
            pv(i)
            while deferred and deferred[0][0] <= i:
                deferred.pop(0)[1]()
            qb, jm, kt = steps[i]
            if bg is not None and i % bg_every == bg_every - 1:
                next(bg, None)
            if jm == nm - 1 and kt == S // 128 - 1:
                cont = finalize(qb, [acc_of(qb, jj) for jj in range(nm)])
                if cont is not None:
                    deferred.append((i + DEFER, cont))
                if qb == 1 and mid_hook is not None:
                    mid_hook()
        while deferred:
            deferred.pop(0)[1]()
        if bg is not None:
            for _ in bg:
                pass
        if nm == 1:
            self._acc_par += NTB

    def mix1(self):
        fw = self.fw
        m = fw.mark()
        lam_init = 0.8 - 0.6 * math.exp(-0.3 * 1)
        scale = 64 ** -0.5
        hT = fw.alloc_grid("hT", NCH, S, BF16)
        self.rmsnorm_T(lambda c, a, b: self.xT.c(c, a, b), self.vec("norm_mix1"),
                       lambda c, a, b: hT.c(c, a, b), S)
        ao = fw.alloc("ao", [S // 128, D], BF16)
        m2 = fw.mark()
        tabD = self.tab_stream("tabD", self.d["ropeD_cos"], self.d["ropeD_sin"])
        lq = self.vec("lqk")
        sm = fw.alloc("lam_sm", [8], F32)
        prod = fw.alloc("lam_prod", [128], F32)
        fw.tt(prod[:, 0:64], lq[:, 0:64], lq[:, 64:128], ALU.mult)
        fw.tt(prod[:, 64:128], lq[:, 128:192], lq[:, 192:256], ALU.mult)
        fw.reduce(sm[:, 0:1], prod[:, 0:64], ALU.add)
        fw.reduce(sm[:, 1:2], prod[:, 64:128], ALU.add)
        fw.act(sm[:, 2:4], sm[:, 0:2], AF.Exp)
        fw.tt(sm[:, 4:5], sm[:, 3:4], sm[:, 2:3], ALU.subtract)
        fw.ts(sm[:, 5:6], sm[:, 4:5], -lam_init, None, op0=ALU.add)
        neglam = sm[:, 5:6]
        sub = fw.alloc("subln_s", [128], F32)
        fw.ts(sub.v, self.vec("subln"), 1.0 - lam_init, None, op0=ALU.mult)

        wqkv = self.d["od_w_qkv"]
        ldw = self.wstream("wqkv", [NCH, 384])
        src3 = wqkv.ap.rearrange("(c p) n -> p c n", p=128)
        kr = fw.alloc("kr", [S], BF16)
        bufs = []
        for bi in range(2):
            B = dict(qr=fw.alloc(f"qr{bi}", [S], BF16), kA=fw.alloc(f"kA{bi}", [S], BF16),
                     kB=fw.alloc(f"kB{bi}", [S], BF16), vtm=fw.alloc(f"vtm{bi}", [S // 128, 129], BF16),
                     negb=fw.alloc(f"negb{bi}", [2], F32), mx=fw.alloc(f"mx{bi}", [16], F32))
            fw.memset(B["vtm"][:, :, 128:129], 1.0, eng="dve")
            bufs.append(B)
        rtmp = [fw.alloc("rtmp0", [TB], F32)] * 2
        ru = [fw.alloc("ru0", [TB], F32)] * 2
        self._ee = [fw.alloc(f"ee{k}", [TB], BF16) for k in range(3)]
        self._eit = 0
        self._sbk = 0
        self._acc_par = 0
        fin = fw.alloc("fin", [4, 8], F32)
        ot = [fw.alloc("ot0", [128], F32)] * 2
        oo = [fw.alloc(f"oo{k}", [128], F32) for k in range(4)]
        junk32 = fw.alloc("junk32", [128], F32)
        groups = [(0, 32, 32), (32, 0, 32), (64, 96, 32), (96, 64, 32)]
        rkc = [0]

        PBs = [fw.banks[2], fw.banks[3]]
        pbc = [0]

        def nextpb():
            pbc[0] += 1
            return PBs[pbc[0] % len(PBs)]

        def proj_rope_unit(slot, which, dst, tb):
            t0, t1 = tb * TB, (tb + 1) * TB
            PB = nextpb()
            for c in range(NCH):
                fw.matmul(PB.v, slot[:, c, which * 128:(which + 1) * 128], hT.c(c, t0, t1), c == 0, c == NCH - 1)
            cv, sv = tabD(tb)
            self.rope_T(dst[:, t0:t1], PB.v, cv, sv,
                        rtmp[rkc[0] % 2].v, ru[rkc[0] % 2].v, groups)
            rkc[0] += 1

        def bounds_unit(B, wi, hj, tb):
            mx = B["mx"]
            mk = (self.maskA, self.maskB)[hj]
            PB = nextpb()
            fw.matmul(PB.v, mk.v, kr[:, tb * TB:(tb + 1) * TB], True, True)
            fw.reduce(mx[:, tb + 8 * hj:tb + 8 * hj + 1], PB.v, ALU.max)
            if tb == NTB - 1:
                fw.reduce(mx[:, 4 + 8 * hj + wi:5 + 8 * hj + wi], mx[:, 8 * hj:8 * hj + 4], ALU.max)

        def prologue_gen(hp, B):
            slot = ldw(V([wqkv], src3[:, :, hp * 128:(hp + 1) * 128]), (slice(None), slice(None), slice(0, 128)))
            self._dma_same_slot(slot, (slice(None), slice(None), slice(128, 256)),
                                V([wqkv], src3[:, :, D + hp * 128:D + (hp + 1) * 128]))
            self._dma_same_slot(slot, (slice(None), slice(None), slice(256, 384)),
                                V([wqkv], src3[:, :, 2 * D + hp * 128:2 * D + (hp + 1) * 128]))
            yield
            for tb in range(NTB):
                proj_rope_unit(slot, 0, B["qr"], tb)
                yield
                yield
            for tb in range(NTB):
                proj_rope_unit(slot, 1, kr, tb)
                yield
                yield
            fw.act(B["kA"].v, kr.v, AF.Copy, scale=self.mcol[:, 0:1])
            fw.act(B["kB"].v, kr.v, AF.Copy, scale=self.mcol[:, 1:2])
            for tb in range(NTB):
                fw.tt(kr[:, tb * TB:(tb + 1) * TB], kr[:, tb * TB:(tb + 1) * TB], kr[:, tb * TB:(tb + 1) * TB],
                      ALU.mult, eng="dve")
            vtm = B["vtm"]
            for t4 in range(4):
                PB = nextpb()
                for q in range(4):
                    tt = t4 * 4 + q
                    for c in range(NCH):
                        fw.matmul(PB[:, q * 128:(q + 1) * 128], hT.c(c, tt * 128, (tt + 1) * 128),
                                  slot[:, c, 256:384], c == 0, c == NCH - 1)
                fw.copy(vtm[:, t4 * 4:(t4 + 1) * 4, 0:128], PB.v.re("p (a b) -> p a b", a=4), eng="dve")
                yield
            for hj in range(2):
                for tb in range(NTB):
                    bounds_unit(B, 1, hj, tb)
                    yield
            for tb in range(NTB):
                fw.tt(kr[:, tb * TB:(tb + 1) * TB], B["qr"][:, tb * TB:(tb + 1) * TB], B["qr"][:, tb * TB:(tb + 1) * TB],
                      ALU.mult, eng="dve")
            yield
            for hj in range(2):
                for tb in range(NTB):
                    bounds_unit(B, 0, hj, tb)
                    yield
            mx = B["mx"]
            for hj in range(2):
                fw.tt(mx[:, 6 + 8 * hj:7 + 8 * hj], mx[:, 4 + 8 * hj:5 + 8 * hj], mx[:, 5 + 8 * hj:6 + 8 * hj], ALU.add)
                fw.ts(B["negb"][:, hj:hj + 1], mx[:, 6 + 8 * hj:7 + 8 * hj], -0.5 * scale, None, op0=ALU.mult)

        accsb = fw.alloc("accsb", [4, 512], F32)
        rden = fw.alloc("rden", [4, 2], F32)
        ssq = fw.alloc("ssq", [8], F32)

        def make_finalize(hp):
            def finalize(qb, accs):
                for b in range(4):
                    fw.copy(accsb[:, b, :], fw.banks[4 + b].v, eng="act" if b % 2 else "dve")
                fw.recip(rden.v, accsb[:, :, 128:512:256])
                fw.ts(rden[:, 2:4, :], rden[:, 2:4, :], neglam, None, op0=ALU.mult)
                for qt in range(4):
                    b, off = qt // 2, (qt % 2) * 256
                    o_o = oo[qt]
                    fw.ts(ot[0].v, accsb[:, b, off:off + 128], rden[:, b, qt % 2:qt % 2 + 1], None, op0=ALU.mult)
                    fw.stt(o_o.v, accsb[:, 2 + b, off:off + 128], rden[:, 2 + b, qt % 2:qt % 2 + 1], ot[0].v, ALU.mult, ALU.add)
                    oa, ja, sa = o_o.ap, junk32.ap, ssq[:, qt:qt + 1].ap
                    fw.emit("dve", lambda e, oa=oa, ja=ja, sa=sa: e.scalar_tensor_tensor(
                        ja, oa, 1.0, oa, ALU.mult, ALU.mult, accum_out=sa), o_o.v.ts, junk32.v.ts + ssq.v.ts)

                def part2():
                    fw.act(ssq[:, 4:8], ssq[:, 0:4], AF.Ln, bias=self.eps_col(1e-5), scale=1.0 / 128)
                    fw.act(ssq[:, 4:8], ssq[:, 4:8], AF.Exp, scale=-0.5)
                    for qt in range(4):
                        tt = qb * 4 + qt
                        fw.stt(ao[:, tt, hp * 128:(hp + 1) * 128], oo[qt].v, ssq[:, 4 + qt:5 + qt], sub.v, ALU.mult, ALU.mult)
                return part2
            return finalize

        for _ in prologue_gen(0, bufs[0]):
            pass
        for hp in range(8):
            B = bufs[hp % 2]
            bg = prologue_gen(hp + 1, bufs[(hp + 1) % 2]) if hp + 1 < 8 else None
            self.attn_core(B["qr"], [B["kA"], B["kB"]], [B["negb"][:, 0:1], B["negb"][:, 1:2]],
                           B["vtm"], 128, scale, make_finalize(hp), bg=bg, bg_every=3, LA=1, sbanks=(0, 1))

        fw.release(m2)
        self.transpose_tm_to_fm(lambda tt, c: ao[:, tt, c * 128:(c + 1) * 128], hT, NCH)
        self.proj_residual(self.d["od_w_out"], hT, NCH)
        fw.release(m)

    def _dma_same_slot(self, slot, idx, src_v):
        fw = self.fw
        sem = slot._sem
        q = "sp" if src_v.ap.dtype == BF16 else "pool"
        fw.dma(q, slot[idx], src_v, sem)

    def filter_phase(self):
        fw = self.fw
        m = fw.mark()
        N2 = 2.0 / 4096.0
        s_c = fw.sem("filt")
        w1 = fw.alloc("fw1", [128], F32)
        w2 = fw.alloc("fw2", [128], F32)
        w3 = fw.alloc("fw3", [2048], F32)
        ft = fw.alloc("feats", [S], F32)
        winf = fw.alloc("winf", [16, 512], F32)
        winb = fw.alloc("winb", [16, 512], F32)
        fw.dma("sp", w1.v, self.d["hy_w1p"].v, s_c)
        fw.dma("sp", w2.v, self.d["hy_w2p"].v, s_c)
        fw.dma("sp", w3.v, self.d["hy_w3p"].v, s_c)
        fw.dma("sp", ft.v, self.d["featsT"].v, s_c)
        fw.dma("sp", winf.v, V([self.d["win_f"]], self.d["win_f"].ap.rearrange("(t p) c -> p t c", p=128)), s_c)
        fw.dma("sp", winb.v, V([self.d["win_b"]], self.d["win_b"].ap.rearrange("(t p) c -> p t c", p=128)), s_c)
        a1 = fw.alloc("a1T", [S], F32)
        a2 = fw.alloc("a2T", [S], F32)
        pre = [fw.alloc(f"pre{k}", [TB], F32) for k in range(2)]
        wr = [fw.alloc(f"wr{k}", [TB], F32) for k in range(2)]
        PI = math.pi
        for (wt, src, dst, bn) in ((w1, ft, a1, "hy_b1"), (w2, a1, a2, "hy_b2")):
            for tb in range(NTB):
                bank = fw.bank()
                fw.matmul(bank.v, wt.v, src[:, tb * TB:(tb + 1) * TB], True, True)
                p = pre[tb % 2]
                fw.ts(p.v, bank.v, self.vec(bn), self.vec("hy_freq"), op0=ALU.add, op1=ALU.mult)
                w1_, w2_ = wr
                fw.ts(w1_.v, p.v, PI, -2 * PI, op0=ALU.is_gt, op1=ALU.mult)
                fw.ts(w2_.v, p.v, -PI, 2 * PI, op0=ALU.is_lt, op1=ALU.mult)
                fw.tt(p.v, p.v, w1_.v, ALU.add)
                fw.tt(p.v, p.v, w2_.v, ALU.add)
                fw.act(dst[:, tb * TB:(tb + 1) * TB], p.v, AF.Sin)
        ed = [[fw.alloc(f"ed{o}{k}", [16, 512], BF16) for k in range(2)] for o in range(2)]
        t12 = [fw.alloc(f"t12{k}", [512], F32) for k in range(4)]
        k = 0
        for pt in range(16):
            for o in range(2):
                bf = fw.bank()
                fw.matmul(bf.v, a2[:, pt * 128:(pt + 1) * 128], w3[:, (2 * o) * 512:(2 * o + 1) * 512], True, True)
                bb = fw.bank()
                fw.matmul(bb.v, a2[:, pt * 128:(pt + 1) * 128], w3[:, (2 * o + 1) * 512:(2 * o + 2) * 512], True, True)
                t1, t2 = t12[(k * 2) % 4], t12[(k * 2 + 1) % 4]
                k += 1
                fw.tt(t1.v, bf.v, winf[:, pt, :], ALU.mult)
                fw.tt(t2.v, bb.v, winb[:, pt, :], ALU.mult)
                fw.tt(ed[o][0][:, pt, :], t1.v, t2.v, ALU.add, eng="dve")
                fw.tt(ed[o][1][:, pt, :], t1.v, t2.v, ALU.subtract, eng="pool")
        ldF = self.dft_stream()
        kst = [fw.alloc(f"kst{k}", [2, 512], BF16) for k in range(2)]
        ksem = [fw.sem(f"kst{k}") for k in range(2)]
        it = 0
        for fc in range(16):
            slot = ldF(self.d["dftF"][fc])
            for o in range(2):
                be = fw.bank()
                for st in range(16):
                    fw.matmul(be.v, slot[:, 0, st, :], ed[o][0][:, st, :], st == 0, st == 15)
                bd = fw.bank()
                for st in range(16):
                    fw.matmul(bd.v, slot[:, 1, st, :], ed[o][1][:, st, :], st == 0, st == 15)
                kt_ = kst[it % 2]
                fw.act(kt_[:, 0, :], be.v, AF.Copy, scale=N2)
                fw.copy(kt_[:, 1, :], bd.v, eng="dve") if False else fw.ts(kt_[:, 1, :], bd.v, N2, None, op0=ALU.mult)
                fw.dma("act", self.kspec[o][fc].v, kt_.v, ksem[it % 2])
                it += 1
        self.precast()
        fw.release(m)

    def dft_stream(self):
        fw = self.fw
        slots = [fw.alloc(f"dft{i}", [2, 16, 128], BF16) for i in range(2)]
        sems = [fw.sem(f"dft{i}") for i in range(2)]
        st = {"i": 0}

        def load(src_v, shape4=None):
            i = st["i"] % 2
            st["i"] += 1
            t = slots[i]
            dst = t.v if shape4 is None else t.v.re("p a b c -> p (a b c)").re("p (a b c) -> p a b c", a=shape4[0], b=shape4[1])
            fw.dma("sp", dst, src_v, sems[i])
            return V([t], dst.ap)
        return load

    def mix0(self):
        fw = self.fw
        m = fw.mark()
        omlaT = fw.alloc_grid("omlaT", 4, S, BF16)
        m1 = fw.mark()
        hT = fw.alloc_grid("hT", NCH, S, BF16)
        self.rmsnorm_T(lambda c, a, b: self.xT.c(c, a, b), self.vec("norm_mix0"),
                       lambda c, a, b: hT.c(c, a, b), S)
        self.mla(hT, omlaT)
        fw.release(m1)
        vx = fw.alloc_grid("vx", 12, S, BF16)
        m1 = fw.mark()
        hT = fw.alloc_grid("hT", NCH, S, BF16)
        self.rmsnorm_T(lambda c, a, b: self.xT.c(c, a, b), self.vec("norm_mix0"),
                       lambda c, a, b: hT.c(c, a, b), S)
        self.hyena_inproj(hT, vx)
        fw.release(m1)
        self.hyena_conv(vx)
        class Cat:
            def c(_, c, t0, t1):
                return vx.c(c, t0, t1) if c < 4 else omlaT.c(c - 4, t0, t1)
        self.proj_residual(self.d["ev_w_out"], Cat(), NCH)
        fw.release(m)

    def mla(self, hT, omlaT):
        fw = self.fw
        m = fw.mark()
        scale = 96 ** -0.5
        win = self.d["ev_w_in"]
        src3 = win.ap.rearrange("(c p) n -> p c n", p=128)
        s_w = fw.sem("mlaw")
        wm = fw.alloc("wm", [NCH, 640], BF16)
        wuq = fw.alloc("wuq", [2, 8, 128], BF16)
        wuqs = fw.alloc("wuqs", [2, 8, 128], BF16)
        wkk = fw.alloc("wkk", [8, 128], BF16)
        wkv = fw.alloc("wkv", [8, 64], BF16)
        for t in (wm, wuq, wuqs, wkk):
            fw.memset(t.v, 0.0, eng="pool")
        fw.dma("pool", wm[:, :, 0:384], V([win], src3[:, :, 1536:1920]), s_w)
        fw.dma("pool", wm[:, :, 448:480], V([win], src3[:, :, 1920:1952]), s_w)
        fw.dma("pool", wm[:, :, 576:592], V([win], src3[:, :, 1936:1952]), s_w)
        fw.dma("pool", wm[:, :, 592:608], V([win], src3[:, :, 1920:1936]), s_w)
        uq = self.d["mla_w_uq"]
        uq4 = uq.ap.rearrange("(c p) (h d) -> p c h d", p=128, d=96)
        for kc in range(2):
            fw.dma("pool", wuq[:, kc, :, 0:96], V([uq], uq4[:, kc, :, :]), s_w)
            fw.dma("pool", wuqs[:, kc, :, 64:80], V([uq], uq4[:, kc, :, 80:96]), s_w)
            fw.dma("pool", wuqs[:, kc, :, 80:96], V([uq], uq4[:, kc, :, 64:80]), s_w)
        ukv = self.d["mla_w_ukv"]
        ukv3 = ukv.ap.rearrange("p (h d) -> p h d", d=128)
        fw.dma("pool", wkk[:, :, 0:64], V([ukv], ukv3[:, :, 0:64]), s_w)
        fw.dma("pool", wkv.v, V([ukv], ukv3[:, :, 64:128]), s_w)
        tabR = self.tab_stream("tabR", self.d["ropeR_cos"], self.d["ropeR_sin"])
        cqn = fw.alloc("cqn", [2, S], BF16)
        ckvn = fw.alloc("ckvn", [1, S], BF16)
        kpe = fw.alloc("kpe", [S], BF16)
        rtmp = [fw.alloc("rtmp0", [TB], F32)] * 2
        ru = [fw.alloc("ru0", [TB], F32)] * 2
        for tb in range(NTB):
            t0, t1 = tb * TB, (tb + 1) * TB
            b1 = fw.bank()
            for c in range(NCH):
                fw.matmul(b1.v, wm[:, c, 384:512], hT.c(c, t0, t1), c == 0, c == NCH - 1)
            b2 = fw.bank()
            for c in range(NCH):
                fw.matmul(b2.v, wm[:, c, 512:640], hT.c(c, t0, t1), c == 0, c == NCH - 1)
            tm, uu = rtmp[tb % 2], ru[tb % 2]
            cv, sv = tabR(tb)
            fw.tt(tm[64:96, :], b1[64:96, :], cv[64:96, :], ALU.mult)
            fw.tt(uu[64:96, :], b2[64:96, :], sv[64:96, :], ALU.mult)
            fw.tt(kpe[64:96, t0:t1], tm[64:96, :], uu[64:96, :], ALU.add, eng="pool")
        for (blks, gname, dstt) in (((0, 1), "q_norm", cqn), ((2,), "kv_norm", ckvn)):
            m2 = fw.mark()
            cq = fw.alloc("cq", [len(blks), S], F32)
            for bi, blk in enumerate(blks):
                for tb in range(NTB):
                    t0, t1 = tb * TB, (tb + 1) * TB
                    bank = fw.bank()
                    for c in range(NCH):
                        fw.matmul(bank.v, wm[:, c, blk * 128:(blk + 1) * 128], hT.c(c, t0, t1), c == 0, c == NCH - 1)
                    fw.copy(cq[:, bi, t0:t1], bank.v, eng="act")
            self.rmsnorm_T(lambda c, a, b: cq[:, c, a:b], self.vec(gname), lambda c, a, b: dstt[:, c, a:b], S, nch=len(blks))
            fw.release(m2)
        sq = fw.alloc("sq", [S], BF16)
        bufs = []
        for bi in range(2):
            B = dict(qh=fw.alloc(f"qh{bi}", [S], BF16), kh=fw.alloc(f"kh{bi}", [S], BF16),
                     negb=fw.alloc(f"negb{bi}", [1], F32), mx=fw.alloc(f"mx{bi}", [8], F32),
                     vt2=fw.alloc(f"vt2{bi}", [S // 128, 2, 65], BF16))
            fw.memset(B["qh"].v, 0.0, eng="pool")
            fw.memset(B["kh"].v, 0.0, eng="pool")
            fw.memset(B["vt2"][:, :, :, 64:65], 1.0, eng="dve")
            bufs.append(B)
        ao2 = fw.alloc("ao2", [S // 128, 128], BF16)
        self._ee = [fw.alloc(f"ee{k}", [TB], BF16) for k in range(3)]
        self._eit = 0
        self._sbk = 0
        self._acc_par = 0
        fin = fw.alloc("fin", [4], F32)
        PBs = [fw.banks[3], fw.banks[6], fw.banks[7]]
        pbc = [0]

        def nextpb():
            pbc[0] += 1
            return PBs[pbc[0] % len(PBs)]

        def prologue_gen(h):
            hp, hj = h // 2, h % 2
            B = bufs[h % 2]
            qh, kh, mx = B["qh"], B["kh"], B["mx"]
            if hj == 0:
                vt2 = bufs[hp % 2]["vt2"]
                for t4 in range(4):
                    PB = nextpb()
                    for q in range(4):
                        tt = t4 * 4 + q
                        fw.matmul(PB[:, q * 128:(q + 1) * 128], ckvn[:, 0, tt * 128:(tt + 1) * 128],
                                  wkv[:, 2 * hp:2 * hp + 2, :], True, True)
                    fw.copy(vt2[:, t4 * 4:(t4 + 1) * 4, :, 0:64],
                            PB.v.re("p (a b c) -> p a b c", a=4, b=2), eng="dve")
                    yield
            fw.copy(kh[64:96, :], kpe[64:96, :], eng="dve")
            for tb in range(NTB):
                t0, t1 = tb * TB, (tb + 1) * TB
                cv, sv = tabR(tb)
                PB = nextpb()
                for kc in range(2):
                    fw.matmul(PB.v, wuq[:, kc, h, :], cqn[:, kc, t0:t1], kc == 0, kc == 1)
                fw.copy(qh[0:64, t0:t1], PB[0:64, :], eng="dve")
                tm, uu = rtmp[tb % 2], ru[tb % 2]
                fw.tt(tm[64:96, :], PB[64:96, :], cv[64:96, :], ALU.mult)
                yield
                PB = nextpb()
                for kc in range(2):
                    fw.matmul(PB.v, wuqs[:, kc, h, :], cqn[:, kc, t0:t1], kc == 0, kc == 1)
                fw.tt(uu[64:96, :], PB[64:96, :], sv[64:96, :], ALU.mult)
                fw.tt(qh[64:96, t0:t1], tm[64:96, :], uu[64:96, :], ALU.add, eng="pool")
                yield
                PB = nextpb()
                fw.matmul(PB.v, wkk[:, h, :], ckvn[:, 0, t0:t1], True, True)
                fw.copy(kh[0:64, t0:t1], PB[0:64, :], eng="dve")
                yield
            for wi, srct in enumerate((qh, kh)):
                for tb in range(NTB):
                    fw.tt(sq[:, tb * TB:(tb + 1) * TB], srct[:, tb * TB:(tb + 1) * TB], srct[:, tb * TB:(tb + 1) * TB],
                          ALU.mult, eng="dve")
                yield
                for tb in range(NTB):
                    PB = nextpb()
                    fw.matmul(PB.v, self.ones_bf.v, sq[:, tb * TB:(tb + 1) * TB], True, True)
                    fw.reduce(mx[:, tb:tb + 1], PB.v, ALU.max)
                    yield
                fw.reduce(mx[:, 4 + wi:5 + wi], mx[:, 0:4], ALU.max)
            fw.tt(mx[:, 6:7], mx[:, 4:5], mx[:, 5:6], ALU.add)
            fw.ts(B["negb"].v, mx[:, 6:7], -0.5 * scale, None, op0=ALU.mult)

        def make_finalize(hj):
            def finalize(qb, accs):
                for qt in range(4):
                    tt = qb * 4 + qt
                    a1 = accs[0][qt]
                    fw.recip(fin[:, qt:qt + 1], a1[:, 64:65])
                    fw.ts(ao2[:, tt, hj * 64:(hj + 1) * 64], a1[:, 0:64], fin[:, qt:qt + 1], None, op0=ALU.mult)
                return None
            return finalize

        for _ in prologue_gen(0):
            pass
        for h in range(8):
            hp, hj = h // 2, h % 2
            B = bufs[h % 2]
            bg = prologue_gen(h + 1) if h + 1 < 8 else None
            self.attn_core(B["qh"], [B["kh"]], [B["negb"].v], bufs[hp % 2]["vt2"][:, :, hj, :], 64, scale,
                           make_finalize(hj), bg=bg, bg_every=2, sbanks=(0, 1, 2), one_bank_acc=True)
            if hj == 1:
                for t4 in range(4):
                    PB = nextpb()
                    bb = V(PB.v.ts, PB.ap.bitcast(BF16))
                    for q in range(4):
                        fw.transpose(bb[:, q * 128:(q + 1) * 128], ao2[:, t4 * 4 + q, :], self.ident_bf.v)
                    fw.copy(omlaT.c(hp, t4 * 512, (t4 + 1) * 512), bb[:, 0:512], eng="dve")
        fw.release(m)

    def hyena_inproj(self, hT, vx):
        fw = self.fw
        m = fw.mark()
        win = self.d["ev_w_in"]
        ld = self.wstream("whin", [NCH, 512])
        ub = [fw.alloc("ub0", [S + 2], F32)] * 2
        ac = [fw.alloc("uacc0", [S], F32)] * 2
        for u in ub[:1]:
            fw.memset(u[:, 0:1], 0.0, eng="dve")
            fw.memset(u[:, S + 1:S + 2], 0.0, eng="dve")
        for g in range(3):
            w = ld(self.wsrc(win, g * 512, (g + 1) * 512))
            for mm in range(4):
                hc = g * 4 + mm
                u, a = ub[hc % 2], ac[hc % 2]
                for tb in range(NTB):
                    bank = fw.bank()
                    for c in range(NCH):
                        fw.matmul(bank.v, w[:, c, mm * 128:(mm + 1) * 128], hT.c(c, tb * TB, (tb + 1) * TB), c == 0, c == NCH - 1)
                    fw.copy(u[:, 1 + tb * TB:1 + (tb + 1) * TB], bank.v, eng="act")
                fw.ts(a.v, u[:, 0:S], self.vec("conv_w0", hc, hc + 1), self.vec("conv_b", hc, hc + 1), op0=ALU.mult, op1=ALU.add)
                fw.stt(a.v, u[:, 1:S + 1], self.vec("conv_w1", hc, hc + 1), a.v, ALU.mult, ALU.add)
                fw.stt(vx.sl(hc, hc + 1, 0, S), u[:, 2:S + 2], self.vec("conv_w2", hc, hc + 1), a.v, ALU.mult, ALU.add)
        fw.release(m)

    def hyena_conv(self, vx):
        fw = self.fw
        m = fw.mark()
        zT = fw.alloc("zT", [16, 512], BF16)
        Ut = fw.alloc("Ut", [16, 512], BF16)
        Vt = fw.alloc("Vt", [16, 512], BF16)
        ldF = self.dft_stream()
        ksl = [fw.alloc("ksl0", [2, 512], BF16)] * 2
        ksem = [fw.sem("ksl0")] * 2
        tq = [fw.alloc(f"tq{k}", [512], F32) for k in range(2)] * 2
        kit = 0
        for o in range(2):
            for tt in range(16):
                bank = fw.bank(0, 4)
                bb = V(bank.v.ts, bank.ap.bitcast(BF16))
                for c in range(4):
                    fw.transpose(bb[:, c * 128:(c + 1) * 128], vx.c(c, tt * 128, (tt + 1) * 128), self.ident_bf.v)
                fw.copy(zT[:, tt, :], bb[:, 0:512], eng="act" if tt % 2 else "dve")
            for fc in range(16):
                slot = ldF(self.d["dftF"][fc])
                ks = ksl[kit % 2]
                fw.dma("act", ks.v, self.kspec[o][fc].v, ksem[kit % 2])
                kit += 1
                ba = fw.bank(0, 4)
                for st in range(16):
                    fw.matmul(ba.v, slot[:, 0, st, :], zT[:, st, :], st == 0, st == 15)
                bb_ = fw.bank(0, 4)
                for st in range(16):
                    fw.matmul(bb_.v, slot[:, 1, st, :], zT[:, st, :], st == 0, st == 15)
                t1, t2 = tq[0], tq[1]
                fw.tt(t1.v, ba.v, ks[:, 0, :], ALU.mult)
                fw.tt(t2.v, bb_.v, ks[:, 1, :], ALU.mult)
                fw.tt(Ut[:, fc, :], t1.v, t2.v, ALU.subtract, eng="pool")
                fw.tt(t1.v, ba.v, ks[:, 1, :], ALU.mult)
                fw.tt(t2.v, bb_.v, ks[:, 0, :], ALU.mult)
                fw.tt(Vt[:, fc, :], t1.v, t2.v, ALU.add, eng="pool")
            gate0 = 4 + 4 * o
            for tb in range(NTB):
                t0, t1_ = tb * TB, (tb + 1) * TB
                for fg in range(4):
                    slot = ldF(self.d["dftI"][tb, fg], shape4=(4, 2))
                    for fci in range(4):
                        fc = fg * 4 + fci
                        for cc in range(4):
                            fw.matmul(fw.banks[4 + cc].v, Ut[:, fc, cc * 128:(cc + 1) * 128], slot[:, fci, 0, :], fc == 0, False)
                            fw.matmul(fw.banks[4 + cc].v, Vt[:, fc, cc * 128:(cc + 1) * 128], slot[:, fci, 1, :], False, fc == 15)
                for cc in range(4):
                    zc = vx.c(cc, t0, t1_)
                    tmp = tq[cc % 2]
                    fw.stt(tmp.v, zc, self.vec(f"hy_skip{o}", cc, cc + 1), fw.banks[4 + cc].v, ALU.mult, ALU.add)
                    fw.tt(zc, tmp.v, vx.c(gate0 + cc, t0, t1_), ALU.mult, eng="pool")
        fw.release(m)

    def cross(self, i):
        fw = self.fw
        m = fw.mark()
        scale = 256 ** -0.5
        wq, wkv, wo = self.d[f"xa_wq{i}"], self.d[f"xa_wkv{i}"], self.d[f"xa_wo{i}"]
        ld = self.wstream("wx", [NCH, 512])
        pre_k = [ld(self.wsrc(wkv, g * 512, (g + 1) * 512)) for g in range(2)]
        self.load_mem(self.cur_seq)
        hT = fw.alloc_grid("hT", NCH, S, BF16)
        self.rmsnorm_T(lambda c, a, b: self.xT.c(c, a, b), self.vec(f"norm_cross{i}"),
                       lambda c, a, b: hT.c(c, a, b), S)
        kT = fw.alloc("kT", [NCH, NMEM], BF16)
        vtm = fw.alloc("vtm", [2, D], BF16)
        qT = fw.alloc_grid("qT", NCH, S, BF16)
        for g in range(2):
            w = pre_k[g]
            for mm in range(4):
                cc = g * 4 + mm
                bank = fw.bank()
                for c in range(NCH):
                    fw.matmul(bank[:, 0:NMEM], w[:, c, mm * 128:(mm + 1) * 128], self.memnT[:, c, :], c == 0, c == NCH - 1)
                fw.copy(kT[:, cc, :], bank[:, 0:NMEM], eng="act")
        for g in range(2):
            w = ld(self.wsrc(wkv, D + g * 512, D + (g + 1) * 512))
            for mt in range(2):
                bank = fw.bank()
                for c in range(NCH):
                    fw.matmul(bank.v, self.memnT[:, c, mt * 128:(mt + 1) * 128], w[:, c, :], c == 0, c == NCH - 1)
                fw.copy(vtm[:, mt, g * 512:(g + 1) * 512], bank.v, eng="act")
        for g in range(2):
            w = ld(self.wsrc(wq, g * 512, (g + 1) * 512))
            for mm in range(4):
                cc = g * 4 + mm
                for tb in range(NTB):
                    bank = fw.bank()
                    for c in range(NCH):
                        fw.matmul(bank.v, w[:, c, mm * 128:(mm + 1) * 128], hT.c(c, tb * TB, (tb + 1) * TB), c == 0, c == NCH - 1)
                    fw.copy(qT.c(cc, tb * TB, (tb + 1) * TB), bank.v, eng="act" if tb % 2 else "dve")
        negb = fw.alloc("negb", [4], F32)
        sqb = [fw.alloc(f"sqb{k}", [2, TB], BF16) for k in range(2)]
        mx = fw.alloc("mx", [8], F32)
        for h in range(4):
            for tb in range(NTB):
                sq = sqb[tb % 2]
                for u in range(2):
                    qs = qT.c(2 * h + u, tb * TB, (tb + 1) * TB)
                    fw.act(sq[:, u, :], qs, AF.Square)
                bank = fw.bank()
                for u in range(2):
                    fw.matmul(bank.v, self.ones_bf.v, sq[:, u, :], u == 0, u == 1)
                fw.reduce(mx[:, tb:tb + 1], bank.v, ALU.max)
            sq = sqb[0]
            for u in range(2):
                ks = kT[:, 2 * h + u, :]
                fw.act(sq[:, u, 0:NMEM], ks, AF.Square)
            bank = fw.bank()
            for u in range(2):
                fw.matmul(bank[:, 0:NMEM], self.ones_bf.v, sq[:, u, 0:NMEM], u == 0, u == 1)
            fw.reduce(mx[:, 4:5], bank[:, 0:NMEM], ALU.max)
            fw.reduce(mx[:, 5:6], mx[:, 0:4], ALU.max)
            fw.tt(mx[:, 6:7], mx[:, 4:5], mx[:, 5:6], ALU.add)
            fw.ts(negb[:, h:h + 1], mx[:, 6:7], -0.5 * scale, None, op0=ALU.mult)
        oT = hT
        ee = [fw.alloc(f"ee{k}", [TB], BF16) for k in range(4)]
        rd = [fw.alloc(f"rd{k}", [TB], F32) for k in range(2)]
        its = [(h, tb) for h in range(4) for tb in range(NTB)]
        ee = ee + [fw.alloc(f"ee{k}", [TB], BF16) for k in range(4, 6)]

        def qk_exp(n):
            h, tb = its[n]
            t0, t1 = tb * TB, (tb + 1) * TB
            es = []
            for kt in range(2):
                bank = fw.banks[(n * 2 + kt) % 4]
                for u in range(2):
                    fw.matmul(bank.v, kT[:, 2 * h + u, kt * 128:(kt + 1) * 128], qT.c(2 * h + u, t0, t1), u == 0, u == 1)
                e = ee[(n * 2 + kt) % 6]
                fw.act(e.v, bank.v, AF.Exp, bias=negb[:, h:h + 1], scale=scale)
                es.append(e)
            return es

        pend = qk_exp(0)
        for n, (h, tb) in enumerate(its):
            t0, t1 = tb * TB, (tb + 1) * TB
            es = pend
            if n + 1 < len(its):
                pend = qk_exp(n + 1)
            bd = fw.bank(4, 8)
            for kt in range(2):
                fw.matmul(bd.v, self.ones_bf.v, es[kt].v, kt == 0, kt == 1)
            r = rd[n % 2]
            fw.act(r.v, bd.v, AF.Ln)
            fw.act(r.v, r.v, AF.Exp, scale=-1.0)
            for dc in range(2):
                bo = fw.bank(4, 8)
                for kt in range(2):
                    fw.matmul(bo.v, vtm[:, kt, h * 256 + dc * 128:h * 256 + (dc + 1) * 128], es[kt].v, kt == 0, kt == 1)
                fw.tt(oT.c(2 * h + dc, t0, t1), bo.v, r.v, ALU.mult)
        for g in range(2):
            w = ld(self.wsrc(wo, g * 512, (g + 1) * 512))
            for mm in range(4):
                c_out = g * 4 + mm
                for tb in range(NTB):
                    t0, t1 = tb * TB, (tb + 1) * TB
                    bank = fw.bank()
                    for c in range(NCH):
                        fw.matmul(bank.v, w[:, c, mm * 128:(mm + 1) * 128], oT.c(c, t0, t1), c == 0, c == NCH - 1)
                    xs = self.xT.c(c_out, t0, t1)
                    fw.tt(xs, bank.v, xs, ALU.add)
        fw.release(m)


ALL_STAGES = ["mix0", "cross0", "mlp0", "mix1", "cross1", "mlp1", "final"]


def rope_consts():
    out = {}
    inv = (10000.0 ** (-np.arange(0, 64, 2, dtype=np.float32) / 64)).astype(np.float32)
    ang = (np.arange(S, dtype=np.float32)[None, :] * inv[:, None]).astype(np.float32)
    cos, sin = np.cos(ang).astype(np.float32), np.sin(ang).astype(np.float32)
    p = np.arange(128)
    out["ropeD_cos"] = np.ascontiguousarray(cos[p % 32])
    sgn = np.where((p % 64) < 32, -1.0, 1.0).astype(np.float32)[:, None]
    out["ropeD_sin"] = np.ascontiguousarray(sin[p % 32] * sgn)
    return out


_CONST_CACHE = {}


def hyena_consts():
    if "c" in _CONST_CACHE:
        return _CONST_CACHE["c"]
    out = {}
    L = S
    N = 2 * L
    t = np.linspace(0.0, 1.0, L, dtype=np.float32)[:, None]
    w = ((2.0 * math.pi / L) * np.arange(L, dtype=np.float32))[:, None].astype(np.float32)
    bands = np.linspace(1e-4, 15, 16, dtype=np.float32)[None, :]
    feats = np.concatenate([t, np.cos(bands * w), -np.sin(bands * w)], axis=-1).astype(np.float32)
    ft = np.zeros((128, L), np.float32)
    ft[:33] = feats.T
    out["featsT"] = ft
    deltas = np.linspace(math.log(1e-2) / 1.5, math.log(1e-2) / 0.3, 512, dtype=np.float32)
    window = np.exp(-t * np.abs(deltas)[None, :]).astype(np.float32)
    out["win_f"] = np.ascontiguousarray(window)
    wb = window.copy()
    wb[0] = 0.0
    out["win_b"] = wb
    idx = np.arange(L, dtype=np.float64)
    ang = 2.0 * np.pi * np.outer(idx, idx + 0.5) / N
    C = np.cos(ang)
    Sn = np.sin(ang)
    bf = ml_dtypes.bfloat16
    F_ = np.stack([C, Sn], 0).reshape(2, 16, 128, 16, 128)
    out["dftF"] = np.ascontiguousarray(F_.transpose(3, 2, 0, 1, 4)).astype(bf)
    I_ = np.stack([C, Sn], 0).reshape(2, 4, 512, 4, 4, 128)
    out["dftI"] = np.ascontiguousarray(I_.transpose(1, 3, 5, 4, 0, 2)).astype(bf)
    inv = (10000.0 ** (-np.arange(0, 32, 2, dtype=np.float32) / 32)).astype(np.float32)
    angr = (np.arange(S, dtype=np.float32)[None, :] * inv[:, None]).astype(np.float32)
    cr, sr = np.cos(angr).astype(np.float32), np.sin(angr).astype(np.float32)
    rc = np.zeros((128, S), np.float32)
    rs = np.zeros((128, S), np.float32)
    rc[64:80] = cr
    rc[80:96] = cr
    rs[64:80] = -sr
    rs[80:96] = sr
    out["ropeR_cos"] = rc
    out["ropeR_sin"] = rs
    _CONST_CACHE["c"] = out
    return out


def make_in_map(inp, x, mem):
    mp = {
        "x": np.ascontiguousarray(x, np.float32),
        "mem": np.ascontiguousarray(mem, np.float32),
        "vecs": pack_vecs(inp),
        "ident": np.eye(128, dtype=np.float32),
    }
    mp["od_w_qkv"] = np.ascontiguousarray(inp["od_w_qkv"][0], np.float32)
    mp["od_w_out"] = np.ascontiguousarray(inp["od_w_out"][0], np.float32)
    mp.update(rope_consts())
    mp.update(hyena_consts())
    mp["ev_w_in"] = np.ascontiguousarray(inp["ev_w_in"][0], np.float32)
    mp["ev_w_out"] = np.ascontiguousarray(inp["ev_w_out"][0], np.float32)
    mp["mla_w_uq"] = np.ascontiguousarray(inp["mla_w_uq"][0], np.float32)
    mp["mla_w_ukv"] = np.ascontiguousarray(inp["mla_w_ukv"][0], np.float32)
    w1p = np.zeros((128, 128), np.float32)
    w1p[:33, :64] = inp["hy_w1"][0]
    w2p = np.zeros((128, 128), np.float32)
    w2p[:64, :64] = inp["hy_w2"][0]
    w3p = np.zeros((128, 2048), np.float32)
    w3p[:64] = inp["hy_w3"][0]
    mp["hy_w1p"], mp["hy_w2p"], mp["hy_w3p"] = w1p, w2p, w3p
    for i in range(2):
        mp[f"xa_wq{i}"] = np.ascontiguousarray(inp["xa_wq"][i], np.float32)
        mp[f"xa_wkv{i}"] = np.ascontiguousarray(inp["xa_wkv"][i], np.float32)
        mp[f"xa_wo{i}"] = np.ascontiguousarray(inp["xa_wo"][i], np.float32)
        mp[f"mlp_up{i}"] = np.ascontiguousarray(inp["mlp_up"][i], np.float32)
        mp[f"mlp_down{i}"] = np.ascontiguousarray(inp["mlp_down"][i], np.float32)
    return mp


def run_stages(inp, x, stages, nseq, n_cores, trace=False):
    k = K(nseq, stages)
    in_maps = []
    for c in range(n_cores):
        mp = make_in_map(inp, x[c * nseq:(c + 1) * nseq], inp["mem"][c * nseq:(c + 1) * nseq])
        in_maps.append({n: mp[n] for n in k.in_names})
    res = run_bass_kernel_spmd(k.nc, in_maps, core_ids=list(range(n_cores)), trace=trace)
    out = np.concatenate([np.asarray(r["y"]) for r in res.results], axis=0)
    return out, res


def kernel(**inputs):
    inp = {k: np.asarray(v) for k, v in inputs.items()}
    out, _ = run_stages(inp, inp["x"], ALL_STAGES, SEQ_PER_CORE, N_CORES)
    return out.astype(np.float32)
```
